# Optimizing a Trainium2 kernel written in Bass

```python
import jax, jax.numpy as jnp
from jax import lax
import numpy as np

D_MODEL = 2048
BATCH = 4
SEQ = 4096
DEPTH = 2

CHUNK = 64
N_A_LAYERS = DEPTH // 2
N_B_LAYERS = DEPTH - N_A_LAYERS
EPS = 1e-6

A_INNER = 2 * D_MODEL
A_HEADS = 8
A_DV = A_INNER // A_HEADS
A_QK = A_INNER // 2
A_DK = A_QK // A_HEADS
CONV_W = 4
A_COLS = 2 * A_QK + 3 * A_INNER + 2 * A_HEADS
A_V0 = 2 * A_QK
A_O0 = A_V0 + A_INNER
A_Z0 = A_O0 + A_INNER
A_G0 = A_Z0 + A_INNER

B_HEADS = 16
B_DH = D_MODEL // B_HEADS
B_WIDTH = B_HEADS * B_DH
LEFT_CHUNKS = 8
BAND = (LEFT_CHUNKS + 1) * CHUNK
MAX_REL = 128

kernel_name = "yoco_mlstm_chunk_relpos_hybrid"

F32 = jnp.float32


def rmsnorm(x, g):
    xf = x.astype(F32)
    return xf * lax.rsqrt(jnp.mean(xf * xf, axis=-1, keepdims=True) + EPS) * g.astype(F32)


def modulate(hn, shift, scale):
    return hn * (1.0 + scale[:, None, :]) + shift[:, None, :]


def causal_conv(u, w, b):
    S = u.shape[1]
    up = jnp.pad(u, ((0, 0), (CONV_W - 1, 0), (0, 0)))
    out = b
    for j in range(CONV_W):
        out = out + up[:, j:j + S] * w[j]
    return out


def mlstm_chunkwise(q, k, v, li, lf):
    Bn, H, S, DK = q.shape
    DV = v.shape[-1]
    NC = S // CHUNK

    def to_chunks(a):
        return jnp.moveaxis(a.reshape(Bn, H, NC, CHUNK, *a.shape[3:]), 2, 0)

    qc, kc, vc, lic, lfc = (to_chunks(a) for a in (q, k * (DK ** -0.5), v, li, lf))
    causal = jnp.tril(jnp.ones((CHUNK, CHUNK), bool))

    def step(carry, inp):
        C, n, m = carry
        qi, ki, vi, lii, lfi = inp
        b = jnp.cumsum(lfi, axis=-1)
        dmat = b[..., :, None] - b[..., None, :] + lii[..., None, :]
        dmat = jnp.where(causal, dmat, -jnp.inf)
        inter = b + m[..., None]
        m_t = jnp.maximum(inter, jnp.max(dmat, axis=-1))
        w_intra = jnp.exp(dmat - m_t[..., None])
        w_inter = jnp.exp(inter - m_t)
        s = jnp.einsum('bhtd,bhsd->bhts', qi, ki) * w_intra
        num = (w_inter[..., None] * jnp.einsum('bhtd,bhde->bhte', qi, C)
               + jnp.einsum('bhts,bhse->bhte', s, vi))
        den = w_inter * jnp.einsum('bhtd,bhd->bht', qi, n) + jnp.sum(s, axis=-1)
        h = num / jnp.maximum(jnp.abs(den), jnp.exp(-m_t))[..., None]
        b_last = b[..., -1]
        g_s = b_last[..., None] - b + lii
        m_new = jnp.maximum(b_last + m, jnp.max(g_s, axis=-1))
        decay = jnp.exp(b_last + m - m_new)
        ws = jnp.exp(g_s - m_new[..., None])
        C_new = decay[..., None, None] * C + jnp.einsum('bhs,bhsd,bhse->bhde', ws, ki, vi)
        n_new = decay[..., None] * n + jnp.einsum('bhs,bhsd->bhd', ws, ki)
        return (C_new, n_new, m_new), h

    init = (jnp.zeros((Bn, H, DK, DV), F32), jnp.zeros((Bn, H, DK), F32), jnp.zeros((Bn, H), F32))
    _, hs = lax.scan(step, init, (qc, kc, vc, lic, lfc))
    return jnp.moveaxis(hs, 0, 2).reshape(Bn, H, S, DV)


def mlstm_mixer(h, w_in, conv_w, conv_b, gate_b, g_head, w_out):
    Bn, S, _ = h.shape
    proj = h @ w_in
    qk = jax.nn.silu(causal_conv(proj[..., :A_V0].astype(F32), conv_w.astype(F32), conv_b.astype(F32)))
    v = proj[..., A_V0:A_O0]
    o = proj[..., A_O0:A_Z0]
    z = proj[..., A_Z0:A_G0]
    gates = proj[..., A_G0:].astype(F32) + gate_b.astype(F32)

    def heads(a, d):
        return a.astype(F32).reshape(Bn, S, A_HEADS, d).transpose(0, 2, 1, 3)

    li = gates[..., :A_HEADS].transpose(0, 2, 1)
    lf = jax.nn.log_sigmoid(gates[..., A_HEADS:]).transpose(0, 2, 1)
    ht = mlstm_chunkwise(heads(qk[..., :A_QK], A_DK), heads(qk[..., A_QK:], A_DK), heads(v, A_DV), li, lf)
    ht = rmsnorm(ht.transpose(0, 2, 1, 3), g_head.reshape(A_HEADS, A_DV)).reshape(Bn, S, A_INNER)
    y = jax.nn.sigmoid(o.astype(F32)) * ht * jax.nn.silu(z.astype(F32))
    return y.astype(h.dtype) @ w_out


def chunk_band_attention(q, k, v, rel_table):
    Bn, S, H, Dh = q.shape
    NC = S // CHUNK
    pad = LEFT_CHUNKS * CHUNK
    q = q.astype(F32)
    kp = jnp.pad(k.astype(F32), ((0, 0), (pad, 0), (0, 0), (0, 0)))
    vp = jnp.pad(v.astype(F32), ((0, 0), (pad, 0), (0, 0), (0, 0)))
    qc = jnp.moveaxis(q.reshape(Bn, NC, CHUNK, H, Dh), 1, 0)
    r = jnp.arange(CHUNK)[:, None]
    idx = jnp.arange(BAND)[None, :]
    bucket = jnp.clip(pad + r - idx, -MAX_REL, MAX_REL) + MAX_REL
    bias = rel_table.astype(F32)[:, bucket]
    key_off = jnp.arange(BAND) - pad
    scale = Dh ** -0.5

    def one_chunk(args):
        ci, qi = args
        start = ci * CHUNK
        kb = lax.dynamic_slice_in_dim(kp, start, BAND, axis=1)
        vb = lax.dynamic_slice_in_dim(vp, start, BAND, axis=1)
        s = jnp.einsum('blhd,bkhd->bhlk', qi, kb) * scale + bias
        s = jnp.where((start + key_off) >= 0, s, -jnp.inf)
        p = jax.nn.softmax(s, axis=-1)
        return jnp.einsum('bhlk,bkhd->blhd', p, vb)

    out = lax.map(one_chunk, (jnp.arange(NC), qc))
    return jnp.moveaxis(out, 0, 1).reshape(Bn, S, H, Dh)


def band_attention_mixer(h, k, v, w_in, rel_table, w_out):
    Bn, S, _ = h.shape
    qz = h @ w_in
    q = qz[..., :B_WIDTH].reshape(Bn, S, B_HEADS, B_DH)
    z = qz[..., B_WIDTH:]
    o = chunk_band_attention(q, k, v, rel_table)
    y = o.reshape(Bn, S, B_WIDTH) * jax.nn.silu(z.astype(F32))
    return y.astype(h.dtype) @ w_out


def setup_inputs(seed: int = 0) -> dict:
    key = jax.random.key(seed)
    ks = jax.random.split(key, 24)

    def nrm(k, shape, fan):
        return jax.random.normal(k, shape, F32) * (fan ** -0.5)

    def small(k, shape, s):
        return jax.random.normal(k, shape, F32) * s

    x = jax.random.normal(ks[0], (BATCH, SEQ, D_MODEL), F32)
    c = jax.random.normal(ks[1], (BATCH, D_MODEL), F32)
    ada_w = 0.5 * nrm(ks[2], (DEPTH, D_MODEL, 3 * D_MODEL), D_MODEL)
    ada_b = small(ks[3], (DEPTH, 3 * D_MODEL), 0.02)
    g_pre = 1.0 + small(ks[4], (DEPTH, D_MODEL), 0.05)
    g_post = 1.0 + small(ks[5], (DEPTH, D_MODEL), 0.05)
    a_w_in = nrm(ks[6], (N_A_LAYERS, D_MODEL, A_COLS), D_MODEL)
    a_conv_w = nrm(ks[7], (N_A_LAYERS, CONV_W, 2 * A_QK), CONV_W)
    a_conv_b = small(ks[8], (N_A_LAYERS, 2 * A_QK), 0.02)
    i_bias = small(ks[9], (N_A_LAYERS, A_HEADS), 0.1)
    f_bias = jnp.linspace(3.0, 6.0, A_HEADS, dtype=F32)[None, :] + small(ks[10], (N_A_LAYERS, A_HEADS), 0.1)
    a_gate_b = jnp.concatenate([i_bias, f_bias], axis=-1)
    a_g_head = 1.0 + small(ks[11], (N_A_LAYERS, A_INNER), 0.05)
    a_w_out = nrm(ks[12], (N_A_LAYERS, A_INNER, D_MODEL), A_INNER)
    kv_ada_w = 0.5 * nrm(ks[13], (D_MODEL, 2 * D_MODEL), D_MODEL)
    kv_ada_b = small(ks[14], (2 * D_MODEL,), 0.02)
    kv_g = 1.0 + small(ks[15], (D_MODEL,), 0.05)
    kv_w = nrm(ks[16], (D_MODEL, 2 * B_WIDTH), D_MODEL)
    b_w_in = nrm(ks[17], (N_B_LAYERS, D_MODEL, 2 * B_WIDTH), D_MODEL)
    b_rel = small(ks[18], (N_B_LAYERS, B_HEADS, 2 * MAX_REL + 1), 0.5)
    b_w_out = nrm(ks[19], (N_B_LAYERS, B_WIDTH, D_MODEL), B_WIDTH)
    return {"x": x, "c": c, "ada_w": ada_w, "ada_b": ada_b, "g_pre": g_pre, "g_post": g_post,
            "a_w_in": a_w_in, "a_conv_w": a_conv_w, "a_conv_b": a_conv_b, "a_gate_b": a_gate_b,
            "a_g_head": a_g_head, "a_w_out": a_w_out, "kv_ada_w": kv_ada_w, "kv_ada_b": kv_ada_b,
            "kv_g": kv_g, "kv_w": kv_w, "b_w_in": b_w_in, "b_rel": b_rel, "b_w_out": b_w_out}


def reference(x, c, ada_w, ada_b, g_pre, g_post, a_w_in, a_conv_w, a_conv_b, a_gate_b, a_g_head,
              a_w_out, kv_ada_w, kv_ada_b, kv_g, kv_w, b_w_in, b_rel, b_w_out):
    Bn, S, _ = x.shape
    sc = jax.nn.silu(c.astype(F32))
    k_sh = None
    v_sh = None
    for l in range(DEPTH):
        mod = sc @ ada_w[l].astype(F32) + ada_b[l].astype(F32)
        shift, scale, gate = mod[:, :D_MODEL], mod[:, D_MODEL:2 * D_MODEL], mod[:, 2 * D_MODEL:]
        h = modulate(rmsnorm(x, g_pre[l]), shift, scale).astype(x.dtype)
        if l < N_A_LAYERS:
            y = mlstm_mixer(h, a_w_in[l], a_conv_w[l], a_conv_b[l], a_gate_b[l], a_g_head[l], a_w_out[l])
        else:
            if l == N_A_LAYERS:
                kmod = sc @ kv_ada_w.astype(F32) + kv_ada_b.astype(F32)
                hkv = modulate(rmsnorm(x, kv_g), kmod[:, :D_MODEL], kmod[:, D_MODEL:]).astype(x.dtype)
                kv = hkv @ kv_w
                k_sh = kv[..., :B_WIDTH].reshape(Bn, S, B_HEADS, B_DH)
                v_sh = kv[..., B_WIDTH:].reshape(Bn, S, B_HEADS, B_DH)
            j = l - N_A_LAYERS
            y = band_attention_mixer(h, k_sh, v_sh, b_w_in[j], b_rel[j], b_w_out[j])
        x = (x.astype(F32) + gate[:, None, :] * rmsnorm(y, g_post[l])).astype(x.dtype)
    return x
```

```python
import numpy as np
import concourse.bass as bass
import concourse.mybir as mybir
from concourse.bass_utils import run_bass_kernel_spmd
from contextlib import ExitStack

F32 = mybir.dt.float32
BF16 = mybir.dt.bfloat16
AF = mybir.ActivationFunctionType
ALU = mybir.AluOpType
AX = mybir.AxisListType

ENGS = ("pe", "act", "dve", "pool", "sp")
NDSEM = 40

D = 2048
SEQ = 4096
NB = 4
EPS = 1e-6
A_INNER = 4096
A_COLS = 16400
A_V0 = 4096
A_O0 = 8192
A_Z0 = 12288
A_G0 = 16384
WIN = 4096
HALO0 = 1536
OWN0 = 2048
NH1 = WIN - HALO0
BIG = 30000.0
LN16 = float(np.log(16.0))


class Buf:
    __slots__ = ("name", "w", "r")

    def __init__(self, name=""):
        self.name = name
        self.w = None
        self.r = {}


class Sched:
    def __init__(self, nc, ctx):
        self.nc = nc
        self.ctx = ctx
        self.ops = {e: [] for e in ENGS}
        self.known = {e: {} for e in ENGS}
        self.awaited = {e: set() for e in ENGS}
        self.pend = {e: [] for e in ENGS}
        self.last = {e: None for e in ENGS}
        self.ndma = 0
        self.dma_tok = [None] * NDSEM
        self.nbuf = 0

    def buf(self, name=""):
        self.nbuf += 1
        return Buf(name or f"b{self.nbuf}")

    def bufs(self, n, name=""):
        return [self.buf(f"{name}{i}") for i in range(n)]

    def _collect(self, eng, reads, writes, extra=()):
        deps = {}

        def add(tok):
            if tok is None:
                return
            sk, v = tok
            if sk == eng and eng == "pe":
                return
            if self.known[eng].get(sk, 0) >= v:
                return
            if deps.get(sk, 0) < v:
                deps[sk] = v

        for b in reads:
            add(b.w)
        for b in writes:
            add(b.w)
            for t in b.r.values():
                add(t)
        for t in extra:
            add(t)
        waits = []
        for sk, v in deps.items():
            self.known[eng][sk] = v
            waits.append((sk, v))
            if not isinstance(sk, tuple):
                self.awaited[sk].add(v)
        return waits

    def op(self, eng, fn, reads=(), writes=()):
        waits = self.pend[eng] + self._collect(eng, reads, writes)
        self.pend[eng] = []
        idx = len(self.ops[eng]) + 1
        self.ops[eng].append((waits, fn, None))
        tok = (eng, idx)
        self.last[eng] = tok
        for b in reads:
            b.r[eng] = tok
        for b in writes:
            b.w = tok
            b.r = {}
        return tok

    def dma(self, q, out, in_, reads=(), writes=(), slow=False):
        k = self.ndma % NDSEM
        n = self.ndma // NDSEM + 1
        self.ndma += 1
        waits = self.pend[q] + self._collect(q, reads, writes, extra=(self.dma_tok[k],))
        self.pend[q] = []
        tok = (("d", k), n)
        self.dma_tok[k] = tok
        self.ops[q].append((waits, (out, in_, slow), tok))
        for b in reads:
            b.r[("d", k)] = tok
        for b in writes:
            b.w = tok
            b.r = {}
        return tok

    def wait_all(self, eng, toks):
        self.pend[eng] = self.pend[eng] + self._collect(eng, (), (), extra=toks)

    def barrier(self):
        toks = [t for t in self.last.values() if t is not None] + [t for t in self.dma_tok if t is not None]
        for e in ENGS:
            self.wait_all(e, toks)

    def finalize(self):
        nc = self.nc
        ctx = self.ctx
        esem = {e: ctx.enter_context(nc.semaphore(f"s_{e}")) for e in ENGS}
        dsem = [ctx.enter_context(nc.semaphore(f"d_{k}")) for k in range(NDSEM)]
        val = {}
        for e in ENGS:
            aw = sorted(self.awaited[e])
            val[e] = {idx: i + 1 for i, idx in enumerate(aw)}
        self.maxval = {e: len(val[e]) for e in ENGS}

        def emit_waits(engobj, waits):
            for sk, v in waits:
                if isinstance(sk, tuple):
                    engobj.wait_ge(dsem[sk[1]], 16 * v)
                else:
                    engobj.wait_ge(esem[sk], val[sk][v])

        def emit(e, engobj):
            for i, (waits, fn, dtok) in enumerate(self.ops[e]):
                emit_waits(engobj, waits)
                if dtok is not None:
                    out, in_, slow = fn
                    if slow:
                        engobj.dma_start(out=out, in_=in_, allow_slow_non_contiguous=True).then_inc(dsem[dtok[0][1]], 16)
                    else:
                        engobj.dma_start(out=out, in_=in_).then_inc(dsem[dtok[0][1]], 16)
                else:
                    ins = fn(engobj)
                    if (i + 1) in val[e]:
                        ins.then_inc(esem[e], 1)
            emit_waits(engobj, self.pend[e])

        with nc.Block() as block:
            @block.tensor
            def _(eng):
                emit("pe", eng)

            @block.scalar
            def _(eng):
                emit("act", eng)

            @block.vector
            def _(eng):
                emit("dve", eng)

            @block.gpsimd
            def _(eng):
                emit("pool", eng)

            @block.sync
            def _(eng):
                emit("sp", eng)


def build_program(debug=False, stop_after=99):
    nc = bass.Bass("TRN2", target_bir_lowering=False)

    def din(name, shape, dt=F32):
        return nc.dram_tensor(name, list(shape), dt, kind="ExternalInput").ap()

    def dscr(name, shape, dt):
        kind = "ExternalOutput" if debug else "Internal"
        return nc.dram_tensor(name, list(shape), dt, kind=kind).ap()

    xw = din("xw", [WIN, D])
    cT = din("cT", [128, 16])
    flag = din("flag", [128, 2])
    consts = din("consts", [128, 4, 128])
    ada_w = din("ada_w", [2, D, 3 * D])
    ada_b = din("ada_b", [2, 3 * D])
    g_pre = din("g_pre", [2, D])
    g_post = din("g_post", [2, D])
    a_w_in = din("a_w_in", [D, A_COLS])
    convw_l = din("convw_l", [128, 32, 4])
    convb_l = din("convb_l", [128, 32])
    gateb_l = din("gateb_l", [8, 2])
    ghead_l = din("ghead_l", [128, 32])
    gvec_l = din("gvec_l", [128, 3, 16])
    a_w_out = din("a_w_out", [A_INNER, D])
    kv_ada_w = din("kv_ada_w", [D, 2 * D])
    kv_ada_b = din("kv_ada_b", [1, 2 * D])
    kv_g = din("kv_g", [1, D])
    kv_w = din("kv_w", [D, 2 * D])
    b_w_in = din("b_w_in", [D, 2 * D])
    bias_l = din("bias_l", [16, 128, 640])
    amask = din("amask", [128, 640])
    b_w_out = din("b_w_out", [D, D])
    y_out = nc.dram_tensor("y_out", [2048, D], F32, kind="ExternalOutput").ap()

    VECS = dscr("VECS", [8, D], F32)
    QKs = dscr("QKs", [32, 128, WIN], BF16)
    Vs = dscr("Vs", [WIN, A_INNER], BF16)
    GZs = dscr("GZs", [32, 128, NH1], BF16)
    GATES = dscr("GATES", [16, WIN], F32)
    YT = dscr("YT", [32, 128, NH1], BF16)
    X1 = dscr("X1", [NH1, D], F32)
    HKVT = dscr("HKVT", [16, 128, NH1], BF16)
    H1T = dscr("H1T", [16, 128, 2048], BF16)
    KT = dscr("KT", [16, 128, NH1], BF16)
    V1 = dscr("V1", [NH1, 2048], BF16)
    QT = dscr("QT", [16, 128, 2048], BF16)
    Z1 = dscr("Z1", [16, 128, 2048], BF16)
    YT1 = dscr("YT1", [16, 128, 2048], BF16)

    with ExitStack() as ctx:
        S = Sched(nc, ctx)

        uniq = [0]

        def sb(c, name, shape, dt):
            uniq[0] += 1
            return c.enter_context(nc.sbuf_tensor(f"{name}_{uniq[0]}", list(shape), dt))

        PS = [ctx.enter_context(nc.psum_tensor(f"ps{i}", [128, 512], F32)) for i in range(8)]
        PB = S.bufs(8, "ps")
        psrr = [0]

        def next_ps():
            i = psrr[0] % 8
            psrr[0] += 1
            return PS[i], PB[i]

        cst = sb(ctx, "cst", [128, 4, 128], F32)
        cstb = sb(ctx, "cstb", [128, 4, 128], BF16)
        flg = sb(ctx, "flg", [128, 2], F32)
        scT = sb(ctx, "scT", [128, 16], F32)
        Bc = S.buf("cst")
        Bcb = S.buf("cstb")
        Bflg = S.buf("flg")
        BscT = S.buf("scT")
        S.dma("sp", cst[:], consts, writes=[Bc])
        S.dma("sp", flg[:], flag, writes=[Bflg])
        S.dma("sp", scT[:], cT, writes=[BscT])
        S.op("dve", lambda e: e.tensor_copy(cstb[:], cst[:]), reads=[Bc], writes=[Bcb])
        S.op("act", lambda e: e.activation(scT[:], scT[:], AF.Silu), reads=[BscT], writes=[BscT])
        ident_b = cstb[:, 0, :]
        ones_f = cst[:, 1, :]
        ident_f = cst[:, 0, :]

        def gemv_items(pc, glist, nwt=4):
            wt = [sb(pc, f"p0w{i}", [128, 1024], F32) for i in range(nwt)]
            Bw = S.bufs(nwt, "p0w")
            accs = [sb(pc, f"p0acc{i}", [128, 2048], F32) for i in range(2)]
            Baccs = S.bufs(2, "p0acc")
            bia = sb(pc, "p0b", [128, 2048], F32)
            Bbia = S.buf("p0b")
            vec = bia
            groups = []
            for l in range(2):
                for j in range(3):
                    groups.append((ada_w[l, :, j * D:(j + 1) * D], ada_b[l:l + 1, j * D:(j + 1) * D]))
            for j in range(2):
                groups.append((kv_ada_w[:, j * D:(j + 1) * D], kv_ada_b[0:1, j * D:(j + 1) * D]))
            chunks = []
            for gi, g in enumerate(glist):
                for kc in range(16):
                    for hc in range(2):
                        chunks.append((g, kc, hc, gi % 2))
            st = {"iss": 0}

            def issue():
                i = st["iss"]
                if i >= len(chunks):
                    return
                st["iss"] += 1
                g, kc, hc, ai = chunks[i]
                wap, bap = groups[g]
                S.dma("sp", wt[i % nwt][:], wap[kc * 128:(kc + 1) * 128, hc * 1024:(hc + 1) * 1024], writes=[Bw[i % nwt]])

            def mk_chunk(i):
                g, kc, hc, ai = chunks[i]
                acc = accs[ai]
                Bacc = Baccs[ai]

                def f():
                    while st["iss"] <= i + nwt - 1 and st["iss"] < len(chunks):
                        issue()
                    w = wt[i % nwt]
                    bw = Bw[i % nwt]
                    asl = acc[:, hc * 1024:(hc + 1) * 1024]
                    if kc == 0:
                        S.op("dve", lambda e: e.tensor_scalar(asl, w[:], scT[:, kc:kc + 1], None, ALU.mult),
                             reads=[bw, BscT], writes=[Bacc])
                    else:
                        S.op("dve", lambda e: e.scalar_tensor_tensor(asl, w[:], scT[:, kc:kc + 1], asl, ALU.mult, ALU.add),
                             reads=[bw, BscT, Bacc], writes=[Bacc])
                return f

            def mk_fin(g, ai):
                acc = accs[ai]
                Bacc = Baccs[ai]

                def f():
                    wap, bap = groups[g]
                    S.dma("sp", bia[:], bap.partition_broadcast(128), writes=[Bbia])
                    pss = []
                    for q in range(4):
                        p, pb = next_ps()
                        S.op("pe", lambda e, p=p, q=q: e.matmul(p[:], ones_f, acc[:, q * 512:(q + 1) * 512], start=True, stop=True),
                             reads=[Bacc, Bc], writes=[pb])
                        pss.append((p, pb))
                    for q, (p, pb) in enumerate(pss):
                        S.op("dve", lambda e, p=p, q=q: e.tensor_tensor(vec[:, q * 512:(q + 1) * 512], p[:], bia[:, q * 512:(q + 1) * 512], ALU.add),
                             reads=[pb, Bbia], writes=[Bbia])
                    S.dma("sp", VECS[g:g + 1, :], vec[0:1, :], reads=[Bbia])
                return f

            items = []
            pending = None
            i = 0
            for gi, g in enumerate(glist):
                for kc in range(16):
                    for hc in range(2):
                        items.append(mk_chunk(i))
                        i += 1
                    if kc == 12 and pending is not None:
                        items.append(pending)
                        pending = None
                pending = mk_fin(g, gi % 2)
            items.append(pending)
            return items

        def phase0():
            with ExitStack() as pc:
                for it in gemv_items(pc, [0, 1], nwt=4):
                    it()
            S.barrier()

        gemv_rest = {"items": None}

        def pump(n=1):
            its = gemv_rest["items"]
            for _ in range(n):
                if its:
                    its.pop(0)()

        def phaseAB(half):
            T0 = half * 2048
            with ExitStack() as pc:
                hT = sb(pc, "hT", [128, 16, 2048], BF16)
                BhT = S.bufs(16, "hT")
                with ExitStack() as pa:
                    xt = [sb(pa, f"xt{i}", [128, 2048], F32) for i in range(2)]
                    Bxt = S.bufs(2, "xt")
                    xnb = [sb(pa, f"xnbA{i}", [128, 2048], BF16) for i in range(2)]
                    Bxnb = S.bufs(2, "xnbA")
                    ss = sb(pa, "ssA", [128, 16], F32)
                    Bss = S.bufs(16, "ssA")
                    fvA = sb(pa, "fvA", [128, 4, 16], F32)
                    BfvA = S.buf("fvA")
                    S.dma("sp", fvA[:, 3, :], gvec_l[:, 2, :], writes=[BfvA])
                    S.dma("sp", fvA[:, 2, :], VECS[1, :].rearrange("(k p) -> p k", p=128), writes=[BfvA], slow=True)
                    S.dma("sp", fvA[:, 1, :], VECS[0, :].rearrange("(k p) -> p k", p=128), writes=[BfvA], slow=True)
                    S.op("dve", lambda e: e.scalar_tensor_tensor(fvA[:, 0, :], fvA[:, 2, :], 1.0, fvA[:, 3, :], ALU.add, ALU.mult), reads=[BfvA], writes=[BfvA])
                    pendA = []
                    for tb in range(16):
                        x_ = xt[tb % 2]
                        bx = Bxt[tb % 2]
                        xn_ = xnb[tb % 2]
                        bxn = Bxnb[tb % 2]
                        S.dma("sp", x_[:], xw[T0 + tb * 128:T0 + (tb + 1) * 128, :], writes=[bx])
                        sc_ = ss[:, tb:tb + 1]
                        S.op("act", lambda e, x_=x_, sc_=sc_, xn_=xn_: e.activation(xn_[:], x_[:], AF.Square, accum_out=sc_),
                             reads=[bx], writes=[bxn, Bss[tb]])
                        while pendA:
                            pendA.pop(0)()
                        S.op("dve", lambda e, sc_=sc_: e.tensor_scalar(sc_, sc_, 1.0 / D, EPS, ALU.mult, ALU.add),
                             reads=[Bss[tb]], writes=[Bss[tb]])
                        S.op("act", lambda e, sc_=sc_: e.sqrt(sc_, sc_), reads=[Bss[tb]], writes=[Bss[tb]])
                        S.op("dve", lambda e, sc_=sc_: e.reciprocal(sc_, sc_), reads=[Bss[tb]], writes=[Bss[tb]])
                        S.op("dve", lambda e, x_=x_, sc_=sc_, xn_=xn_: e.tensor_scalar(xn_[:], x_[:], sc_, None, ALU.mult),
                             reads=[bx, Bss[tb]], writes=[bxn])
                        for hh in range(2):
                            p, pb = next_ps()
                            pv = p[:].bitcast(BF16)
                            for j in range(8):
                                kc = hh * 8 + j
                                S.op("pe", lambda e, pv=pv, j=j, kc=kc, xn_=xn_: e.transpose(pv[:, j * 128:(j + 1) * 128], xn_[:, kc * 128:(kc + 1) * 128], ident_b),
                                     reads=[bxn, Bcb], writes=[pb])
                            def evA(pv=pv, pb=pb, hh=hh, tb=tb):
                                for j in range(8):
                                    kc = hh * 8 + j
                                    S.op("act", lambda e, pv=pv, j=j, kc=kc, tb=tb: e.activation(hT[:, kc, tb * 128:(tb + 1) * 128], pv[:, j * 128:(j + 1) * 128], AF.Identity,
                                                                                                bias=fvA[:, 1, kc:kc + 1], scale=fvA[:, 0, kc:kc + 1]),
                                         reads=[pb, BfvA], writes=[BhT[tb]])
                            pendA.append(evA)
                    while pendA:
                        pendA.pop(0)()
                S.barrier()
                if stop_after == 1:
                    return
                with ExitStack() as pb_:
                    wst = [sb(pb_, f"wst{i}", [128, 16, 256], F32) for i in range(3)]
                    Bwst = S.bufs(3, "wst")
                    wbf = [sb(pb_, f"wbf{i}", [128, 16, 256], BF16) for i in range(3)]
                    Bwbf = S.bufs(3, "wbf")
                    Bwbf2 = S.bufs(3, "wbf2")
                    cw = sb(pb_, "cw", [128, 32, 4], F32)
                    cb = sb(pb_, "cb", [128, 32], F32)
                    gh = sb(pb_, "gh", [128, 32], F32)
                    Bcw = S.buf("cw")
                    S.dma("sp", cw[:], convw_l, writes=[Bcw])
                    S.dma("sp", cb[:], convb_l, writes=[Bcw])
                    S.dma("sp", gh[:], ghead_l, writes=[Bcw])
                    cwf = sb(pb_, "cwf", [128, 32], F32)
                    if half == 0:
                        S.op("dve", lambda e: e.tensor_scalar(cwf[:], cw[:, :, 3], flg[:, 0:1], None, ALU.mult), reads=[Bcw, Bflg], writes=[Bcw])
                    else:
                        S.op("dve", lambda e: e.tensor_copy(cwf[:], cw[:, :, 3]), reads=[Bcw], writes=[Bcw])
                    ub = [sb(pb_, f"ub{i}", [128, 515], F32) for i in range(2)]
                    Bub = S.bufs(2, "ub")
                    cacc = [sb(pb_, f"cacc{i}", [128, 512], F32) for i in range(2)]
                    Bcacc = S.bufs(2, "cacc")
                    ob = [sb(pb_, f"ob{i}", [128, 512], BF16) for i in range(3)]
                    Bob = S.bufs(3, "ob")
                    so = [sb(pb_, f"so{i}", [128, 512], F32) for i in range(2)]
                    Bso = S.bufs(2, "so")
                    sz = [sb(pb_, f"sz{i}", [128, 512], F32) for i in range(2)]
                    Bsz = S.bufs(2, "sz")
                    gt = [sb(pb_, f"gt{i}", [16, 512], F32) for i in range(2)]
                    Bgt = S.bufs(2, "gt")
                    cnt = {"w": 0, "u": 0, "ob": 0, "so": 0, "gt": 0, "iss": 0}
                    if half == 1:
                        gemv_rest["items"] = gemv_items(pb_, [2, 3, 4, 5, 6, 7], nwt=4)
                    w_view = a_w_in.rearrange("(kc p) c -> p kc c", p=128)
                    wseq = [(g * 256, 256) for g in range(16)] + [(A_V0 + g * 256, 256) for g in range(16)]
                    for g in range(16):
                        wseq += [(A_O0 + g * 256, 256), (A_Z0 + g * 256, 256)]
                    wseq.append((A_G0, 16))

                    cnt["cv"] = 0

                    def issue_load():
                        k = cnt["iss"]
                        cnt["iss"] += 1
                        c0, ncols = wseq[k]
                        i = k % 3
                        S.dma("sp", wst[i][:, :, 0:ncols], w_view[:, :, c0:c0 + ncols], writes=[Bwst[i]])

                    def issue_cv():
                        k = cnt["cv"]
                        cnt["cv"] += 1
                        c0, ncols = wseq[k]
                        i = k % 3
                        S.op("pool", lambda e, i=i, ncols=ncols: e.tensor_copy(wbf[i][:, 0:8, 0:ncols], wst[i][:, 0:8, 0:ncols]),
                             reads=[Bwst[i]], writes=[Bwbf[i]])
                        S.op("act", lambda e, i=i, ncols=ncols: e.copy(wbf[i][:, 8:16, 0:ncols], wst[i][:, 8:16, 0:ncols]),
                             reads=[Bwst[i]], writes=[Bwbf2[i]])

                    def load_w(c0, ncols):
                        k = cnt["w"]
                        assert wseq[k] == (c0, ncols)
                        cnt["w"] += 1
                        while cnt["iss"] <= k + 2 and cnt["iss"] < len(wseq):
                            issue_load()
                        while cnt["cv"] <= k + 1 and cnt["cv"] < len(wseq):
                            issue_cv()
                        i = k % 3
                        return wbf[i], [Bwbf[i], Bwbf2[i]]

                    def mm_feat(w, bw, cblk, tt, M=128):
                        p, pb = next_ps()
                        for kc in range(16):
                            S.op("pe", lambda e, p=p, kc=kc: e.matmul(p[0:M, :], w[:, kc, cblk * 128:cblk * 128 + M],
                                                                      hT[:, kc, tt * 512:(tt + 1) * 512],
                                                                      start=(kc == 0), stop=(kc == 15)),
                                 reads=bw + BhT[tt * 4:(tt + 1) * 4], writes=[pb])
                        return p, pb

                    tts_all = [0, 1, 2, 3]
                    tts_full = [3] if half == 0 else [0, 1, 2, 3]
                    tts_q = [2, 3] if half == 0 else [0, 1, 2, 3]

                    pend_silu = []
                    for grp in range(16):
                        isq = grp < 8
                        tts = tts_q if isq else tts_all
                        w, bw = load_w(grp * 256, 256)
                        for cblk in range(2):
                            blk = grp * 2 + cblk
                            for tt in tts:
                                p, pb = mm_feat(w, bw, cblk, tt)
                                i = cnt["u"] % 2
                                cnt["u"] += 1
                                u = ub[i]
                                bu = Bub[i]
                                ca = cacc[i]
                                bca = Bcacc[i]
                                first = (half == 0 and tt == tts[0])
                                if first:
                                    S.op("dve", lambda e, u=u: e.memset(u[:, 0:3], 0.0), writes=[bu])
                                else:
                                    S.op("act", lambda e, u=u, blk=blk: e.copy(u[:, 0:3], carry[:, blk, :]),
                                         reads=[Bcarry[blk]], writes=[bu])
                                if half == 0:
                                    S.op("act", lambda e, u=u, p=p: e.activation(u[:, 3:515], p[:], AF.Copy, scale=flg[:, 0:1]),
                                         reads=[pb, Bflg, bu], writes=[bu])
                                else:
                                    S.op("act", lambda e, u=u, p=p: e.copy(u[:, 3:515], p[:]), reads=[pb, bu], writes=[bu])
                                S.op("act", lambda e, u=u, blk=blk: e.copy(carry[:, blk, :], u[:, 512:515]),
                                     reads=[bu], writes=[Bcarry[blk]])
                                S.op("act", lambda e, p=p, ca=ca, blk=blk: e.activation(ca[:], p[:], AF.Identity, bias=cb[:, blk:blk + 1], scale=cwf[:, blk:blk + 1]),
                                     reads=[pb, Bcw], writes=[bca])
                                for j in (2, 1, 0):
                                    S.op("dve", lambda e, u=u, ca=ca, blk=blk, j=j: e.scalar_tensor_tensor(ca[:], u[:, j:j + 512], cw[:, blk, j:j + 1], ca[:], ALU.mult, ALU.add),
                                         reads=[bu, Bcw, bca], writes=[bca])
                                def fin_silu(ca=ca, bca=bca, blk=blk, tt=tt):
                                    oi = cnt["ob"] % 3
                                    cnt["ob"] += 1
                                    S.op("act", lambda e, ca=ca, oi=oi: e.activation(ob[oi][:], ca[:], AF.Silu),
                                         reads=[bca], writes=[Bob[oi]])
                                    S.dma("sp", QKs[blk, :, T0 + tt * 512:T0 + (tt + 1) * 512], ob[oi][:], reads=[Bob[oi]])
                                while pend_silu:
                                    pend_silu.pop(0)()
                                pend_silu.append(fin_silu)
                    while pend_silu:
                        pend_silu.pop(0)()
                    for grp in range(16):
                        w, bw = load_w(A_V0 + grp * 256, 256)
                        for tb2 in range(8):
                            p, pb = next_ps()
                            for s2 in range(2):
                                tb = tb2 * 2 + s2
                                for kc in range(16):
                                    S.op("pe", lambda e, p=p, kc=kc, tb=tb, s2=s2, w=w: e.matmul(p[:, s2 * 256:(s2 + 1) * 256], hT[:, kc, tb * 128:(tb + 1) * 128],
                                                                                               w[:, kc, :], start=(kc == 0), stop=(kc == 15)),
                                         reads=bw + [BhT[tb]], writes=[pb])
                            oi = cnt["ob"] % 3
                            cnt["ob"] += 1
                            S.op("act", lambda e, p=p, oi=oi: e.copy(ob[oi][:], p[:]), reads=[pb], writes=[Bob[oi]])
                            S.dma("sp", Vs[T0 + tb2 * 256:T0 + (tb2 + 1) * 256, A_V0 - A_V0 + grp * 256:grp * 256 + 256].rearrange("(s p) c -> p s c", p=128),
                                  ob[oi][:].rearrange("p (s c) -> p s c", s=2), reads=[Bob[oi]])
                    for grp in range(16):
                        wo, bwo = load_w(A_O0 + grp * 256, 256)
                        wz, bwz = load_w(A_Z0 + grp * 256, 256)
                        for cblk in range(2):
                            eblk = grp * 2 + cblk
                            for tt in tts_full:
                                if half == 1:
                                    pump(2)
                                po, pbo = mm_feat(wo, bwo, cblk, tt)
                                pz, pbz = mm_feat(wz, bwz, cblk, tt)
                                i = cnt["so"] % 2
                                cnt["so"] += 1
                                S.op("act", lambda e, i=i, po=po: e.activation(so[i][:], po[:], AF.Sigmoid), reads=[pbo], writes=[Bso[i]])
                                S.op("act", lambda e, i=i, pz=pz: e.activation(sz[i][:], pz[:], AF.Silu), reads=[pbz], writes=[Bsz[i]])
                                oi = cnt["ob"] % 3
                                cnt["ob"] += 1
                                S.op("dve", lambda e, i=i, oi=oi, eblk=eblk: e.scalar_tensor_tensor(ob[oi][:], sz[i][:], gh[:, eblk:eblk + 1], so[i][:], ALU.mult, ALU.mult),
                                     reads=[Bso[i], Bsz[i], Bcw], writes=[Bob[oi]])
                                t0 = T0 + tt * 512 - HALO0
                                S.dma("sp", GZs[eblk, :, t0:t0 + 512], ob[oi][:], reads=[Bob[oi]])
                    if half == 1:
                        pump(1000)
                    w, bw = load_w(A_G0, 16)
                    for tt in tts_all:
                        p, pb = mm_feat(w, bw, 0, tt, M=16)
                        i = cnt["gt"] % 2
                        cnt["gt"] += 1
                        S.op("act", lambda e, i=i, p=p: e.copy(gt[i][:], p[0:16, :]), reads=[pb], writes=[Bgt[i]])
                        S.dma("sp", GATES[:, T0 + tt * 512:T0 + (tt + 1) * 512], gt[i][:], reads=[Bgt[i]])
            S.barrier()


        def phaseC():
            NBLK = WIN // 128
            FB0 = HALO0 // 128
            with ExitStack() as pc:
                TT = sb(pc, "TT", [128, 3, NBLK, 8], F32)
                BTT = S.buf("TT")
                Rend = sb(pc, "Rend", [128, NBLK, 8], F32)
                Rprev = sb(pc, "Rprev", [128, NBLK, 8], F32)
                fa = sb(pc, "fa", [128, NBLK, 8], F32)
                fbeta = sb(pc, "fbeta", [128, NBLK, 8], F32)
                fb2 = sb(pc, "fb2", [128, NBLK, 8], F32)
                fdec = sb(pc, "fdec", [128, NBLK, 8], F32)
                fem = sb(pc, "fem", [128, NBLK, 8], F32)
                Bfac = S.buf("fac")
                with ExitStack() as pg:
                    gi = sb(pg, "gi", [8, WIN], F32)
                    gf = sb(pg, "gf", [8, WIN], F32)
                    one8 = sb(pg, "one8", [8, WIN], F32)
                    Bn = sb(pg, "Bn", [8, WIN], F32)
                    U = sb(pg, "U", [8, WIN], F32)
                    R = sb(pg, "R", [8, WIN], F32)
                    M = sb(pg, "M", [8, WIN], F32)
                    gb = sb(pg, "gb", [8, 4], F32)
                    Bg = S.buf("gates")
                    S.dma("sp", gi[:], GATES[0:8, :], writes=[Bg])
                    S.dma("sp", gf[:], GATES[8:16, :], writes=[Bg])
                    S.dma("sp", gb[:, 0:2], gateb_l, writes=[Bg])
                    S.op("pool", lambda e: e.memset(one8[:], 1.0), writes=[Bg])
                    S.op("dve", lambda e: e.tensor_scalar(gi[:], gi[:], gb[:, 0:1], None, ALU.add), reads=[Bg], writes=[Bg])
                    S.op("dve", lambda e: e.tensor_scalar(gi[:, 0:OWN0], gi[:, 0:OWN0], flg[0:8, 0:1], flg[0:8, 1:2], ALU.mult, ALU.add),
                         reads=[Bg, Bflg], writes=[Bg])
                    S.op("dve", lambda e: e.tensor_scalar(gb[:, 2:3], gb[:, 1:2], -1.0, None, ALU.mult), reads=[Bg], writes=[Bg])
                    S.op("act", lambda e: e.activation(gf[:], gf[:], AF.Exp, bias=gb[:, 2:3], scale=-1.0), reads=[Bg], writes=[Bg])
                    S.op("dve", lambda e: e.tensor_scalar(gf[:], gf[:], 1.0, None, ALU.add), reads=[Bg], writes=[Bg])
                    S.op("act", lambda e: e.activation(gf[:], gf[:], AF.Ln), reads=[Bg], writes=[Bg])
                    S.op("dve", lambda e: e.tensor_scalar(gf[:, 0:OWN0], gf[:, 0:OWN0], flg[0:8, 0:1], None, ALU.mult), reads=[Bg, Bflg], writes=[Bg])
                    S.op("dve", lambda e: e.tensor_tensor_scan(Bn[:], one8[:], gf[:], 0.0, ALU.mult, ALU.add), reads=[Bg], writes=[Bg])
                    S.op("dve", lambda e: e.tensor_tensor(U[:], gi[:], Bn[:], ALU.add), reads=[Bg], writes=[Bg])
                    S.op("dve", lambda e: e.tensor_tensor_scan(R[:], U[:], U[:], 0.0, ALU.max, ALU.max), reads=[Bg], writes=[Bg])
                    S.op("dve", lambda e: e.tensor_tensor(M[:], R[:], Bn[:], ALU.subtract), reads=[Bg], writes=[Bg])
                    for qi, src in enumerate((U, R, M)):
                        for half_ in range(2):
                            p, pb = next_ps()
                            for jj in range(16):
                                j = half_ * 16 + jj
                                S.op("pe", lambda e, p=p, jj=jj, j=j, src=src: e.transpose(p[:, jj * 8:(jj + 1) * 8], src[0:8, j * 128:(j + 1) * 128], ident_f[0:8, 0:8]),
                                     reads=[Bg, Bc], writes=[pb])
                            S.op("act", lambda e, p=p, qi=qi, half_=half_: e.copy(TT[:, qi, half_ * 16:(half_ + 1) * 16, :], p[:, 0:128].rearrange("p (j h) -> p j h", h=8)),
                                 reads=[pb], writes=[BTT])
                    p, pb = next_ps()
                    S.op("pe", lambda e, p=p: e.matmul(p[:, 0:256], cst[:, 3, :], TT[:, 1, :, :], start=True, stop=True),
                         reads=[BTT, Bc], writes=[pb])
                    S.op("act", lambda e, p=p: e.copy(Rend[:], p[:, 0:256].rearrange("p (j h) -> p j h", h=8)), reads=[pb], writes=[Bfac])
                    S.op("pool", lambda e: e.memset(Rprev[:, 0, :], 0.0), writes=[Bfac])
                    S.op("dve", lambda e: e.tensor_copy(Rprev[:, 1:NBLK, :], Rend[:, 0:NBLK - 1, :]), reads=[Bfac], writes=[Bfac])
                    S.op("dve", lambda e: e.tensor_tensor(fa[:], TT[:, 0, :, :], Rend[:], ALU.subtract), reads=[BTT, Bfac], writes=[Bfac])
                    S.op("act", lambda e: e.activation(fa[:], fa[:], AF.Exp), reads=[Bfac], writes=[Bfac])
                    S.op("dve", lambda e: e.tensor_tensor(fbeta[:], Rend[:], TT[:, 1, :, :], ALU.subtract), reads=[BTT, Bfac], writes=[Bfac])
                    S.op("dve", lambda e: e.tensor_scalar(fbeta[:], fbeta[:], -LN16, None, ALU.add), reads=[Bfac], writes=[Bfac])
                    S.op("act", lambda e: e.activation(fbeta[:], fbeta[:], AF.Exp), reads=[Bfac], writes=[Bfac])
                    S.op("dve", lambda e: e.scalar_tensor_tensor(fb2[:], fbeta[:], 1.0 / 512.0, fbeta[:], ALU.mult, ALU.mult), reads=[Bfac], writes=[Bfac])
                    S.op("dve", lambda e: e.tensor_tensor(fdec[:], Rprev[:], Rend[:], ALU.subtract), reads=[Bfac], writes=[Bfac])
                    S.op("act", lambda e: e.activation(fdec[:], fdec[:], AF.Exp), reads=[Bfac], writes=[Bfac])
                    S.op("dve", lambda e: e.tensor_tensor(fem[:], TT[:, 1, :, :], TT[:, 2, :, :], ALU.subtract), reads=[BTT], writes=[Bfac])
                    S.op("dve", lambda e: e.tensor_tensor(fem[:], fem[:], Rend[:], ALU.subtract), reads=[Bfac], writes=[Bfac])
                    S.op("dve", lambda e: e.tensor_scalar(fem[:], fem[:], 2.0, 2.0 * LN16 + float(np.log(EPS)), ALU.mult, ALU.add), reads=[Bfac], writes=[Bfac])
                    S.op("act", lambda e: e.activation(fem[:], fem[:], AF.Exp), reads=[Bfac], writes=[Bfac])
                S.barrier()
                kf = [sb(pc, f"kf{i}", [128, 16, 128], BF16) for i in range(2)]
                qf = [sb(pc, f"qf{i}", [128, 16, 128], BF16) for i in range(2)]
                vt = [sb(pc, f"vt{i}", [128, 4096], BF16) for i in range(2)]
                gz = [sb(pc, f"gz{i}", [128, 32, 128], BF16) for i in range(2)]
                yt = [sb(pc, f"yt{i}", [128, 32, 128], BF16) for i in range(2)]
                Bkf = S.bufs(2, "kf"); Bqf = S.bufs(2, "qf"); Bvt = S.bufs(2, "vt"); Bgz = S.bufs(2, "gz"); Byt = S.bufs(2, "yt")
                Cst = sb(pc, "Cst", [128, 8, 2, 512], F32)
                BC = S.bufs(8, "Cst")
                Nst = sb(pc, "Nst", [128, 8, 2], F32)
                BN = S.buf("Nst")
                Cb = [sb(pc, f"Cb{i}", [128, 2, 512], BF16) for i in range(2)]
                BCb = S.bufs(2, "Cb")
                Cbn = sb(pc, "Cbn", [128, 8, 2], BF16)
                BCbn = S.buf("Cbn")
                kT = [sb(pc, f"kT{i}", [128, 256], BF16) for i in range(3)]
                BkT = S.bufs(3, "kT")
                vp = [sb(pc, f"vp{i}", [128, 513], BF16) for i in range(3)]
                Bvp = S.bufs(3, "vp")
                sd = [sb(pc, f"sd{i}", [128, 128], BF16) for i in range(2)]
                Bsd = S.bufs(2, "sd")
                hn = [sb(pc, f"hn{i}", [128, 512], BF16) for i in range(2)]
                Bhn = S.bufs(2, "hn")
                junk = sb(pc, "junkC", [128, 512], BF16)
                Bjunk = S.buf("junkC")
                sm = sb(pc, "smC", [128, 8, 8], F32)
                Bsm = S.bufs(8, "smC")
                tri_f = cst[:, 2, :]
                S.op("pool", lambda e: e.memset(Cst[:], 0.0), writes=BC)
                S.op("pool", lambda e: e.memset(Nst[:], 0.0), writes=[BN])
                pKT, bKT = PS[0], PB[0]
                pS, bS = PS[1], PB[1]
                pACC = [PS[2], PS[3]]; bACC = [PB[2], PB[3]]
                pSM = PS[4]
                bDEN = S.bufs(8, "pden"); bPN = S.bufs(8, "ppn")
                pHT, bHT = PS[5], PB[5]
                pC = [PS[6], PS[7]]; bC = [PB[6], PB[7]]
                def loadsC(jb):
                    i2 = jb % 2
                    t0 = jb * 128
                    S.dma("sp", kf[i2][:], QKs[16:32, :, t0:t0 + 128].rearrange("b p t -> p b t"), writes=[Bkf[i2]])
                    S.dma("sp", vt[i2][:], Vs[t0:t0 + 128, :], writes=[Bvt[i2]])
                    if jb >= FB0:
                        S.dma("sp", qf[i2][:], QKs[0:16, :, t0:t0 + 128].rearrange("b p t -> p b t"), writes=[Bqf[i2]])
                        S.dma("sp", gz[i2][:], GZs[:, :, t0 - HALO0:t0 - HALO0 + 128].rearrange("b p t -> p b t"), writes=[Bgz[i2]])

                loadsC(0)
                for jb in range(NBLK):
                    full = jb >= FB0
                    i2 = jb % 2
                    t0 = jb * 128
                    if jb + 1 < NBLK:
                        loadsC(jb + 1)
                    if full:
                        S.op("dve", lambda e, jb=jb: e.tensor_tensor(Cbn[:], Nst[:], fdec[:, jb, :].unsqueeze(2).to_broadcast([128, 8, 2]), ALU.mult),
                             reads=[BN, Bfac], writes=[BCbn])
                    def A1(h, jb=jb, i2=i2, full=full):
                        hi = (jb * 8 + h) % 2
                        hk = (jb * 8 + h) % 3
                        a_ap = fa[:, jb, h:h + 1]
                        dec_ap = fdec[:, jb, h:h + 1]
                        pv = pKT[:].bitcast(BF16)
                        for c in range(2):
                            S.op("pe", lambda e, pv=pv, c=c, h=h: e.transpose(pv[:, c * 128:(c + 1) * 128], kf[i2][:, h * 2 + c, :], ident_b),
                                 reads=[Bkf[i2], Bcb], writes=[bKT])
                        if full:
                            for c in range(2):
                                S.op("pe", lambda e, c=c, h=h: e.matmul(pS[:, 0:128], kf[i2][:, h * 2 + c, :], qf[i2][:, h * 2 + c, :], start=(c == 0), stop=(c == 1)),
                                     reads=[Bkf[i2], Bqf[i2]], writes=[bS])
                        S.op("act", lambda e, pv=pv, hk=hk: e.copy(kT[hk][:], pv[:, 0:256]), reads=[bKT], writes=[BkT[hk]])
                        S.op("dve", lambda e, hk=hk, h=h, a_ap=a_ap: e.tensor_scalar(vp[hk][:, 0:512], vt[i2][:, h * 512:(h + 1) * 512], a_ap, None, ALU.mult),
                             reads=[Bvt[i2], Bfac], writes=[Bvp[hk]])
                        S.op("dve", lambda e, hk=hk, a_ap=a_ap: e.tensor_copy(vp[hk][:, 512:513], a_ap), reads=[Bfac, Bvp[hk]], writes=[Bvp[hk]])
                        if full:
                            S.op("act", lambda e, hi=hi, h=h, dec_ap=dec_ap: e.activation(Cb[hi][:], Cst[:, h, :, :], AF.Copy, scale=dec_ap),
                                 reads=[BC[h], Bfac], writes=[BCb[hi]])
                            S.op("dve", lambda e, hi=hi: e.tensor_tensor(sd[hi][:], pS[:, 0:128], tri_f, ALU.mult), reads=[bS, Bc], writes=[Bsd[hi]])

                    def A2(h, jb=jb, i2=i2, full=full):
                        if not full:
                            return
                        hi = (jb * 8 + h) % 2
                        hk = (jb * 8 + h) % 3
                        pa = pACC[hi]; ba = bACC[hi]
                        for c in range(2):
                            S.op("pe", lambda e, pa=pa, c=c, h=h, hi=hi: e.matmul(pa[:], qf[i2][:, h * 2 + c, :], Cb[hi][:, c, :], start=(c == 0), stop=False),
                                 reads=[Bqf[i2], BCb[hi]], writes=[ba])
                        S.op("pe", lambda e, pa=pa, hi=hi, hk=hk: e.matmul(pa[:], sd[hi][:], vp[hk][:, 0:512], start=False, stop=True),
                             reads=[Bsd[hi], Bvp[hk]], writes=[ba])
                        for c in range(2):
                            S.op("pe", lambda e, c=c, h=h: e.matmul(pSM[:, h:h + 1], qf[i2][:, h * 2 + c, :], Cbn[:, h, c:c + 1], start=(c == 0), stop=False),
                                 reads=[Bqf[i2], BCbn], writes=[bDEN[h]])
                        S.op("pe", lambda e, hi=hi, h=h, hk=hk: e.matmul(pSM[:, h:h + 1], sd[hi][:], vp[hk][:, 512:513], start=False, stop=True),
                             reads=[Bsd[hi], Bvp[hk]], writes=[bDEN[h]])

                    def B1(h, jb=jb, i2=i2, full=full):
                        if not full:
                            return
                        hi = (jb * 8 + h) % 2
                        pa = pACC[hi]; ba = bACC[hi]
                        smh = sm[:, h, :]
                        S.op("act", lambda e, pa=pa, smh=smh: e.activation(junk[:], pa[:], AF.Square, scale=512.0 ** -0.5, accum_out=smh[:, 0:1]),
                             reads=[ba], writes=[Bjunk, Bsm[h]])
                        S.op("act", lambda e, smh=smh, h=h: e.activation(smh[:, 1:2], pSM[:, h:h + 1], AF.Square, scale=EPS ** 0.5),
                             reads=[bDEN[h], Bsm[h]], writes=[Bsm[h]])
                        S.op("dve", lambda e, smh=smh, h=h, jb=jb: e.scalar_tensor_tensor(smh[:, 3:4], smh[:, 1:2], fem[:, jb, h:h + 1], smh[:, 0:1], ALU.max, ALU.add),
                             reads=[Bfac, Bsm[h]], writes=[Bsm[h]])
                        S.op("act", lambda e, smh=smh: e.sqrt(smh[:, 3:4], smh[:, 3:4]), reads=[Bsm[h]], writes=[Bsm[h]])
                        S.op("dve", lambda e, smh=smh: e.reciprocal(smh[:, 4:5], smh[:, 3:4]), reads=[Bsm[h]], writes=[Bsm[h]])
                        S.op("act", lambda e, pa=pa, hi=hi, smh=smh: e.activation(hn[hi][:], pa[:], AF.Copy, scale=smh[:, 4:5]),
                             reads=[ba, Bsm[h]], writes=[Bhn[hi]])

                    def B2(h, jb=jb, i2=i2, full=full):
                        hi = (jb * 8 + h) % 2
                        hk = (jb * 8 + h) % 3
                        dec_ap = fdec[:, jb, h:h + 1]
                        if full:
                            ph = pHT[:].bitcast(BF16)
                            for ec in range(4):
                                S.op("pe", lambda e, ph=ph, ec=ec, hi=hi: e.transpose(ph[:, ec * 128:(ec + 1) * 128], hn[hi][:, ec * 128:(ec + 1) * 128], ident_b),
                                     reads=[Bhn[hi], Bcb], writes=[bHT])
                            S.op("dve", lambda e, ph=ph, h=h: e.tensor_tensor(yt[i2][:, h * 4:(h + 1) * 4, :], ph[:, 0:512].rearrange("p (a t) -> p a t", a=4),
                                                                              gz[i2][:, h * 4:(h + 1) * 4, :], ALU.mult),
                                 reads=[bHT, Bgz[i2]], writes=[Byt[i2]])
                        for c in range(2):
                            S.op("pe", lambda e, c=c, hk=hk: e.matmul(pC[c][:], kT[hk][:, c * 128:(c + 1) * 128], vp[hk][:, 0:512], start=True, stop=True),
                                 reads=[BkT[hk], Bvp[hk]], writes=[bC[c]])
                            S.op("pe", lambda e, c=c, hk=hk, h=h: e.matmul(pSM[:, 8 + h * 2 + c:8 + h * 2 + c + 1], kT[hk][:, c * 128:(c + 1) * 128], vp[hk][:, 512:513], start=True, stop=True),
                                 reads=[BkT[hk], Bvp[hk]], writes=[bPN[h]])
                            S.op("dve", lambda e, c=c, h=h, dec_ap=dec_ap: e.scalar_tensor_tensor(Cst[:, h, c, :], Cst[:, h, c, :], dec_ap, pC[c][:], ALU.mult, ALU.add),
                                 reads=[BC[h], Bfac, bC[c]], writes=[BC[h]])

                    A1(0)
                    A2(0)
                    for h in range(8):
                        if h + 1 < 8:
                            A1(h + 1)
                        B1(h)
                        if h + 1 < 8:
                            A2(h + 1)
                        if h >= 1:
                            B2(h - 1)
                    B2(7)
                    S.op("dve", lambda e, jb=jb: e.tensor_tensor(Nst[:], Nst[:], fdec[:, jb, :].unsqueeze(2).to_broadcast([128, 8, 2]), ALU.mult),
                         reads=[BN, Bfac, BCbn], writes=[BN])
                    S.op("dve", lambda e: e.tensor_tensor(Nst[:], Nst[:], pSM[:, 8:24].rearrange("p (h c) -> p h c", c=2), ALU.add),
                         reads=[BN] + bPN, writes=[BN])
                    if full:
                        S.dma("sp", YT[:, :, t0 - HALO0:t0 - HALO0 + 128].rearrange("b p t -> p b t"), yt[i2][:], reads=[Byt[i2]])
            S.barrier()


        def phase_out(layer):
            if layer == 0:
                NE, w_ap, YTsrc, nblk = 32, a_w_out, YT, NH1 // 128
            else:
                NE, w_ap, YTsrc, nblk = 16, b_w_out, YT1, 16
            with ExitStack() as pc:
                wob = sb(pc, "wob", [128, NE, 2048], BF16)
                Bwob = S.bufs(NE, "wob")
                GP = sb(pc, "GP", [128, 2048], F32)
                BGP = S.buf("GP")
                fv = sb(pc, "fv", [128, 6, 16], F32)
                Bfv = S.buf("fv")
                gl = sb(pc, "gl", [128, 3, 16], F32)
                with ExitStack() as pw:
                    wst = [sb(pw, f"wstO{i}", [128, 2048], F32) for i in range(2)]
                    Bwst = S.bufs(2, "wstO")
                    for ec in range(NE):
                        i = ec % 2
                        S.dma("sp", wst[i][:], w_ap[ec * 128:(ec + 1) * 128, :], writes=[Bwst[i]])
                        if ec % 2 == 0:
                            S.op("dve", lambda e, i=i, ec=ec: e.tensor_copy(wob[:, ec, :], wst[i][:]), reads=[Bwst[i]], writes=[Bwob[ec]])
                        else:
                            S.op("act", lambda e, i=i, ec=ec: e.copy(wob[:, ec, :], wst[i][:]), reads=[Bwst[i]], writes=[Bwob[ec]])
                    tl = wst[0]
                    btl = Bwst[0]
                    gate_row = 2 if layer == 0 else 5
                    S.dma("sp", GP[:], VECS[gate_row:gate_row + 1, :].partition_broadcast(128), writes=[BGP])
                    S.dma("sp", tl[:], g_post[layer:layer + 1, :].partition_broadcast(128), writes=[btl])
                    S.op("dve", lambda e: e.tensor_tensor(GP[:], GP[:], tl[:], ALU.mult), reads=[btl, BGP], writes=[BGP])
                    if layer == 0:
                        S.dma("sp", gl[:], gvec_l, writes=[Bfv])
                        for slot, row in ((4, 7), (1, 6), (5, 4), (3, 3)):
                            S.dma("sp", fv[:, slot, :], VECS[row, :].rearrange("(k p) -> p k", p=128), writes=[Bfv], slow=True)
                        S.op("dve", lambda e: e.scalar_tensor_tensor(fv[:, 0, :], fv[:, 4, :], 1.0, gl[:, 1, :], ALU.add, ALU.mult), reads=[Bfv], writes=[Bfv])
                        S.op("dve", lambda e: e.scalar_tensor_tensor(fv[:, 2, :], fv[:, 5, :], 1.0, gl[:, 0, :], ALU.add, ALU.mult), reads=[Bfv], writes=[Bfv])
                    S.barrier()
                ytl = [sb(pc, f"ytl{i}", [128, NE, 128], BF16) for i in range(2)]
                Bytl = S.bufs(2, "ytl")
                xr = [sb(pc, f"xr{i}", [128, 2048], F32) for i in range(2)]
                Bxr = S.bufs(2, "xr")
                x1 = [sb(pc, f"x1{i}", [128, 2048], F32) for i in range(2)]
                Bx1 = S.bufs(2, "x1")
                Bx1q = [S.bufs(4, "x1q0"), S.bufs(4, "x1q1")]
                junk = sb(pc, "junkO", [128, 512], BF16)
                Bjunk = S.buf("junkO")
                ssq = sb(pc, "ssqO", [128, 8], F32)
                Bssq = S.buf("ssqO")
                pso = [PS[0], PS[1], PS[2], PS[3]]
                bso = [PB[0], PB[1], PB[2], PB[3]]
                if layer == 0:
                    xnb = [sb(pc, f"xnb{i}", [128, 2048], BF16) for i in range(2)]
                    Bxnb = S.bufs(2, "xnb")
                    hTt = [sb(pc, f"hTt{i}", [128, 16, 128], BF16) for i in range(2)]
                    BhTt = S.bufs(2, "hTt")
                    BhTt2 = S.bufs(2, "hTt2")
                ptr = [PS[4], PS[5], PS[6], PS[7]]
                btr = [PB[4], PB[5], PB[6], PB[7]]
                cnt = {"t": 0, "h": 0}

                def rstd_chain(col):
                    S.op("dve", lambda e: e.tensor_scalar(col, col, 1.0 / D, EPS, ALU.mult, ALU.add), reads=[Bssq], writes=[Bssq])
                    S.op("act", lambda e: e.sqrt(col, col), reads=[Bssq], writes=[Bssq])
                    S.op("dve", lambda e: e.reciprocal(col, col), reads=[Bssq], writes=[Bssq])

                def loadsO(jb):
                    i2 = jb % 2
                    t0 = jb * 128
                    S.dma("sp", ytl[i2][:], YTsrc[:, :, t0:t0 + 128].rearrange("b p t -> p b t"), writes=[Bytl[i2]])
                    if layer == 0:
                        S.dma("sp", xr[i2][:], xw[HALO0 + t0:HALO0 + t0 + 128, :], writes=[Bxr[i2]])
                    else:
                        S.dma("sp", xr[i2][:], X1[512 + t0:512 + t0 + 128, :], writes=[Bxr[i2]])

                deferred = []
                loadsO(0)
                for jb in range(nblk):
                    i2 = jb % 2
                    t0 = jb * 128
                    if jb + 1 < nblk:
                        loadsO(jb + 1)
                    order = [(q, ec) for ec in range(NE) for q in range(4)] if jb == 0 else [(q, ec) for q in range(4) for ec in range(NE)]
                    for q, ec in order:
                        S.op("pe", lambda e, q=q, ec=ec, i2=i2: e.matmul(pso[q][:], ytl[i2][:, ec, :], wob[:, ec, q * 512:(q + 1) * 512], start=(ec == 0), stop=(ec == NE - 1)),
                             reads=[Bytl[i2], Bwob[ec]], writes=[bso[q]])
                    for q in range(4):
                        S.op("act", lambda e, q=q, i2=i2: e.copy(x1[i2][:, q * 512:(q + 1) * 512], pso[q][:]),
                             reads=[bso[q]], writes=[Bx1q[i2][q]] + ([Bx1[i2]] if q == 0 else []))
                        S.op("dve", lambda e, q=q, i2=i2: e.scalar_tensor_tensor(junk[:], x1[i2][:, q * 512:(q + 1) * 512], 1.0, x1[i2][:, q * 512:(q + 1) * 512],
                                                                               ALU.mult, ALU.mult, accum_out=ssq[:, q:q + 1]),
                             reads=[Bx1q[i2][q]], writes=[Bjunk, Bssq])
                    S.op("dve", lambda e: e.tensor_reduce(ssq[:, 4:5], ssq[:, 0:4], AX.X, ALU.add), reads=[Bssq], writes=[Bssq])
                    rstd_chain(ssq[:, 4:5])
                    S.op("dve", lambda e, i2=i2: e.scalar_tensor_tensor(x1[i2][:], x1[i2][:], ssq[:, 4:5], GP[:], ALU.mult, ALU.mult),
                         reads=Bx1q[i2] + [Bssq, BGP, Bx1[i2]], writes=[Bx1[i2]] + Bx1q[i2])
                    S.op("dve", lambda e, i2=i2: e.tensor_tensor(x1[i2][:], x1[i2][:], xr[i2][:], ALU.add), reads=[Bxr[i2], Bx1[i2]], writes=[Bx1[i2]])
                    if layer == 0:
                        S.dma("sp", X1[t0:t0 + 128, :], x1[i2][:], reads=[Bx1[i2]])
                        S.op("act", lambda e, i2=i2: e.activation(xnb[i2][:], x1[i2][:], AF.Square, accum_out=ssq[:, 5:6]),
                             reads=[Bx1[i2]], writes=[Bxnb[i2], Bssq])
                        rstd_chain(ssq[:, 5:6])
                        S.op("dve", lambda e, i2=i2: e.tensor_scalar(xnb[i2][:], x1[i2][:], ssq[:, 5:6], None, ALU.mult),
                             reads=[Bx1[i2], Bssq], writes=[Bxnb[i2]])
                        def E2(jb=jb, i2=i2, t0=t0):
                            pvs = []
                            for hh in range(2):
                                p, pb = ptr[i2 * 2 + hh], btr[i2 * 2 + hh]
                                pv = p[:].bitcast(BF16)
                                for j in range(8):
                                    kc = hh * 8 + j
                                    S.op("pe", lambda e, pv=pv, j=j, kc=kc: e.transpose(pv[:, j * 128:(j + 1) * 128], xnb[i2][:, kc * 128:(kc + 1) * 128], ident_b),
                                         reads=[Bxnb[i2], Bcb], writes=[pb])
                                pvs.append((pv, pb))
                            sets = [(0, 1, HKVT[:, :, t0:t0 + 128])]
                            if jb >= 4:
                                sets.append((2, 3, H1T[:, :, t0 - 512:t0 - 512 + 128]))
                            for gs, ss_, dst in sets:
                                hi = cnt["h"] % 2
                                cnt["h"] += 1
                                for hh in range(2):
                                    pv, pb = pvs[hh]
                                    for j in range(8):
                                        kc = hh * 8 + j
                                        S.op("act", lambda e, pv=pv, j=j, kc=kc, gs=gs, ss_=ss_, hi=hi: e.activation(hTt[hi][:, kc, :], pv[:, j * 128:(j + 1) * 128], AF.Identity,
                                                                                                               bias=fv[:, ss_, kc:kc + 1], scale=fv[:, gs, kc:kc + 1]),
                                             reads=[pb, Bfv], writes=[BhTt[hi]])
                                S.dma("sp", dst.rearrange("k p t -> p k t"), hTt[hi][:], reads=[BhTt[hi], BhTt2[hi]], writes=[])
                        while deferred:
                            deferred.pop(0)()
                        deferred.append(E2)
                    else:
                        S.dma("sp", y_out[t0:t0 + 128, :], x1[i2][:], reads=[Bx1[i2]])
                while deferred:
                    deferred.pop(0)()
            S.barrier()

        def phase_proj1(src, ntok, w_ap, feat_dst, feat_silu, tok_dst):
            with ExitStack() as pc:
                aT = sb(pc, "aT", [128, 16, ntok], BF16)
                BaT = S.bufs(ntok // 128, "aT")
                for kc in range(16):
                    S.dma("sp", aT[:, kc, :], src[kc, :, :], writes=BaT)
                wst = [sb(pc, f"wstE{i}", [128, 16, 256], F32) for i in range(3)]
                Bwst = S.bufs(3, "wstE")
                wbf = [sb(pc, f"wbfE{i}", [128, 16, 256], BF16) for i in range(3)]
                Bwbf = S.bufs(3, "wbfE")
                ob = [sb(pc, f"obE{i}", [128, 512], BF16) for i in range(3)]
                Bob = S.bufs(3, "obE")
                cnt = {"w": 0, "ob": 0, "iss": 0}
                w_view = w_ap.rearrange("(kc p) c -> p kc c", p=128)
                ntt = ntok // 512
                wseq = [g * 256 for g in range(16)]

                cnt["cv"] = 0
                Bwbf2 = S.bufs(3, "wbfE2")

                def issue_load():
                    k = cnt["iss"]
                    cnt["iss"] += 1
                    c0 = wseq[k]
                    i = k % 3
                    S.dma("sp", wst[i][:], w_view[:, :, c0:c0 + 256], writes=[Bwst[i]])

                def issue_cv():
                    k = cnt["cv"]
                    cnt["cv"] += 1
                    i = k % 3
                    S.op("pool", lambda e, i=i: e.tensor_copy(wbf[i][:, 0:8, :], wst[i][:, 0:8, :]), reads=[Bwst[i]], writes=[Bwbf[i]])
                    S.op("act", lambda e, i=i: e.copy(wbf[i][:, 8:16, :], wst[i][:, 8:16, :]), reads=[Bwst[i]], writes=[Bwbf2[i]])

                def load_w(c0):
                    k = cnt["w"]
                    assert wseq[k] == c0
                    cnt["w"] += 1
                    while cnt["iss"] <= k + 2 and cnt["iss"] < len(wseq):
                        issue_load()
                    while cnt["cv"] <= k + 1 and cnt["cv"] < len(wseq):
                        issue_cv()
                    i = k % 3
                    return wbf[i], [Bwbf[i], Bwbf2[i]]

                def feat_group(c0, dst, silu):
                    w, bw = load_w(c0)
                    for cblk in range(2):
                        blk = (c0 % 2048) // 128 + cblk
                        for tt in range(ntt):
                            p, pb = next_ps()
                            for kc in range(16):
                                S.op("pe", lambda e, p=p, kc=kc, w=w, cblk=cblk, tt=tt: e.matmul(p[:], w[:, kc, cblk * 128:(cblk + 1) * 128], aT[:, kc, tt * 512:(tt + 1) * 512],
                                                                                             start=(kc == 0), stop=(kc == 15)),
                                     reads=bw + BaT[tt * 4:(tt + 1) * 4], writes=[pb])
                            oi = cnt["ob"] % 3
                            cnt["ob"] += 1
                            if silu:
                                S.op("act", lambda e, p=p, oi=oi: e.activation(ob[oi][:], p[:], AF.Silu), reads=[pb], writes=[Bob[oi]])
                            else:
                                S.op("act", lambda e, p=p, oi=oi: e.copy(ob[oi][:], p[:]), reads=[pb], writes=[Bob[oi]])
                            S.dma("sp", dst[blk, :, tt * 512:(tt + 1) * 512], ob[oi][:], reads=[Bob[oi]])

                for grp in range(8):
                    feat_group(grp * 256, feat_dst, False)
                for grp in range(8):
                    if tok_dst is None:
                        feat_group(2048 + grp * 256, feat_silu, True)
                    else:
                        w, bw = load_w(2048 + grp * 256)
                        for tb2 in range(ntok // 256):
                            p, pb = next_ps()
                            for s2 in range(2):
                                tb = tb2 * 2 + s2
                                for kc in range(16):
                                    S.op("pe", lambda e, p=p, kc=kc, tb=tb, s2=s2, w=w: e.matmul(p[:, s2 * 256:(s2 + 1) * 256], aT[:, kc, tb * 128:(tb + 1) * 128],
                                                                                               w[:, kc, :], start=(kc == 0), stop=(kc == 15)),
                                         reads=bw + [BaT[tb]], writes=[pb])
                            oi = cnt["ob"] % 3
                            cnt["ob"] += 1
                            S.op("act", lambda e, p=p, oi=oi: e.copy(ob[oi][:], p[:]), reads=[pb], writes=[Bob[oi]])
                            S.dma("sp", tok_dst[tb2 * 256:(tb2 + 1) * 256, grp * 256:(grp + 1) * 256].rearrange("(s p) c -> p s c", p=128),
                                  ob[oi][:].rearrange("p (s c) -> p s c", s=2), reads=[Bob[oi]])
            S.barrier()

        def phaseF():
            SCALE = 128.0 ** -0.5
            with ExitStack() as pc:
                bm = sb(pc, "bm", [128, 16, 640], F32)
                Bbm = S.buf("bm")
                am = sb(pc, "am", [128, 640], F32)
                S.dma("sp", bm[:], bias_l.rearrange("h p j -> p h j"), writes=[Bbm])
                S.dma("sp", am[:], amask, writes=[Bbm])
                S.op("dve", lambda e: e.tensor_tensor(bm[:], bm[:], am[:].unsqueeze(1).to_broadcast([128, 16, 640]), ALU.add), reads=[Bbm], writes=[Bbm])
                qT = [sb(pc, f"qT{i}", [128, 16, 128], BF16) for i in range(2)]
                Kt = [sb(pc, f"Kt{i}", [128, 16, 640], BF16) for i in range(2)]
                Vt = [sb(pc, f"Vt{i}", [128, 5, 2048], BF16) for i in range(2)]
                zt = [sb(pc, f"zt{i}", [128, 16, 128], BF16) for i in range(2)]
                yt = [sb(pc, f"ytF{i}", [128, 16, 128], BF16) for i in range(2)]
                BqT = S.bufs(2, "qT"); BKt = S.bufs(2, "Kt"); BVt = S.bufs(2, "Vt"); Bzt = S.bufs(2, "zt"); Byt = S.bufs(2, "ytF")
                st = [sb(pc, f"st{i}", [128, 640], F32) for i in range(3)]
                Bst = S.bufs(3, "st")
                pb16 = [sb(pc, f"pb16{i}", [128, 640], BF16) for i in range(3)]
                Bpb = S.bufs(3, "pb16")
                pT = [sb(pc, f"pT{i}", [128, 640], BF16) for i in range(2)]
                BpT = S.bufs(2, "pT")
                on = [sb(pc, f"on{i}", [128, 4, 128], BF16) for i in range(2)]
                Bon = S.bufs(2, "on")
                sm = sb(pc, "smF", [128, 3, 16], F32)
                Bsm = S.bufs(16, "smF")
                Brinv = S.bufs(4, "rinv")
                pA = [PS[0], PS[1], PS[2]]; bA = [PB[0], PB[1], PB[2]]
                pBk = [PS[3][:, k * 128:(k + 1) * 128] for k in range(3)]; bBk = S.bufs(3, "pBk")
                pTr, bTr = PS[4], PB[4]
                pO = [PS[5], PS[6]]; bO = [PB[5], PB[6]]
                pOT, bOT = PS[7], PB[7]
                def loadsF(qb):
                    i2 = qb % 2
                    t0 = qb * 128
                    S.dma("sp", qT[i2][:], QT[:, :, t0:t0 + 128].rearrange("h p t -> p h t"), writes=[BqT[i2]])
                    S.dma("sp", Kt[i2][:], KT[:, :, t0:t0 + 640].rearrange("h p t -> p h t"), writes=[BKt[i2]])
                    S.dma("sp", Vt[i2][:], V1[t0:t0 + 640, :].rearrange("(kb p) c -> p kb c", p=128), writes=[BVt[i2]])
                    S.dma("sp", zt[i2][:], Z1[:, :, t0:t0 + 128].rearrange("h p t -> p h t"), writes=[Bzt[i2]])

                loadsF(0)
                for qb in range(16):
                    i2 = qb % 2
                    t0 = qb * 128
                    if qb + 1 < 16:
                        loadsF(qb + 1)
                    def FA(h, qb=qb, i2=i2):
                        hi = h % 3
                        S.op("pe", lambda e, hi=hi, h=h: e.matmul(pA[hi][:], qT[i2][:, h, :], Kt[i2][:, h, 0:512], start=True, stop=True),
                             reads=[BqT[i2], BKt[i2]], writes=[bA[hi]])
                        S.op("pe", lambda e, hi=hi, h=h: e.matmul(pBk[hi], qT[i2][:, h, :], Kt[i2][:, h, 512:640], start=True, stop=True),
                             reads=[BqT[i2], BKt[i2]], writes=[bBk[hi]])
                        S.op("dve", lambda e, hi=hi, h=h: e.scalar_tensor_tensor(st[hi][:, 0:512], pA[hi][:], SCALE, bm[:, h, 0:512], ALU.mult, ALU.add),
                             reads=[bA[hi], Bbm], writes=[Bst[hi]])
                        S.op("dve", lambda e, hi=hi, h=h: e.scalar_tensor_tensor(st[hi][:, 512:640], pBk[hi], SCALE, bm[:, h, 512:640], ALU.mult, ALU.add),
                             reads=[bBk[hi], Bbm, Bst[hi]], writes=[Bst[hi]])
                        if qb < 4:
                            j0 = 512 - qb * 128
                            S.op("dve", lambda e, hi=hi, j0=j0: e.tensor_scalar(st[hi][:, 0:j0], st[hi][:, 0:j0], flg[:, 1:2], None, ALU.add),
                                 reads=[Bst[hi], Bflg], writes=[Bst[hi]])
                        S.op("dve", lambda e, hi=hi, h=h: e.tensor_reduce(sm[:, 0, h:h + 1], st[hi][:], AX.X, ALU.max, negate=True),
                             reads=[Bst[hi]], writes=[Bsm[h]])
                        S.op("act", lambda e, hi=hi, h=h: e.activation(pb16[hi][:], st[hi][:], AF.Exp, bias=sm[:, 0, h:h + 1], accum_out=sm[:, 1, h:h + 1]),
                             reads=[Bst[hi], Bsm[h]], writes=[Bpb[hi], Bsm[h]])

                    def FB(h, qb=qb, i2=i2):
                        hi = h % 2
                        h3 = h % 3
                        pv = pTr[:].bitcast(BF16)
                        for kb in range(5):
                            S.op("pe", lambda e, pv=pv, kb=kb, h3=h3: e.transpose(pv[:, kb * 128:(kb + 1) * 128], pb16[h3][:, kb * 128:(kb + 1) * 128], ident_b),
                                 reads=[Bpb[h3], Bcb], writes=[bTr])
                        S.op("act", lambda e, pv=pv, hi=hi: e.copy(pT[hi][:], pv[:, 0:640]), reads=[bTr], writes=[BpT[hi]])
                        if h + 2 < 16:
                            FA(h + 2)
                        quad = h // 4
                        oq = quad % 2
                        for kb in range(5):
                            S.op("pe", lambda e, kb=kb, hi=hi, h=h, oq=oq: e.matmul(pO[oq][:, (h % 4) * 128:(h % 4 + 1) * 128], pT[hi][:, kb * 128:(kb + 1) * 128],
                                                                                   Vt[i2][:, kb, h * 128:(h + 1) * 128], start=(kb == 0), stop=(kb == 4)),
                                 reads=[BpT[hi], BVt[i2]], writes=[bO[oq]])
                        if h % 4 == 3:
                            h0 = quad * 4
                            S.op("dve", lambda e, h0=h0: e.reciprocal(sm[:, 2, h0:h0 + 4], sm[:, 1, h0:h0 + 4]),
                                 reads=Bsm[h0:h0 + 4], writes=[Brinv[quad]])
                            S.op("dve", lambda e, h0=h0, oq=oq: e.tensor_tensor(on[oq][:], pO[oq][:].rearrange("p (a d) -> p a d", a=4),
                                                                                sm[:, 2, h0:h0 + 4].unsqueeze(2).to_broadcast([128, 4, 128]), ALU.mult),
                                 reads=[bO[oq], Brinv[quad]], writes=[Bon[oq]])
                            pv2 = pOT[:].bitcast(BF16)
                            for a in range(4):
                                S.op("pe", lambda e, pv2=pv2, a=a, oq=oq: e.transpose(pv2[:, a * 128:(a + 1) * 128], on[oq][:, a, :], ident_b),
                                     reads=[Bon[oq], Bcb], writes=[bOT])
                            S.op("dve", lambda e, pv2=pv2, h0=h0: e.tensor_tensor(yt[i2][:, h0:h0 + 4, :], pv2[:, 0:512].rearrange("p (a t) -> p a t", a=4),
                                                                                 zt[i2][:, h0:h0 + 4, :], ALU.mult),
                                 reads=[bOT, Bzt[i2]], writes=[Byt[i2]])

                    FA(0)
                    FA(1)
                    for h in range(16):
                        FB(h)
                    S.dma("sp", YT1[:, :, t0:t0 + 128].rearrange("h p t -> p h t"), yt[i2][:], reads=[Byt[i2]])
            S.barrier()

        carry = sb(ctx, "carry", [128, 32, 3], F32)
        Bcarry = S.bufs(32, "carry")

        phase0()
        if stop_after >= 1:
            phaseAB(0)
        if stop_after >= 3:
            phaseAB(1)
        if stop_after >= 4:
            phaseC()
        if stop_after >= 5:
            phase_out(0)
        if stop_after >= 6:
            phase_proj1(HKVT, NH1, kv_w, KT, None, V1)
            phase_proj1(H1T, 2048, b_w_in, QT, Z1, None)
        if stop_after >= 7:
            phaseF()
        if stop_after >= 8:
            phase_out(1)

        S.wait_all("sp", [t for t in S.dma_tok if t is not None])
        S.finalize()
    return nc


def make_consts():
    c = np.zeros((128, 4, 128), np.float32)
    c[:, 0, :] = np.eye(128, dtype=np.float32)
    c[:, 1, :] = 1.0
    c[:, 2, :] = np.triu(np.ones((128, 128), np.float32))
    c[127, 3, :] = 1.0
    return c


def make_amask():
    r = np.arange(128)[:, None]
    j = np.arange(640)[None, :]
    key = j - 512
    ch = r // 64
    lo = ch * 64 - 512
    hi = ch * 64 + 64
    ok = (key >= lo) & (key < hi)
    return np.where(ok, 0.0, -BIG).astype(np.float32)


def make_core_inputs(inputs, b, p):
    x = np.asarray(inputs["x"])
    if p == 1:
        xw = x[b]
    else:
        xw = np.concatenate([x[b, 2048:], x[b, :2048]], axis=0)
    f = 1.0 if p == 1 else 0.0
    flag = np.zeros((128, 2), np.float32)
    flag[:, 0] = f
    flag[:, 1] = (f - 1.0) * BIG
    c = np.asarray(inputs["c"])[b]
    cT = np.ascontiguousarray(c.reshape(16, 128).T)
    conv_w = np.asarray(inputs["a_conv_w"])[0]
    convw_l = np.ascontiguousarray(conv_w.reshape(4, 32, 128).transpose(2, 1, 0))
    convb_l = np.ascontiguousarray(np.asarray(inputs["a_conv_b"])[0].reshape(32, 128).T)
    gateb_l = np.ascontiguousarray(np.asarray(inputs["a_gate_b"])[0].reshape(2, 8).T)
    ghead_l = np.ascontiguousarray(np.asarray(inputs["a_g_head"])[0].reshape(32, 128).T)
    gvec_l = np.ascontiguousarray(np.stack([np.asarray(inputs["g_pre"])[1].reshape(16, 128).T,
                                            np.asarray(inputs["kv_g"]).reshape(16, 128).T,
                                            np.asarray(inputs["g_pre"])[0].reshape(16, 128).T], axis=1))
    rel = np.asarray(inputs["b_rel"])[0]
    r = np.arange(128)[:, None]
    j = np.arange(640)[None, :]
    bucket = np.clip(r + 512 - j, -128, 128) + 128
    bias_l = np.ascontiguousarray(rel[:, bucket])
    return {
        "xw": np.ascontiguousarray(xw), "cT": cT, "flag": flag, "consts": make_consts(),
        "ada_w": np.asarray(inputs["ada_w"]), "ada_b": np.asarray(inputs["ada_b"]),
        "g_pre": np.asarray(inputs["g_pre"]), "g_post": np.asarray(inputs["g_post"]),
        "a_w_in": np.asarray(inputs["a_w_in"])[0], "convw_l": convw_l, "convb_l": convb_l,
        "gateb_l": gateb_l, "ghead_l": ghead_l, "gvec_l": gvec_l, "a_w_out": np.asarray(inputs["a_w_out"])[0],
        "kv_ada_w": np.asarray(inputs["kv_ada_w"]), "kv_ada_b": np.asarray(inputs["kv_ada_b"]).reshape(1, -1),
        "kv_g": np.asarray(inputs["kv_g"]).reshape(1, -1), "kv_w": np.asarray(inputs["kv_w"]),
        "b_w_in": np.asarray(inputs["b_w_in"])[0], "bias_l": bias_l, "amask": make_amask(),
        "b_w_out": np.asarray(inputs["b_w_out"])[0],
    }


def kernel(**inputs):
    nc = build_program()
    in_maps = []
    for core in range(8):
        b, p = core // 2, core % 2
        in_maps.append(make_core_inputs(inputs, b, p))
    res = run_bass_kernel_spmd(nc, in_maps, core_ids=list(range(8)))
    out = np.zeros((NB, SEQ, D), np.float32)
    for core in range(8):
        b, p = core // 2, core % 2
        out[b, p * 2048:(p + 1) * 2048] = res.results[core]["y_out"]
    return out
```

```python
import numpy as np
import concourse.bass as bass
import concourse.mybir as mybir
from concourse.bass_utils import run_bass_kernel_spmd
from contextlib import ExitStack

F32 = mybir.dt.float32
BF16 = mybir.dt.bfloat16
AF = mybir.ActivationFunctionType
ALU = mybir.AluOpType
AX = mybir.AxisListType

ENGS = ("pe", "act", "dve", "pool", "sp")
NDSEM = 40

D = 2048
SEQ = 4096
NB = 4
EPS = 1e-6
A_INNER = 4096
A_COLS = 16400
A_V0 = 4096
A_O0 = 8192
A_Z0 = 12288
A_G0 = 16384
WIN = 4096
HALO0 = 1536
OWN0 = 2048
NH1 = WIN - HALO0
BIG = 30000.0
LN16 = float(np.log(16.0))


class Buf:
    __slots__ = ("name", "w", "r")

    def __init__(self, name=""):
        self.name = name
        self.w = None
        self.r = {}


class Sched:
    def __init__(self, nc, ctx):
        self.nc = nc
        self.ctx = ctx
        self.ops = {e: [] for e in ENGS}
        self.known = {e: {} for e in ENGS}
        self.awaited = {e: set() for e in ENGS}
        self.pend = {e: [] for e in ENGS}
        self.last = {e: None for e in ENGS}
        self.ndma = 0
        self.dma_tok = [None] * NDSEM
        self.nbuf = 0

    def buf(self, name=""):
        self.nbuf += 1
        return Buf(name or f"b{self.nbuf}")

    def bufs(self, n, name=""):
        return [self.buf(f"{name}{i}") for i in range(n)]

    def _collect(self, eng, reads, writes, extra=()):
        deps = {}

        def add(tok):
            if tok is None:
                return
            sk, v = tok
            if sk == eng and eng == "pe":
                return
            if self.known[eng].get(sk, 0) >= v:
                return
            if deps.get(sk, 0) < v:
                deps[sk] = v

        for b in reads:
            add(b.w)
        for b in writes:
            add(b.w)
            for t in b.r.values():
                add(t)
        for t in extra:
            add(t)
        waits = []
        for sk, v in deps.items():
            self.known[eng][sk] = v
            waits.append((sk, v))
            if not isinstance(sk, tuple):
                self.awaited[sk].add(v)
        return waits

    def op(self, eng, fn, reads=(), writes=()):
        waits = self.pend[eng] + self._collect(eng, reads, writes)
        self.pend[eng] = []
        idx = len(self.ops[eng]) + 1
        self.ops[eng].append((waits, fn, None))
        tok = (eng, idx)
        self.last[eng] = tok
        for b in reads:
            b.r[eng] = tok
        for b in writes:
            b.w = tok
            b.r = {}
        return tok

    def dma(self, q, out, in_, reads=(), writes=(), slow=False):
        k = self.ndma % NDSEM
        n = self.ndma // NDSEM + 1
        self.ndma += 1
        waits = self.pend[q] + self._collect(q, reads, writes, extra=(self.dma_tok[k],))
        self.pend[q] = []
        tok = (("d", k), n)
        self.dma_tok[k] = tok
        self.ops[q].append((waits, (out, in_, slow), tok))
        for b in reads:
            b.r[("d", k)] = tok
        for b in writes:
            b.w = tok
            b.r = {}
        return tok

    def wait_all(self, eng, toks):
        self.pend[eng] = self.pend[eng] + self._collect(eng, (), (), extra=toks)

    def barrier(self):
        toks = [t for t in self.last.values() if t is not None] + [t for t in self.dma_tok if t is not None]
        for e in ENGS:
            self.wait_all(e, toks)

    def finalize(self):
        nc = self.nc
        ctx = self.ctx
        esem = {e: ctx.enter_context(nc.semaphore(f"s_{e}")) for e in ENGS}
        dsem = [ctx.enter_context(nc.semaphore(f"d_{k}")) for k in range(NDSEM)]
        val = {}
        for e in ENGS:
            aw = sorted(self.awaited[e])
            val[e] = {idx: i + 1 for i, idx in enumerate(aw)}
        self.maxval = {e: len(val[e]) for e in ENGS}

        def emit_waits(engobj, waits):
            for sk, v in waits:
                if isinstance(sk, tuple):
                    engobj.wait_ge(dsem[sk[1]], 16 * v)
                else:
                    engobj.wait_ge(esem[sk], val[sk][v])

        def emit(e, engobj):
            for i, (waits, fn, dtok) in enumerate(self.ops[e]):
                emit_waits(engobj, waits)
                if dtok is not None:
                    out, in_, slow = fn
                    if slow:
                        engobj.dma_start(out=out, in_=in_, allow_slow_non_contiguous=True).then_inc(dsem[dtok[0][1]], 16)
                    else:
                        engobj.dma_start(out=out, in_=in_).then_inc(dsem[dtok[0][1]], 16)
                else:
                    ins = fn(engobj)
                    if (i + 1) in val[e]:
                        ins.then_inc(esem[e], 1)
            emit_waits(engobj, self.pend[e])

        with nc.Block() as block:
            @block.tensor
            def _(eng):
                emit("pe", eng)

            @block.scalar
            def _(eng):
                emit("act", eng)

            @block.vector
            def _(eng):
                emit("dve", eng)

            @block.gpsimd
            def _(eng):
                emit("pool", eng)

            @block.sync
            def _(eng):
                emit("sp", eng)


def build_program(debug=False, stop_after=99):
    nc = bass.Bass("TRN2", target_bir_lowering=False)

    def din(name, shape, dt=F32):
        return nc.dram_tensor(name, list(shape), dt, kind="ExternalInput").ap()

    def dscr(name, shape, dt):
        kind = "ExternalOutput" if debug else "Internal"
        return nc.dram_tensor(name, list(shape), dt, kind=kind).ap()

    xw = din("xw", [WIN, D])
    cT = din("cT", [128, 16])
    flag = din("flag", [128, 2])
    consts = din("consts", [128, 4, 128])
    ada_w = din("ada_w", [2, D, 3 * D])
    ada_b = din("ada_b", [2, 3 * D])
    g_pre = din("g_pre", [2, D])
    g_post = din("g_post", [2, D])
    a_w_in = din("a_w_in", [D, A_COLS])
    convw_l = din("convw_l", [128, 32, 4])
    convb_l = din("convb_l", [128, 32])
    gateb_l = din("gateb_l", [8, 2])
    ghead_l = din("ghead_l", [128, 32])
    gvec_l = din("gvec_l", [128, 3, 16])
    a_w_out = din("a_w_out", [A_INNER, D])
    kv_ada_w = din("kv_ada_w", [D, 2 * D])
    kv_ada_b = din("kv_ada_b", [1, 2 * D])
    kv_g = din("kv_g", [1, D])
    kv_w = din("kv_w", [D, 2 * D])
    b_w_in = din("b_w_in", [D, 2 * D])
    bias_l = din("bias_l", [16, 128, 640])
    amask = din("amask", [128, 640])
    b_w_out = din("b_w_out", [D, D])
    y_out = nc.dram_tensor("y_out", [2048, D], F32, kind="ExternalOutput").ap()

    VECS = dscr("VECS", [8, D], F32)
    QKs = dscr("QKs", [32, 128, WIN], BF16)
    Vs = dscr("Vs", [WIN, A_INNER], BF16)
    GZs = dscr("GZs", [32, 128, NH1], BF16)
    GATES = dscr("GATES", [16, WIN], F32)
    YT = dscr("YT", [32, 128, NH1], BF16)
    X1 = dscr("X1", [NH1, D], F32)
    HKVT = dscr("HKVT", [16, 128, NH1], BF16)
    H1T = dscr("H1T", [16, 128, 2048], BF16)
    KT = dscr("KT", [16, 128, NH1], BF16)
    V1 = dscr("V1", [NH1, 2048], BF16)
    QT = dscr("QT", [16, 128, 2048], BF16)
    Z1 = dscr("Z1", [16, 128, 2048], BF16)
    YT1 = dscr("YT1", [16, 128, 2048], BF16)

    with ExitStack() as ctx:
        S = Sched(nc, ctx)

        uniq = [0]

        def sb(c, name, shape, dt):
            uniq[0] += 1
            return c.enter_context(nc.sbuf_tensor(f"{name}_{uniq[0]}", list(shape), dt))

        PS = [ctx.enter_context(nc.psum_tensor(f"ps{i}", [128, 512], F32)) for i in range(8)]
        PB = S.bufs(8, "ps")
        psrr = [0]

        def next_ps():
            i = psrr[0] % 8
            psrr[0] += 1
            return PS[i], PB[i]

        cst = sb(ctx, "cst", [128, 4, 128], F32)
        cstb = sb(ctx, "cstb", [128, 4, 128], BF16)
        flg = sb(ctx, "flg", [128, 2], F32)
        scT = sb(ctx, "scT", [128, 16], F32)
        Bc = S.buf("cst")
        Bcb = S.buf("cstb")
        Bflg = S.buf("flg")
        BscT = S.buf("scT")
        S.dma("sp", cst[:], consts, writes=[Bc])
        S.dma("sp", flg[:], flag, writes=[Bflg])
        S.dma("sp", scT[:], cT, writes=[BscT])
        S.op("dve", lambda e: e.tensor_copy(cstb[:], cst[:]), reads=[Bc], writes=[Bcb])
        S.op("act", lambda e: e.activation(scT[:], scT[:], AF.Silu), reads=[BscT], writes=[BscT])
        ident_b = cstb[:, 0, :]
        ones_f = cst[:, 1, :]
        ident_f = cst[:, 0, :]

        def gemv_items(pc, glist, nwt=4):
            wt = [sb(pc, f"p0w{i}", [128, 1024], F32) for i in range(nwt)]
            Bw = S.bufs(nwt, "p0w")
            accs = [sb(pc, f"p0acc{i}", [128, 2048], F32) for i in range(2)]
            Baccs = S.bufs(2, "p0acc")
            bia = sb(pc, "p0b", [128, 2048], F32)
            Bbia = S.buf("p0b")
            vec = bia
            groups = []
            for l in range(2):
                for j in range(3):
                    groups.append((ada_w[l, :, j * D:(j + 1) * D], ada_b[l:l + 1, j * D:(j + 1) * D]))
            for j in range(2):
                groups.append((kv_ada_w[:, j * D:(j + 1) * D], kv_ada_b[0:1, j * D:(j + 1) * D]))
            chunks = []
            for gi, g in enumerate(glist):
                for kc in range(16):
                    for hc in range(2):
                        chunks.append((g, kc, hc, gi % 2))
            st = {"iss": 0}

            def issue():
                i = st["iss"]
                if i >= len(chunks):
                    return
                st["iss"] += 1
                g, kc, hc, ai = chunks[i]
                wap, bap = groups[g]
                S.dma("sp", wt[i % nwt][:], wap[kc * 128:(kc + 1) * 128, hc * 1024:(hc + 1) * 1024], writes=[Bw[i % nwt]])

            def mk_chunk(i):
                g, kc, hc, ai = chunks[i]
                acc = accs[ai]
                Bacc = Baccs[ai]

                def f():
                    while st["iss"] <= i + nwt - 1 and st["iss"] < len(chunks):
                        issue()
                    w = wt[i % nwt]
                    bw = Bw[i % nwt]
                    asl = acc[:, hc * 1024:(hc + 1) * 1024]
                    if kc == 0:
                        S.op("dve", lambda e: e.tensor_scalar(asl, w[:], scT[:, kc:kc + 1], None, ALU.mult),
                             reads=[bw, BscT], writes=[Bacc])
                    else:
                        S.op("dve", lambda e: e.scalar_tensor_tensor(asl, w[:], scT[:, kc:kc + 1], asl, ALU.mult, ALU.add),
                             reads=[bw, BscT, Bacc], writes=[Bacc])
                return f

            def mk_fin(g, ai):
                acc = accs[ai]
                Bacc = Baccs[ai]

                def f():
                    wap, bap = groups[g]
                    S.dma("sp", bia[:], bap.partition_broadcast(128), writes=[Bbia])
                    pss = []
                    for q in range(4):
                        p, pb = next_ps()
                        S.op("pe", lambda e, p=p, q=q: e.matmul(p[:], ones_f, acc[:, q * 512:(q + 1) * 512], start=True, stop=True),
                             reads=[Bacc, Bc], writes=[pb])
                        pss.append((p, pb))
                    for q, (p, pb) in enumerate(pss):
                        S.op("dve", lambda e, p=p, q=q: e.tensor_tensor(vec[:, q * 512:(q + 1) * 512], p[:], bia[:, q * 512:(q + 1) * 512], ALU.add),
                             reads=[pb, Bbia], writes=[Bbia])
                    S.dma("sp", VECS[g:g + 1, :], vec[0:1, :], reads=[Bbia])
                return f

            items = []
            pending = None
            i = 0
            for gi, g in enumerate(glist):
                for kc in range(16):
                    for hc in range(2):
                        items.append(mk_chunk(i))
                        i += 1
                    if kc == 12 and pending is not None:
                        items.append(pending)
                        pending = None
                pending = mk_fin(g, gi % 2)
            items.append(pending)
            return items

        def phase0():
            with ExitStack() as pc:
                for it in gemv_items(pc, [0, 1], nwt=4):
                    it()
            S.barrier()

        gemv_rest = {"items": None}

        def pump(n=1):
            its = gemv_rest["items"]
            for _ in range(n):
                if its:
                    its.pop(0)()

        def phaseAB(half):
            T0 = half * 2048
            with ExitStack() as pc:
                hT = sb(pc, "hT", [128, 16, 2048], BF16)
                BhT = S.bufs(16, "hT")
                with ExitStack() as pa:
                    xt = [sb(pa, f"xt{i}", [128, 2048], F32) for i in range(2)]
                    Bxt = S.bufs(2, "xt")
                    xnb = [sb(pa, f"xnbA{i}", [128, 2048], BF16) for i in range(2)]
                    Bxnb = S.bufs(2, "xnbA")
                    ss = sb(pa, "ssA", [128, 16], F32)
                    Bss = S.bufs(16, "ssA")
                    fvA = sb(pa, "fvA", [128, 4, 16], F32)
                    BfvA = S.buf("fvA")
                    S.dma("sp", fvA[:, 3, :], gvec_l[:, 2, :], writes=[BfvA])
                    S.dma("sp", fvA[:, 2, :], VECS[1, :].rearrange("(k p) -> p k", p=128), writes=[BfvA], slow=True)
                    S.dma("sp", fvA[:, 1, :], VECS[0, :].rearrange("(k p) -> p k", p=128), writes=[BfvA], slow=True)
                    S.op("dve", lambda e: e.scalar_tensor_tensor(fvA[:, 0, :], fvA[:, 2, :], 1.0, fvA[:, 3, :], ALU.add, ALU.mult), reads=[BfvA], writes=[BfvA])
                    pendA = []
                    for tb in range(16):
                        x_ = xt[tb % 2]
                        bx = Bxt[tb % 2]
                        xn_ = xnb[tb % 2]
                        bxn = Bxnb[tb % 2]
                        S.dma("sp", x_[:], xw[T0 + tb * 128:T0 + (tb + 1) * 128, :], writes=[bx])
                        sc_ = ss[:, tb:tb + 1]
                        S.op("act", lambda e, x_=x_, sc_=sc_, xn_=xn_: e.activation(xn_[:], x_[:], AF.Square, accum_out=sc_),
                             reads=[bx], writes=[bxn, Bss[tb]])
                        while pendA:
                            pendA.pop(0)()
                        S.op("dve", lambda e, sc_=sc_: e.tensor_scalar(sc_, sc_, 1.0 / D, EPS, ALU.mult, ALU.add),
                             reads=[Bss[tb]], writes=[Bss[tb]])
                        S.op("act", lambda e, sc_=sc_: e.sqrt(sc_, sc_), reads=[Bss[tb]], writes=[Bss[tb]])
                        S.op("dve", lambda e, sc_=sc_: e.reciprocal(sc_, sc_), reads=[Bss[tb]], writes=[Bss[tb]])
                        S.op("dve", lambda e, x_=x_, sc_=sc_, xn_=xn_: e.tensor_scalar(xn_[:], x_[:], sc_, None, ALU.mult),
                             reads=[bx, Bss[tb]], writes=[bxn])
                        for hh in range(2):
                            p, pb = next_ps()
                            pv = p[:].bitcast(BF16)
                            for j in range(8):
                                kc = hh * 8 + j
                                S.op("pe", lambda e, pv=pv, j=j, kc=kc, xn_=xn_: e.transpose(pv[:, j * 128:(j + 1) * 128], xn_[:, kc * 128:(kc + 1) * 128], ident_b),
                                     reads=[bxn, Bcb], writes=[pb])
                            def evA(pv=pv, pb=pb, hh=hh, tb=tb):
                                for j in range(8):
                                    kc = hh * 8 + j
                                    S.op("act", lambda e, pv=pv, j=j, kc=kc, tb=tb: e.activation(hT[:, kc, tb * 128:(tb + 1) * 128], pv[:, j * 128:(j + 1) * 128], AF.Identity,
                                                                                                bias=fvA[:, 1, kc:kc + 1], scale=fvA[:, 0, kc:kc + 1]),
                                         reads=[pb, BfvA], writes=[BhT[tb]])
                            pendA.append(evA)
                    while pendA:
                        pendA.pop(0)()
                S.barrier()
                if stop_after == 1:
                    return
                with ExitStack() as pb_:
                    wst = [sb(pb_, f"wst{i}", [128, 16, 256], F32) for i in range(3)]
                    Bwst = S.bufs(3, "wst")
                    wbf = [sb(pb_, f"wbf{i}", [128, 16, 256], BF16) for i in range(3)]
                    Bwbf = S.bufs(3, "wbf")
                    Bwbf2 = S.bufs(3, "wbf2")
                    cw = sb(pb_, "cw", [128, 32, 4], F32)
                    cb = sb(pb_, "cb", [128, 32], F32)
                    gh = sb(pb_, "gh", [128, 32], F32)
                    Bcw = S.buf("cw")
                    S.dma("sp", cw[:], convw_l, writes=[Bcw])
                    S.dma("sp", cb[:], convb_l, writes=[Bcw])
                    S.dma("sp", gh[:], ghead_l, writes=[Bcw])
                    cwf = sb(pb_, "cwf", [128, 32], F32)
                    if half == 0:
                        S.op("dve", lambda e: e.tensor_scalar(cwf[:], cw[:, :, 3], flg[:, 0:1], None, ALU.mult), reads=[Bcw, Bflg], writes=[Bcw])
                    else:
                        S.op("dve", lambda e: e.tensor_copy(cwf[:], cw[:, :, 3]), reads=[Bcw], writes=[Bcw])
                    ub = [sb(pb_, f"ub{i}", [128, 515], F32) for i in range(2)]
                    Bub = S.bufs(2, "ub")
                    cacc = [sb(pb_, f"cacc{i}", [128, 512], F32) for i in range(2)]
                    Bcacc = S.bufs(2, "cacc")
                    ob = [sb(pb_, f"ob{i}", [128, 512], BF16) for i in range(3)]
                    Bob = S.bufs(3, "ob")
                    so = [sb(pb_, f"so{i}", [128, 512], F32) for i in range(2)]
                    Bso = S.bufs(2, "so")
                    sz = [sb(pb_, f"sz{i}", [128, 512], F32) for i in range(2)]
                    Bsz = S.bufs(2, "sz")
                    gt = [sb(pb_, f"gt{i}", [16, 512], F32) for i in range(2)]
                    Bgt = S.bufs(2, "gt")
                    cnt = {"w": 0, "u": 0, "ob": 0, "so": 0, "gt": 0, "iss": 0}
                    if half == 1:
                        gemv_rest["items"] = gemv_items(pb_, [2, 3, 4, 5, 6, 7], nwt=4)
                    w_view = a_w_in.rearrange("(kc p) c -> p kc c", p=128)
                    wseq = [(g * 256, 256) for g in range(16)] + [(A_V0 + g * 256, 256) for g in range(16)]
                    for g in range(16):
                        wseq += [(A_O0 + g * 256, 256), (A_Z0 + g * 256, 256)]
                    wseq.append((A_G0, 16))

                    cnt["cv"] = 0

                    def issue_load():
                        k = cnt["iss"]
                        cnt["iss"] += 1
                        c0, ncols = wseq[k]
                        i = k % 3
                        S.dma("sp", wst[i][:, :, 0:ncols], w_view[:, :, c0:c0 + ncols], writes=[Bwst[i]])

                    def issue_cv():
                        k = cnt["cv"]
                        cnt["cv"] += 1
                        c0, ncols = wseq[k]
                        i = k % 3
                        S.op("pool", lambda e, i=i, ncols=ncols: e.tensor_copy(wbf[i][:, 0:8, 0:ncols], wst[i][:, 0:8, 0:ncols]),
                             reads=[Bwst[i]], writes=[Bwbf[i]])
                        S.op("act", lambda e, i=i, ncols=ncols: e.copy(wbf[i][:, 8:16, 0:ncols], wst[i][:, 8:16, 0:ncols]),
                             reads=[Bwst[i]], writes=[Bwbf2[i]])

                    def load_w(c0, ncols):
                        k = cnt["w"]
                        assert wseq[k] == (c0, ncols)
                        cnt["w"] += 1
                        while cnt["iss"] <= k + 2 and cnt["iss"] < len(wseq):
                            issue_load()
                        while cnt["cv"] <= k + 1 and cnt["cv"] < len(wseq):
                            issue_cv()
                        i = k % 3
                        return wbf[i], [Bwbf[i], Bwbf2[i]]

                    def mm_feat(w, bw, cblk, tt, M=128):
                        p, pb = next_ps()
                        for kc in range(16):
                            S.op("pe", lambda e, p=p, kc=kc: e.matmul(p[0:M, :], w[:, kc, cblk * 128:cblk * 128 + M],
                                                                      hT[:, kc, tt * 512:(tt + 1) * 512],
                                                                      start=(kc == 0), stop=(kc == 15)),
                                 reads=bw + BhT[tt * 4:(tt + 1) * 4], writes=[pb])
                        return p, pb

                    tts_all = [0, 1, 2, 3]
                    tts_full = [3] if half == 0 else [0, 1, 2, 3]
                    tts_q = [2, 3] if half == 0 else [0, 1, 2, 3]

                    pend_silu = []
                    for grp in range(16):
                        isq = grp < 8
                        tts = tts_q if isq else tts_all
                        w, bw = load_w(grp * 256, 256)
                        for cblk in range(2):
                            blk = grp * 2 + cblk
                            for tt in tts:
                                p, pb = mm_feat(w, bw, cblk, tt)
                                i = cnt["u"] % 2
                                cnt["u"] += 1
                                u = ub[i]
                                bu = Bub[i]
                                ca = cacc[i]
                                bca = Bcacc[i]
                                first = (half == 0 and tt == tts[0])
                                if first:
                                    S.op("dve", lambda e, u=u: e.memset(u[:, 0:3], 0.0), writes=[bu])
                                else:
                                    S.op("act", lambda e, u=u, blk=blk: e.copy(u[:, 0:3], carry[:, blk, :]),
                                         reads=[Bcarry[blk]], writes=[bu])
                                if half == 0:
                                    S.op("act", lambda e, u=u, p=p: e.activation(u[:, 3:515], p[:], AF.Copy, scale=flg[:, 0:1]),
                                         reads=[pb, Bflg, bu], writes=[bu])
                                else:
                                    S.op("act", lambda e, u=u, p=p: e.copy(u[:, 3:515], p[:]), reads=[pb, bu], writes=[bu])
                                S.op("act", lambda e, u=u, blk=blk: e.copy(carry[:, blk, :], u[:, 512:515]),
                                     reads=[bu], writes=[Bcarry[blk]])
                                S.op("act", lambda e, p=p, ca=ca, blk=blk: e.activation(ca[:], p[:], AF.Identity, bias=cb[:, blk:blk + 1], scale=cwf[:, blk:blk + 1]),
                                     reads=[pb, Bcw], writes=[bca])
                                for j in (2, 1, 0):
                                    S.op("dve", lambda e, u=u, ca=ca, blk=blk, j=j: e.scalar_tensor_tensor(ca[:], u[:, j:j + 512], cw[:, blk, j:j + 1], ca[:], ALU.mult, ALU.add),
                                         reads=[bu, Bcw, bca], writes=[bca])
                                def fin_silu(ca=ca, bca=bca, blk=blk, tt=tt):
                                    oi = cnt["ob"] % 3
                                    cnt["ob"] += 1
                                    S.op("act", lambda e, ca=ca, oi=oi: e.activation(ob[oi][:], ca[:], AF.Silu),
                                         reads=[bca], writes=[Bob[oi]])
                                    S.dma("sp", QKs[blk, :, T0 + tt * 512:T0 + (tt + 1) * 512], ob[oi][:], reads=[Bob[oi]])
                                while pend_silu:
                                    pend_silu.pop(0)()
                                pend_silu.append(fin_silu)
                    while pend_silu:
                        pend_silu.pop(0)()
                    for grp in range(16):
                        w, bw = load_w(A_V0 + grp * 256, 256)
                        for tb2 in range(8):
                            p, pb = next_ps()
                            for s2 in range(2):
                                tb = tb2 * 2 + s2
                                for kc in range(16):
                                    S.op("pe", lambda e, p=p, kc=kc, tb=tb, s2=s2, w=w: e.matmul(p[:, s2 * 256:(s2 + 1) * 256], hT[:, kc, tb * 128:(tb + 1) * 128],
                                                                                               w[:, kc, :], start=(kc == 0), stop=(kc == 15)),
                                         reads=bw + [BhT[tb]], writes=[pb])
                            oi = cnt["ob"] % 3
                            cnt["ob"] += 1
                            S.op("act", lambda e, p=p, oi=oi: e.copy(ob[oi][:], p[:]), reads=[pb], writes=[Bob[oi]])
                            S.dma("sp", Vs[T0 + tb2 * 256:T0 + (tb2 + 1) * 256, A_V0 - A_V0 + grp * 256:grp * 256 + 256].rearrange("(s p) c -> p s c", p=128),
                                  ob[oi][:].rearrange("p (s c) -> p s c", s=2), reads=[Bob[oi]])
                    for grp in range(16):
                        wo, bwo = load_w(A_O0 + grp * 256, 256)
                        wz, bwz = load_w(A_Z0 + grp * 256, 256)
                        for cblk in range(2):
                            eblk = grp * 2 + cblk
                            for tt in tts_full:
                                if half == 1:
                                    pump(2)
                                po, pbo = mm_feat(wo, bwo, cblk, tt)
                                pz, pbz = mm_feat(wz, bwz, cblk, tt)
                                i = cnt["so"] % 2
                                cnt["so"] += 1
                                S.op("act", lambda e, i=i, po=po: e.activation(so[i][:], po[:], AF.Sigmoid), reads=[pbo], writes=[Bso[i]])
                                S.op("act", lambda e, i=i, pz=pz: e.activation(sz[i][:], pz[:], AF.Silu), reads=[pbz], writes=[Bsz[i]])
                                oi = cnt["ob"] % 3
                                cnt["ob"] += 1
                                S.op("dve", lambda e, i=i, oi=oi, eblk=eblk: e.scalar_tensor_tensor(ob[oi][:], sz[i][:], gh[:, eblk:eblk + 1], so[i][:], ALU.mult, ALU.mult),
                                     reads=[Bso[i], Bsz[i], Bcw], writes=[Bob[oi]])
                                t0 = T0 + tt * 512 - HALO0
                                S.dma("sp", GZs[eblk, :, t0:t0 + 512], ob[oi][:], reads=[Bob[oi]])
                    if half == 1:
                        pump(1000)
                    w, bw = load_w(A_G0, 16)
                    for tt in tts_all:
                        p, pb = mm_feat(w, bw, 0, tt, M=16)
                        i = cnt["gt"] % 2
                        cnt["gt"] += 1
                        S.op("act", lambda e, i=i, p=p: e.copy(gt[i][:], p[0:16, :]), reads=[pb], writes=[Bgt[i]])
                        S.dma("sp", GATES[:, T0 + tt * 512:T0 + (tt + 1) * 512], gt[i][:], reads=[Bgt[i]])
            S.barrier()


        def phaseC():
            NBLK = WIN // 128
            FB0 = HALO0 // 128
            with ExitStack() as pc:
                TT = sb(pc, "TT", [128, 3, NBLK, 8], F32)
                BTT = S.buf("TT")
                Rend = sb(pc, "Rend", [128, NBLK, 8], F32)
                Rprev = sb(pc, "Rprev", [128, NBLK, 8], F32)
                fa = sb(pc, "fa", [128, NBLK, 8], F32)
                fbeta = sb(pc, "fbeta", [128, NBLK, 8], F32)
                fb2 = sb(pc, "fb2", [128, NBLK, 8], F32)
                fdec = sb(pc, "fdec", [128, NBLK, 8], F32)
                fem = sb(pc, "fem", [128, NBLK, 8], F32)
                Bfac = S.buf("fac")
                with ExitStack() as pg:
                    gi = sb(pg, "gi", [8, WIN], F32)
                    gf = sb(pg, "gf", [8, WIN], F32)
                    one8 = sb(pg, "one8", [8, WIN], F32)
                    Bn = sb(pg, "Bn", [8, WIN], F32)
                    U = sb(pg, "U", [8, WIN], F32)
                    R = sb(pg, "R", [8, WIN], F32)
                    M = sb(pg, "M", [8, WIN], F32)
                    gb = sb(pg, "gb", [8, 4], F32)
                    Bg = S.buf("gates")
                    S.dma("sp", gi[:], GATES[0:8, :], writes=[Bg])
                    S.dma("sp", gf[:], GATES[8:16, :], writes=[Bg])
                    S.dma("sp", gb[:, 0:2], gateb_l, writes=[Bg])
                    S.op("pool", lambda e: e.memset(one8[:], 1.0), writes=[Bg])
                    S.op("dve", lambda e: e.tensor_scalar(gi[:], gi[:], gb[:, 0:1], None, ALU.add), reads=[Bg], writes=[Bg])
                    S.op("dve", lambda e: e.tensor_scalar(gi[:, 0:OWN0], gi[:, 0:OWN0], flg[0:8, 0:1], flg[0:8, 1:2], ALU.mult, ALU.add),
                         reads=[Bg, Bflg], writes=[Bg])
                    S.op("dve", lambda e: e.tensor_scalar(gb[:, 2:3], gb[:, 1:2], -1.0, None, ALU.mult), reads=[Bg], writes=[Bg])
                    S.op("act", lambda e: e.activation(gf[:], gf[:], AF.Exp, bias=gb[:, 2:3], scale=-1.0), reads=[Bg], writes=[Bg])
                    S.op("dve", lambda e: e.tensor_scalar(gf[:], gf[:], 1.0, None, ALU.add), reads=[Bg], writes=[Bg])
                    S.op("act", lambda e: e.activation(gf[:], gf[:], AF.Ln), reads=[Bg], writes=[Bg])
                    S.op("dve", lambda e: e.tensor_scalar(gf[:, 0:OWN0], gf[:, 0:OWN0], flg[0:8, 0:1], None, ALU.mult), reads=[Bg, Bflg], writes=[Bg])
                    S.op("dve", lambda e: e.tensor_tensor_scan(Bn[:], one8[:], gf[:], 0.0, ALU.mult, ALU.add), reads=[Bg], writes=[Bg])
                    S.op("dve", lambda e: e.tensor_tensor(U[:], gi[:], Bn[:], ALU.add), reads=[Bg], writes=[Bg])
                    S.op("dve", lambda e: e.tensor_tensor_scan(R[:], U[:], U[:], 0.0, ALU.max, ALU.max), reads=[Bg], writes=[Bg])
                    S.op("dve", lambda e: e.tensor_tensor(M[:], R[:], Bn[:], ALU.subtract), reads=[Bg], writes=[Bg])
                    for qi, src in enumerate((U, R, M)):
                        for half_ in range(2):
                            p, pb = next_ps()
                            for jj in range(16):
                                j = half_ * 16 + jj
                                S.op("pe", lambda e, p=p, jj=jj, j=j, src=src: e.transpose(p[:, jj * 8:(jj + 1) * 8], src[0:8, j * 128:(j + 1) * 128], ident_f[0:8, 0:8]),
                                     reads=[Bg, Bc], writes=[pb])
                            S.op("act", lambda e, p=p, qi=qi, half_=half_: e.copy(TT[:, qi, half_ * 16:(half_ + 1) * 16, :], p[:, 0:128].rearrange("p (j h) -> p j h", h=8)),
                                 reads=[pb], writes=[BTT])
                    p, pb = next_ps()
                    S.op("pe", lambda e, p=p: e.matmul(p[:, 0:256], cst[:, 3, :], TT[:, 1, :, :], start=True, stop=True),
                         reads=[BTT, Bc], writes=[pb])
                    S.op("act", lambda e, p=p: e.copy(Rend[:], p[:, 0:256].rearrange("p (j h) -> p j h", h=8)), reads=[pb], writes=[Bfac])
                    S.op("pool", lambda e: e.memset(Rprev[:, 0, :], 0.0), writes=[Bfac])
                    S.op("dve", lambda e: e.tensor_copy(Rprev[:, 1:NBLK, :], Rend[:, 0:NBLK - 1, :]), reads=[Bfac], writes=[Bfac])
                    S.op("dve", lambda e: e.tensor_tensor(fa[:], TT[:, 0, :, :], Rend[:], ALU.subtract), reads=[BTT, Bfac], writes=[Bfac])
                    S.op("act", lambda e: e.activation(fa[:], fa[:], AF.Exp), reads=[Bfac], writes=[Bfac])
                    S.op("dve", lambda e: e.tensor_tensor(fbeta[:], Rend[:], TT[:, 1, :, :], ALU.subtract), reads=[BTT, Bfac], writes=[Bfac])
                    S.op("dve", lambda e: e.tensor_scalar(fbeta[:], fbeta[:], -LN16, None, ALU.add), reads=[Bfac], writes=[Bfac])
                    S.op("act", lambda e: e.activation(fbeta[:], fbeta[:], AF.Exp), reads=[Bfac], writes=[Bfac])
                    S.op("dve", lambda e: e.scalar_tensor_tensor(fb2[:], fbeta[:], 1.0 / 512.0, fbeta[:], ALU.mult, ALU.mult), reads=[Bfac], writes=[Bfac])
                    S.op("dve", lambda e: e.tensor_tensor(fdec[:], Rprev[:], Rend[:], ALU.subtract), reads=[Bfac], writes=[Bfac])
                    S.op("act", lambda e: e.activation(fdec[:], fdec[:], AF.Exp), reads=[Bfac], writes=[Bfac])
                    S.op("dve", lambda e: e.tensor_tensor(fem[:], TT[:, 1, :, :], TT[:, 2, :, :], ALU.subtract), reads=[BTT], writes=[Bfac])
                    S.op("dve", lambda e: e.tensor_tensor(fem[:], fem[:], Rend[:], ALU.subtract), reads=[Bfac], writes=[Bfac])
                    S.op("dve", lambda e: e.tensor_scalar(fem[:], fem[:], 2.0, 2.0 * LN16 + float(np.log(EPS)), ALU.mult, ALU.add), reads=[Bfac], writes=[Bfac])
                    S.op("act", lambda e: e.activation(fem[:], fem[:], AF.Exp), reads=[Bfac], writes=[Bfac])
                S.barrier()
                kf = [sb(pc, f"kf{i}", [128, 16, 128], BF16) for i in range(2)]
                qf = [sb(pc, f"qf{i}", [128, 16, 128], BF16) for i in range(2)]
                vt = [sb(pc, f"vt{i}", [128, 4096], BF16) for i in range(2)]
                gz = [sb(pc, f"gz{i}", [128, 32, 128], BF16) for i in range(2)]
                yt = [sb(pc, f"yt{i}", [128, 32, 128], BF16) for i in range(2)]
                Bkf = S.bufs(2, "kf"); Bqf = S.bufs(2, "qf"); Bvt = S.bufs(2, "vt"); Bgz = S.bufs(2, "gz"); Byt = S.bufs(2, "yt")
                Cst = sb(pc, "Cst", [128, 8, 2, 512], F32)
                BC = S.bufs(8, "Cst")
                Nst = sb(pc, "Nst", [128, 8, 2], F32)
                BN = S.buf("Nst")
                Cb = [sb(pc, f"Cb{i}", [128, 2, 512], BF16) for i in range(2)]
                BCb = S.bufs(2, "Cb")
                Cbn = sb(pc, "Cbn", [128, 8, 2], BF16)
                BCbn = S.buf("Cbn")
                kT = [sb(pc, f"kT{i}", [128, 256], BF16) for i in range(3)]
                BkT = S.bufs(3, "kT")
                vp = [sb(pc, f"vp{i}", [128, 513], BF16) for i in range(3)]
                Bvp = S.bufs(3, "vp")
                sd = [sb(pc, f"sd{i}", [128, 128], BF16) for i in range(2)]
                Bsd = S.bufs(2, "sd")
                hn = [sb(pc, f"hn{i}", [128, 512], BF16) for i in range(2)]
                Bhn = S.bufs(2, "hn")
                junk = sb(pc, "junkC", [128, 512], BF16)
                Bjunk = S.buf("junkC")
                sm = sb(pc, "smC", [128, 8, 8], F32)
                Bsm = S.bufs(8, "smC")
                tri_f = cst[:, 2, :]
                S.op("pool", lambda e: e.memset(Cst[:], 0.0), writes=BC)
                S.op("pool", lambda e: e.memset(Nst[:], 0.0), writes=[BN])
                pKT, bKT = PS[0], PB[0]
                pS, bS = PS[1], PB[1]
                pACC = [PS[2], PS[3]]; bACC = [PB[2], PB[3]]
                pSM = PS[4]
                bDEN = [PB[4]] * 8; bPN = [PB[4]] * 8
                pHT, bHT = PS[5], PB[5]
                pC = [PS[6], PS[7]]; bC = [PB[6], PB[7]]
                def loadsC(jb):
                    i2 = jb % 2
                    t0 = jb * 128
                    S.dma("sp", kf[i2][:], QKs[16:32, :, t0:t0 + 128].rearrange("b p t -> p b t"), writes=[Bkf[i2]])
                    S.dma("sp", vt[i2][:], Vs[t0:t0 + 128, :], writes=[Bvt[i2]])
                    if jb >= FB0:
                        S.dma("sp", qf[i2][:], QKs[0:16, :, t0:t0 + 128].rearrange("b p t -> p b t"), writes=[Bqf[i2]])
                        S.dma("sp", gz[i2][:], GZs[:, :, t0 - HALO0:t0 - HALO0 + 128].rearrange("b p t -> p b t"), writes=[Bgz[i2]])

                loadsC(0)
                for jb in range(NBLK):
                    full = jb >= FB0
                    i2 = jb % 2
                    t0 = jb * 128
                    if jb + 1 < NBLK:
                        loadsC(jb + 1)
                    if full:
                        S.op("dve", lambda e, jb=jb: e.tensor_tensor(Cbn[:], Nst[:], fdec[:, jb, :].unsqueeze(2).to_broadcast([128, 8, 2]), ALU.mult),
                             reads=[BN, Bfac], writes=[BCbn])
                    def A1(h, jb=jb, i2=i2, full=full):
                        hi = (jb * 8 + h) % 2
                        hk = (jb * 8 + h) % 3
                        a_ap = fa[:, jb, h:h + 1]
                        dec_ap = fdec[:, jb, h:h + 1]
                        pv = pKT[:].bitcast(BF16)
                        for c in range(2):
                            S.op("pe", lambda e, pv=pv, c=c, h=h: e.transpose(pv[:, c * 128:(c + 1) * 128], kf[i2][:, h * 2 + c, :], ident_b),
                                 reads=[Bkf[i2], Bcb], writes=[bKT])
                        if full:
                            for c in range(2):
                                S.op("pe", lambda e, c=c, h=h: e.matmul(pS[:, 0:128], kf[i2][:, h * 2 + c, :], qf[i2][:, h * 2 + c, :], start=(c == 0), stop=(c == 1)),
                                     reads=[Bkf[i2], Bqf[i2]], writes=[bS])
                        S.op("act", lambda e, pv=pv, hk=hk: e.copy(kT[hk][:], pv[:, 0:256]), reads=[bKT], writes=[BkT[hk]])
                        S.op("dve", lambda e, hk=hk, h=h, a_ap=a_ap: e.tensor_scalar(vp[hk][:, 0:512], vt[i2][:, h * 512:(h + 1) * 512], a_ap, None, ALU.mult),
                             reads=[Bvt[i2], Bfac], writes=[Bvp[hk]])
                        S.op("dve", lambda e, hk=hk, a_ap=a_ap: e.tensor_copy(vp[hk][:, 512:513], a_ap), reads=[Bfac, Bvp[hk]], writes=[Bvp[hk]])
                        if full:
                            S.op("act", lambda e, hi=hi, h=h, dec_ap=dec_ap: e.activation(Cb[hi][:], Cst[:, h, :, :], AF.Copy, scale=dec_ap),
                                 reads=[BC[h], Bfac], writes=[BCb[hi]])
                            S.op("dve", lambda e, hi=hi: e.tensor_tensor(sd[hi][:], pS[:, 0:128], tri_f, ALU.mult), reads=[bS, Bc], writes=[Bsd[hi]])

                    def A2(h, jb=jb, i2=i2, full=full):
                        if not full:
                            return
                        hi = (jb * 8 + h) % 2
                        hk = (jb * 8 + h) % 3
                        pa = pACC[hi]; ba = bACC[hi]
                        for c in range(2):
                            S.op("pe", lambda e, pa=pa, c=c, h=h, hi=hi: e.matmul(pa[:], qf[i2][:, h * 2 + c, :], Cb[hi][:, c, :], start=(c == 0), stop=False),
                                 reads=[Bqf[i2], BCb[hi]], writes=[ba])
                        S.op("pe", lambda e, pa=pa, hi=hi, hk=hk: e.matmul(pa[:], sd[hi][:], vp[hk][:, 0:512], start=False, stop=True),
                             reads=[Bsd[hi], Bvp[hk]], writes=[ba])
                        for c in range(2):
                            S.op("pe", lambda e, c=c, h=h: e.matmul(pSM[:, h:h + 1], qf[i2][:, h * 2 + c, :], Cbn[:, h, c:c + 1], start=(c == 0), stop=False),
                                 reads=[Bqf[i2], BCbn], writes=[bDEN[h]])
                        S.op("pe", lambda e, hi=hi, h=h, hk=hk: e.matmul(pSM[:, h:h + 1], sd[hi][:], vp[hk][:, 512:513], start=False, stop=True),
                             reads=[Bsd[hi], Bvp[hk]], writes=[bDEN[h]])

                    def B1(h, jb=jb, i2=i2, full=full):
                        if not full:
                            return
                        hi = (jb * 8 + h) % 2
                        pa = pACC[hi]; ba = bACC[hi]
                        smh = sm[:, h, :]
                        S.op("act", lambda e, pa=pa, smh=smh: e.activation(junk[:], pa[:], AF.Square, scale=512.0 ** -0.5, accum_out=smh[:, 0:1]),
                             reads=[ba], writes=[Bjunk, Bsm[h]])
                        S.op("act", lambda e, smh=smh, h=h: e.activation(smh[:, 1:2], pSM[:, h:h + 1], AF.Square, scale=EPS ** 0.5),
                             reads=[bDEN[h], Bsm[h]], writes=[Bsm[h]])
                        S.op("dve", lambda e, smh=smh, h=h, jb=jb: e.scalar_tensor_tensor(smh[:, 3:4], smh[:, 1:2], fem[:, jb, h:h + 1], smh[:, 0:1], ALU.max, ALU.add),
                             reads=[Bfac, Bsm[h]], writes=[Bsm[h]])
                        S.op("act", lambda e, smh=smh: e.sqrt(smh[:, 3:4], smh[:, 3:4]), reads=[Bsm[h]], writes=[Bsm[h]])
                        S.op("dve", lambda e, smh=smh: e.reciprocal(smh[:, 4:5], smh[:, 3:4]), reads=[Bsm[h]], writes=[Bsm[h]])
                        S.op("act", lambda e, pa=pa, hi=hi, smh=smh: e.activation(hn[hi][:], pa[:], AF.Copy, scale=smh[:, 4:5]),
                             reads=[ba, Bsm[h]], writes=[Bhn[hi]])

                    def B2(h, jb=jb, i2=i2, full=full):
                        hi = (jb * 8 + h) % 2
                        hk = (jb * 8 + h) % 3
                        dec_ap = fdec[:, jb, h:h + 1]
                        if full:
                            ph = pHT[:].bitcast(BF16)
                            for ec in range(4):
                                S.op("pe", lambda e, ph=ph, ec=ec, hi=hi: e.transpose(ph[:, ec * 128:(ec + 1) * 128], hn[hi][:, ec * 128:(ec + 1) * 128], ident_b),
                                     reads=[Bhn[hi], Bcb], writes=[bHT])
                            S.op("dve", lambda e, ph=ph, h=h: e.tensor_tensor(yt[i2][:, h * 4:(h + 1) * 4, :], ph[:, 0:512].rearrange("p (a t) -> p a t", a=4),
                                                                              gz[i2][:, h * 4:(h + 1) * 4, :], ALU.mult),
                                 reads=[bHT, Bgz[i2]], writes=[Byt[i2]])
                        for c in range(2):
                            S.op("pe", lambda e, c=c, hk=hk: e.matmul(pC[c][:], kT[hk][:, c * 128:(c + 1) * 128], vp[hk][:, 0:512], start=True, stop=True),
                                 reads=[BkT[hk], Bvp[hk]], writes=[bC[c]])
                            S.op("pe", lambda e, c=c, hk=hk, h=h: e.matmul(pSM[:, 8 + h * 2 + c:8 + h * 2 + c + 1], kT[hk][:, c * 128:(c + 1) * 128], vp[hk][:, 512:513], start=True, stop=True),
                                 reads=[BkT[hk], Bvp[hk]], writes=[bPN[h]])
                            S.op("dve", lambda e, c=c, h=h, dec_ap=dec_ap: e.scalar_tensor_tensor(Cst[:, h, c, :], Cst[:, h, c, :], dec_ap, pC[c][:], ALU.mult, ALU.add),
                                 reads=[BC[h], Bfac, bC[c]], writes=[BC[h]])

                    A1(0)
                    A2(0)
                    for h in range(8):
                        if h + 1 < 8:
                            A1(h + 1)
                        B1(h)
                        if h + 1 < 8:
                            A2(h + 1)
                        if h >= 1:
                            B2(h - 1)
                    B2(7)
                    S.op("dve", lambda e, jb=jb: e.tensor_tensor(Nst[:], Nst[:], fdec[:, jb, :].unsqueeze(2).to_broadcast([128, 8, 2]), ALU.mult),
                         reads=[BN, Bfac, BCbn], writes=[BN])
                    S.op("dve", lambda e: e.tensor_tensor(Nst[:], Nst[:], pSM[:, 8:24].rearrange("p (h c) -> p h c", c=2), ALU.add),
                         reads=[BN] + bPN, writes=[BN])
                    if full:
                        S.dma("sp", YT[:, :, t0 - HALO0:t0 - HALO0 + 128].rearrange("b p t -> p b t"), yt[i2][:], reads=[Byt[i2]])
            S.barrier()


        def phase_out(layer):
            if layer == 0:
                NE, w_ap, YTsrc, nblk = 32, a_w_out, YT, NH1 // 128
            else:
                NE, w_ap, YTsrc, nblk = 16, b_w_out, YT1, 16
            with ExitStack() as pc:
                wob = sb(pc, "wob", [128, NE, 2048], BF16)
                Bwob = S.bufs(NE, "wob")
                GP = sb(pc, "GP", [128, 2048], F32)
                BGP = S.buf("GP")
                fv = sb(pc, "fv", [128, 6, 16], F32)
                Bfv = S.buf("fv")
                gl = sb(pc, "gl", [128, 3, 16], F32)
                with ExitStack() as pw:
                    wst = [sb(pw, f"wstO{i}", [128, 2048], F32) for i in range(2)]
                    Bwst = S.bufs(2, "wstO")
                    for ec in range(NE):
                        i = ec % 2
                        S.dma("sp", wst[i][:], w_ap[ec * 128:(ec + 1) * 128, :], writes=[Bwst[i]])
                        if ec % 2 == 0:
                            S.op("dve", lambda e, i=i, ec=ec: e.tensor_copy(wob[:, ec, :], wst[i][:]), reads=[Bwst[i]], writes=[Bwob[ec]])
                        else:
                            S.op("act", lambda e, i=i, ec=ec: e.copy(wob[:, ec, :], wst[i][:]), reads=[Bwst[i]], writes=[Bwob[ec]])
                    tl = wst[0]
                    btl = Bwst[0]
                    gate_row = 2 if layer == 0 else 5
                    S.dma("sp", GP[:], VECS[gate_row:gate_row + 1, :].partition_broadcast(128), writes=[BGP])
                    S.dma("sp", tl[:], g_post[layer:layer + 1, :].partition_broadcast(128), writes=[btl])
                    S.op("dve", lambda e: e.tensor_tensor(GP[:], GP[:], tl[:], ALU.mult), reads=[btl, BGP], writes=[BGP])
                    if layer == 0:
                        S.dma("sp", gl[:], gvec_l, writes=[Bfv])
                        for slot, row in ((4, 7), (1, 6), (5, 4), (3, 3)):
                            S.dma("sp", fv[:, slot, :], VECS[row, :].rearrange("(k p) -> p k", p=128), writes=[Bfv], slow=True)
                        S.op("dve", lambda e: e.scalar_tensor_tensor(fv[:, 0, :], fv[:, 4, :], 1.0, gl[:, 1, :], ALU.add, ALU.mult), reads=[Bfv], writes=[Bfv])
                        S.op("dve", lambda e: e.scalar_tensor_tensor(fv[:, 2, :], fv[:, 5, :], 1.0, gl[:, 0, :], ALU.add, ALU.mult), reads=[Bfv], writes=[Bfv])
                    S.barrier()
                ytl = [sb(pc, f"ytl{i}", [128, NE, 128], BF16) for i in range(2)]
                Bytl = S.bufs(2, "ytl")
                xr = [sb(pc, f"xr{i}", [128, 2048], F32) for i in range(2)]
                Bxr = S.bufs(2, "xr")
                x1 = [sb(pc, f"x1{i}", [128, 2048], F32) for i in range(2)]
                Bx1 = S.bufs(2, "x1")
                Bx1q = [S.bufs(4, "x1q0"), S.bufs(4, "x1q1")]
                junk = sb(pc, "junkO", [128, 512], BF16)
                Bjunk = S.buf("junkO")
                ssq = sb(pc, "ssqO", [128, 8], F32)
                Bssq = S.buf("ssqO")
                pso = [PS[0], PS[1], PS[2], PS[3]]
                bso = [PB[0], PB[1], PB[2], PB[3]]
                if layer == 0:
                    xnb = [sb(pc, f"xnb{i}", [128, 2048], BF16) for i in range(2)]
                    Bxnb = S.bufs(2, "xnb")
                    hTt = [sb(pc, f"hTt{i}", [128, 16, 128], BF16) for i in range(2)]
                    BhTt = S.bufs(2, "hTt")
                    BhTt2 = S.bufs(2, "hTt2")
                ptr = [PS[4], PS[5], PS[6], PS[7]]
                btr = [PB[4], PB[5], PB[6], PB[7]]
                cnt = {"t": 0, "h": 0}

                def rstd_chain(col):
                    S.op("dve", lambda e: e.tensor_scalar(col, col, 1.0 / D, EPS, ALU.mult, ALU.add), reads=[Bssq], writes=[Bssq])
                    S.op("act", lambda e: e.sqrt(col, col), reads=[Bssq], writes=[Bssq])
                    S.op("dve", lambda e: e.reciprocal(col, col), reads=[Bssq], writes=[Bssq])

                def loadsO(jb):
                    i2 = jb % 2
                    t0 = jb * 128
                    S.dma("sp", ytl[i2][:], YTsrc[:, :, t0:t0 + 128].rearrange("b p t -> p b t"), writes=[Bytl[i2]])
                    if layer == 0:
                        S.dma("sp", xr[i2][:], xw[HALO0 + t0:HALO0 + t0 + 128, :], writes=[Bxr[i2]])
                    else:
                        S.dma("sp", xr[i2][:], X1[512 + t0:512 + t0 + 128, :], writes=[Bxr[i2]])

                deferred = []
                loadsO(0)
                for jb in range(nblk):
                    i2 = jb % 2
                    t0 = jb * 128
                    if jb + 1 < nblk:
                        loadsO(jb + 1)
                    order = [(q, ec) for ec in range(NE) for q in range(4)] if jb == 0 else [(q, ec) for q in range(4) for ec in range(NE)]
                    for q, ec in order:
                        S.op("pe", lambda e, q=q, ec=ec, i2=i2: e.matmul(pso[q][:], ytl[i2][:, ec, :], wob[:, ec, q * 512:(q + 1) * 512], start=(ec == 0), stop=(ec == NE - 1)),
                             reads=[Bytl[i2], Bwob[ec]], writes=[bso[q]])
                    for q in range(4):
                        S.op("act", lambda e, q=q, i2=i2: e.copy(x1[i2][:, q * 512:(q + 1) * 512], pso[q][:]),
                             reads=[bso[q]], writes=[Bx1q[i2][q]] + ([Bx1[i2]] if q == 0 else []))
                        S.op("dve", lambda e, q=q, i2=i2: e.scalar_tensor_tensor(junk[:], x1[i2][:, q * 512:(q + 1) * 512], 1.0, x1[i2][:, q * 512:(q + 1) * 512],
                                                                               ALU.mult, ALU.mult, accum_out=ssq[:, q:q + 1]),
                             reads=[Bx1q[i2][q]], writes=[Bjunk, Bssq])
                    S.op("dve", lambda e: e.tensor_reduce(ssq[:, 4:5], ssq[:, 0:4], AX.X, ALU.add), reads=[Bssq], writes=[Bssq])
                    rstd_chain(ssq[:, 4:5])
                    S.op("dve", lambda e, i2=i2: e.scalar_tensor_tensor(x1[i2][:], x1[i2][:], ssq[:, 4:5], GP[:], ALU.mult, ALU.mult),
                         reads=Bx1q[i2] + [Bssq, BGP, Bx1[i2]], writes=[Bx1[i2]] + Bx1q[i2])
                    S.op("dve", lambda e, i2=i2: e.tensor_tensor(x1[i2][:], x1[i2][:], xr[i2][:], ALU.add), reads=[Bxr[i2], Bx1[i2]], writes=[Bx1[i2]])
                    if layer == 0:
                        S.dma("sp", X1[t0:t0 + 128, :], x1[i2][:], reads=[Bx1[i2]])
                        S.op("act", lambda e, i2=i2: e.activation(xnb[i2][:], x1[i2][:], AF.Square, accum_out=ssq[:, 5:6]),
                             reads=[Bx1[i2]], writes=[Bxnb[i2], Bssq])
                        rstd_chain(ssq[:, 5:6])
                        S.op("dve", lambda e, i2=i2: e.tensor_scalar(xnb[i2][:], x1[i2][:], ssq[:, 5:6], None, ALU.mult),
                             reads=[Bx1[i2], Bssq], writes=[Bxnb[i2]])
                        def E2(jb=jb, i2=i2, t0=t0):
                            pvs = []
                            for hh in range(2):
                                p, pb = ptr[i2 * 2 + hh], btr[i2 * 2 + hh]
                                pv = p[:].bitcast(BF16)
                                for j in range(8):
                                    kc = hh * 8 + j
                                    S.op("pe", lambda e, pv=pv, j=j, kc=kc: e.transpose(pv[:, j * 128:(j + 1) * 128], xnb[i2][:, kc * 128:(kc + 1) * 128], ident_b),
                                         reads=[Bxnb[i2], Bcb], writes=[pb])
                                pvs.append((pv, pb))
                            sets = [(0, 1, HKVT[:, :, t0:t0 + 128])]
                            if jb >= 4:
                                sets.append((2, 3, H1T[:, :, t0 - 512:t0 - 512 + 128]))
                            for gs, ss_, dst in sets:
                                hi = cnt["h"] % 2
                                cnt["h"] += 1
                                for hh in range(2):
                                    pv, pb = pvs[hh]
                                    for j in range(8):
                                        kc = hh * 8 + j
                                        S.op("act", lambda e, pv=pv, j=j, kc=kc, gs=gs, ss_=ss_, hi=hi: e.activation(hTt[hi][:, kc, :], pv[:, j * 128:(j + 1) * 128], AF.Identity,
                                                                                                               bias=fv[:, ss_, kc:kc + 1], scale=fv[:, gs, kc:kc + 1]),
                                             reads=[pb, Bfv], writes=[BhTt[hi]])
                                S.dma("sp", dst.rearrange("k p t -> p k t"), hTt[hi][:], reads=[BhTt[hi], BhTt2[hi]], writes=[])
                        while deferred:
                            deferred.pop(0)()
                        deferred.append(E2)
                    else:
                        S.dma("sp", y_out[t0:t0 + 128, :], x1[i2][:], reads=[Bx1[i2]])
                while deferred:
                    deferred.pop(0)()
            S.barrier()

        def phase_proj1(src, ntok, w_ap, feat_dst, feat_silu, tok_dst):
            with ExitStack() as pc:
                aT = sb(pc, "aT", [128, 16, ntok], BF16)
                BaT = S.bufs(ntok // 128, "aT")
                for kc in range(16):
                    S.dma("sp", aT[:, kc, :], src[kc, :, :], writes=BaT)
                wst = [sb(pc, f"wstE{i}", [128, 16, 256], F32) for i in range(3)]
                Bwst = S.bufs(3, "wstE")
                wbf = [sb(pc, f"wbfE{i}", [128, 16, 256], BF16) for i in range(3)]
                Bwbf = S.bufs(3, "wbfE")
                ob = [sb(pc, f"obE{i}", [128, 512], BF16) for i in range(3)]
                Bob = S.bufs(3, "obE")
                cnt = {"w": 0, "ob": 0, "iss": 0}
                w_view = w_ap.rearrange("(kc p) c -> p kc c", p=128)
                ntt = ntok // 512
                wseq = [g * 256 for g in range(16)]

                cnt["cv"] = 0
                Bwbf2 = S.bufs(3, "wbfE2")

                def issue_load():
                    k = cnt["iss"]
                    cnt["iss"] += 1
                    c0 = wseq[k]
                    i = k % 3
                    S.dma("sp", wst[i][:], w_view[:, :, c0:c0 + 256], writes=[Bwst[i]])

                def issue_cv():
                    k = cnt["cv"]
                    cnt["cv"] += 1
                    i = k % 3
                    S.op("pool", lambda e, i=i: e.tensor_copy(wbf[i][:, 0:8, :], wst[i][:, 0:8, :]), reads=[Bwst[i]], writes=[Bwbf[i]])
                    S.op("act", lambda e, i=i: e.copy(wbf[i][:, 8:16, :], wst[i][:, 8:16, :]), reads=[Bwst[i]], writes=[Bwbf2[i]])

                def load_w(c0):
                    k = cnt["w"]
                    assert wseq[k] == c0
                    cnt["w"] += 1
                    while cnt["iss"] <= k + 2 and cnt["iss"] < len(wseq):
                        issue_load()
                    while cnt["cv"] <= k + 1 and cnt["cv"] < len(wseq):
                        issue_cv()
                    i = k % 3
                    return wbf[i], [Bwbf[i], Bwbf2[i]]

                def feat_group(c0, dst, silu):
                    w, bw = load_w(c0)
                    for cblk in range(2):
                        blk = (c0 % 2048) // 128 + cblk
                        for tt in range(ntt):
                            p, pb = next_ps()
                            for kc in range(16):
                                S.op("pe", lambda e, p=p, kc=kc, w=w, cblk=cblk, tt=tt: e.matmul(p[:], w[:, kc, cblk * 128:(cblk + 1) * 128], aT[:, kc, tt * 512:(tt + 1) * 512],
                                                                                             start=(kc == 0), stop=(kc == 15)),
                                     reads=bw + BaT[tt * 4:(tt + 1) * 4], writes=[pb])
                            oi = cnt["ob"] % 3
                            cnt["ob"] += 1
                            if silu:
                                S.op("act", lambda e, p=p, oi=oi: e.activation(ob[oi][:], p[:], AF.Silu), reads=[pb], writes=[Bob[oi]])
                            else:
                                S.op("act", lambda e, p=p, oi=oi: e.copy(ob[oi][:], p[:]), reads=[pb], writes=[Bob[oi]])
                            S.dma("sp", dst[blk, :, tt * 512:(tt + 1) * 512], ob[oi][:], reads=[Bob[oi]])

                for grp in range(8):
                    feat_group(grp * 256, feat_dst, False)
                for grp in range(8):
                    if tok_dst is None:
                        feat_group(2048 + grp * 256, feat_silu, True)
                    else:
                        w, bw = load_w(2048 + grp * 256)
                        for tb2 in range(ntok // 256):
                            p, pb = next_ps()
                            for s2 in range(2):
                                tb = tb2 * 2 + s2
                                for kc in range(16):
                                    S.op("pe", lambda e, p=p, kc=kc, tb=tb, s2=s2, w=w: e.matmul(p[:, s2 * 256:(s2 + 1) * 256], aT[:, kc, tb * 128:(tb + 1) * 128],
                                                                                               w[:, kc, :], start=(kc == 0), stop=(kc == 15)),
                                         reads=bw + [BaT[tb]], writes=[pb])
                            oi = cnt["ob"] % 3
                            cnt["ob"] += 1
                            S.op("act", lambda e, p=p, oi=oi: e.copy(ob[oi][:], p[:]), reads=[pb], writes=[Bob[oi]])
                            S.dma("sp", tok_dst[tb2 * 256:(tb2 + 1) * 256, grp * 256:(grp + 1) * 256].rearrange("(s p) c -> p s c", p=128),
                                  ob[oi][:].rearrange("p (s c) -> p s c", s=2), reads=[Bob[oi]])
            S.barrier()

        def phaseF():
            SCALE = 128.0 ** -0.5
            with ExitStack() as pc:
                bm = sb(pc, "bm", [128, 16, 640], F32)
                Bbm = S.buf("bm")
                am = sb(pc, "am", [128, 640], F32)
                S.dma("sp", bm[:], bias_l.rearrange("h p j -> p h j"), writes=[Bbm])
                S.dma("sp", am[:], amask, writes=[Bbm])
                S.op("dve", lambda e: e.tensor_tensor(bm[:], bm[:], am[:].unsqueeze(1).to_broadcast([128, 16, 640]), ALU.add), reads=[Bbm], writes=[Bbm])
                qT = [sb(pc, f"qT{i}", [128, 16, 128], BF16) for i in range(2)]
                Kt = [sb(pc, f"Kt{i}", [128, 16, 640], BF16) for i in range(2)]
                Vt = [sb(pc, f"Vt{i}", [128, 5, 2048], BF16) for i in range(2)]
                zt = [sb(pc, f"zt{i}", [128, 16, 128], BF16) for i in range(2)]
                yt = [sb(pc, f"ytF{i}", [128, 16, 128], BF16) for i in range(2)]
                BqT = S.bufs(2, "qT"); BKt = S.bufs(2, "Kt"); BVt = S.bufs(2, "Vt"); Bzt = S.bufs(2, "zt"); Byt = S.bufs(2, "ytF")
                st = [sb(pc, f"st{i}", [128, 640], F32) for i in range(3)]
                Bst = S.bufs(3, "st")
                pb16 = [sb(pc, f"pb16{i}", [128, 640], BF16) for i in range(3)]
                Bpb = S.bufs(3, "pb16")
                pT = [sb(pc, f"pT{i}", [128, 640], BF16) for i in range(2)]
                BpT = S.bufs(2, "pT")
                on = [sb(pc, f"on{i}", [128, 4, 128], BF16) for i in range(2)]
                Bon = S.bufs(2, "on")
                sm = sb(pc, "smF", [128, 3, 16], F32)
                Bsm = S.bufs(16, "smF")
                Brinv = S.bufs(4, "rinv")
                pA = [PS[0], PS[1], PS[2]]; bA = [PB[0], PB[1], PB[2]]
                pBk = [PS[3][:, k * 128:(k + 1) * 128] for k in range(3)]; bBk = [PB[3]] * 3
                pTr, bTr = PS[4], PB[4]
                pO = [PS[5], PS[6]]; bO = [PB[5], PB[6]]
                pOT, bOT = PS[7], PB[7]
                def loadsF(qb):
                    i2 = qb % 2
                    t0 = qb * 128
                    S.dma("sp", qT[i2][:], QT[:, :, t0:t0 + 128].rearrange("h p t -> p h t"), writes=[BqT[i2]])
                    S.dma("sp", Kt[i2][:], KT[:, :, t0:t0 + 640].rearrange("h p t -> p h t"), writes=[BKt[i2]])
                    S.dma("sp", Vt[i2][:], V1[t0:t0 + 640, :].rearrange("(kb p) c -> p kb c", p=128), writes=[BVt[i2]])
                    S.dma("sp", zt[i2][:], Z1[:, :, t0:t0 + 128].rearrange("h p t -> p h t"), writes=[Bzt[i2]])

                loadsF(0)
                for qb in range(16):
                    i2 = qb % 2
                    t0 = qb * 128
                    if qb + 1 < 16:
                        loadsF(qb + 1)
                    def FA(h, qb=qb, i2=i2):
                        hi = h % 3
                        S.op("pe", lambda e, hi=hi, h=h: e.matmul(pA[hi][:], qT[i2][:, h, :], Kt[i2][:, h, 0:512], start=True, stop=True),
                             reads=[BqT[i2], BKt[i2]], writes=[bA[hi]])
                        S.op("pe", lambda e, hi=hi, h=h: e.matmul(pBk[hi], qT[i2][:, h, :], Kt[i2][:, h, 512:640], start=True, stop=True),
                             reads=[BqT[i2], BKt[i2]], writes=[bBk[hi]])
                        S.op("dve", lambda e, hi=hi, h=h: e.scalar_tensor_tensor(st[hi][:, 0:512], pA[hi][:], SCALE, bm[:, h, 0:512], ALU.mult, ALU.add),
                             reads=[bA[hi], Bbm], writes=[Bst[hi]])
                        S.op("dve", lambda e, hi=hi, h=h: e.scalar_tensor_tensor(st[hi][:, 512:640], pBk[hi], SCALE, bm[:, h, 512:640], ALU.mult, ALU.add),
                             reads=[bBk[hi], Bbm, Bst[hi]], writes=[Bst[hi]])
                        if qb < 4:
                            j0 = 512 - qb * 128
                            S.op("dve", lambda e, hi=hi, j0=j0: e.tensor_scalar(st[hi][:, 0:j0], st[hi][:, 0:j0], flg[:, 1:2], None, ALU.add),
                                 reads=[Bst[hi], Bflg], writes=[Bst[hi]])
                        S.op("dve", lambda e, hi=hi, h=h: e.tensor_reduce(sm[:, 0, h:h + 1], st[hi][:], AX.X, ALU.max, negate=True),
                             reads=[Bst[hi]], writes=[Bsm[h]])
                        S.op("act", lambda e, hi=hi, h=h: e.activation(pb16[hi][:], st[hi][:], AF.Exp, bias=sm[:, 0, h:h + 1], accum_out=sm[:, 1, h:h + 1]),
                             reads=[Bst[hi], Bsm[h]], writes=[Bpb[hi], Bsm[h]])

                    def FB(h, qb=qb, i2=i2):
                        hi = h % 2
                        h3 = h % 3
                        pv = pTr[:].bitcast(BF16)
                        for kb in range(5):
                            S.op("pe", lambda e, pv=pv, kb=kb, h3=h3: e.transpose(pv[:, kb * 128:(kb + 1) * 128], pb16[h3][:, kb * 128:(kb + 1) * 128], ident_b),
                                 reads=[Bpb[h3], Bcb], writes=[bTr])
                        S.op("act", lambda e, pv=pv, hi=hi: e.copy(pT[hi][:], pv[:, 0:640]), reads=[bTr], writes=[BpT[hi]])
                        if h + 2 < 16:
                            FA(h + 2)
                        quad = h // 4
                        oq = quad % 2
                        for kb in range(5):
                            S.op("pe", lambda e, kb=kb, hi=hi, h=h, oq=oq: e.matmul(pO[oq][:, (h % 4) * 128:(h % 4 + 1) * 128], pT[hi][:, kb * 128:(kb + 1) * 128],
                                                                                   Vt[i2][:, kb, h * 128:(h + 1) * 128], start=(kb == 0), stop=(kb == 4)),
                                 reads=[BpT[hi], BVt[i2]], writes=[bO[oq]])
                        if h % 4 == 3:
                            h0 = quad * 4
                            S.op("dve", lambda e, h0=h0: e.reciprocal(sm[:, 2, h0:h0 + 4], sm[:, 1, h0:h0 + 4]),
                                 reads=Bsm[h0:h0 + 4], writes=[Brinv[quad]])
                            S.op("dve", lambda e, h0=h0, oq=oq: e.tensor_tensor(on[oq][:], pO[oq][:].rearrange("p (a d) -> p a d", a=4),
                                                                                sm[:, 2, h0:h0 + 4].unsqueeze(2).to_broadcast([128, 4, 128]), ALU.mult),
                                 reads=[bO[oq], Brinv[quad]], writes=[Bon[oq]])
                            pv2 = pOT[:].bitcast(BF16)
                            for a in range(4):
                                S.op("pe", lambda e, pv2=pv2, a=a, oq=oq: e.transpose(pv2[:, a * 128:(a + 1) * 128], on[oq][:, a, :], ident_b),
                                     reads=[Bon[oq], Bcb], writes=[bOT])
                            S.op("dve", lambda e, pv2=pv2, h0=h0: e.tensor_tensor(yt[i2][:, h0:h0 + 4, :], pv2[:, 0:512].rearrange("p (a t) -> p a t", a=4),
                                                                                 zt[i2][:, h0:h0 + 4, :], ALU.mult),
                                 reads=[bOT, Bzt[i2]], writes=[Byt[i2]])

                    FA(0)
                    FA(1)
                    for h in range(16):
                        FB(h)
                    S.dma("sp", YT1[:, :, t0:t0 + 128].rearrange("h p t -> p h t"), yt[i2][:], reads=[Byt[i2]])
            S.barrier()

        carry = sb(ctx, "carry", [128, 32, 3], F32)
        Bcarry = S.bufs(32, "carry")

        phase0()
        if stop_after >= 1:
            phaseAB(0)
        if stop_after >= 3:
            phaseAB(1)
        if stop_after >= 4:
            phaseC()
        if stop_after >= 5:
            phase_out(0)
        if stop_after >= 6:
            phase_proj1(HKVT, NH1, kv_w, KT, None, V1)
            phase_proj1(H1T, 2048, b_w_in, QT, Z1, None)
        if stop_after >= 7:
            phaseF()
        if stop_after >= 8:
            phase_out(1)

        S.wait_all("sp", [t for t in S.dma_tok if t is not None])
        S.finalize()
    return nc


def make_consts():
    c = np.zeros((128, 4, 128), np.float32)
    c[:, 0, :] = np.eye(128, dtype=np.float32)
    c[:, 1, :] = 1.0
    c[:, 2, :] = np.triu(np.ones((128, 128), np.float32))
    c[127, 3, :] = 1.0
    return c


def make_amask():
    r = np.arange(128)[:, None]
    j = np.arange(640)[None, :]
    key = j - 512
    ch = r // 64
    lo = ch * 64 - 512
    hi = ch * 64 + 64
    ok = (key >= lo) & (key < hi)
    return np.where(ok, 0.0, -BIG).astype(np.float32)


def make_core_inputs(inputs, b, p):
    x = np.asarray(inputs["x"])
    if p == 1:
        xw = x[b]
    else:
        xw = np.concatenate([x[b, 2048:], x[b, :2048]], axis=0)
    f = 1.0 if p == 1 else 0.0
    flag = np.zeros((128, 2), np.float32)
    flag[:, 0] = f
    flag[:, 1] = (f - 1.0) * BIG
    c = np.asarray(inputs["c"])[b]
    cT = np.ascontiguousarray(c.reshape(16, 128).T)
    conv_w = np.asarray(inputs["a_conv_w"])[0]
    convw_l = np.ascontiguousarray(conv_w.reshape(4, 32, 128).transpose(2, 1, 0))
    convb_l = np.ascontiguousarray(np.asarray(inputs["a_conv_b"])[0].reshape(32, 128).T)
    gateb_l = np.ascontiguousarray(np.asarray(inputs["a_gate_b"])[0].reshape(2, 8).T)
    ghead_l = np.ascontiguousarray(np.asarray(inputs["a_g_head"])[0].reshape(32, 128).T)
    gvec_l = np.ascontiguousarray(np.stack([np.asarray(inputs["g_pre"])[1].reshape(16, 128).T,
                                            np.asarray(inputs["kv_g"]).reshape(16, 128).T,
                                            np.asarray(inputs["g_pre"])[0].reshape(16, 128).T], axis=1))
    rel = np.asarray(inputs["b_rel"])[0]
    r = np.arange(128)[:, None]
    j = np.arange(640)[None, :]
    bucket = np.clip(r + 512 - j, -128, 128) + 128
    bias_l = np.ascontiguousarray(rel[:, bucket])
    return {
        "xw": np.ascontiguousarray(xw), "cT": cT, "flag": flag, "consts": make_consts(),
        "ada_w": np.asarray(inputs["ada_w"]), "ada_b": np.asarray(inputs["ada_b"]),
        "g_pre": np.asarray(inputs["g_pre"]), "g_post": np.asarray(inputs["g_post"]),
        "a_w_in": np.asarray(inputs["a_w_in"])[0], "convw_l": convw_l, "convb_l": convb_l,
        "gateb_l": gateb_l, "ghead_l": ghead_l, "gvec_l": gvec_l, "a_w_out": np.asarray(inputs["a_w_out"])[0],
        "kv_ada_w": np.asarray(inputs["kv_ada_w"]), "kv_ada_b": np.asarray(inputs["kv_ada_b"]).reshape(1, -1),
        "kv_g": np.asarray(inputs["kv_g"]).reshape(1, -1), "kv_w": np.asarray(inputs["kv_w"]),
        "b_w_in": np.asarray(inputs["b_w_in"])[0], "bias_l": bias_l, "amask": make_amask(),
        "b_w_out": np.asarray(inputs["b_w_out"])[0],
    }


def kernel(**inputs):
    nc = build_program()
    in_maps = []
    for core in range(8):
        b, p = core // 2, core % 2
        in_maps.append(make_core_inputs(inputs, b, p))
    res = run_bass_kernel_spmd(nc, in_maps, core_ids=list(range(8)))
    out = np.zeros((NB, SEQ, D), np.float32)
    for core in range(8):
        b, p = core // 2, core % 2
        out[b, p * 2048:(p + 1) * 2048] = res.results[core]["y_out"]
    return out
```

```python
import numpy as np
import concourse.bass as bass
import concourse.mybir as mybir
from concourse.bass_utils import run_bass_kernel_spmd
from contextlib import ExitStack

F32 = mybir.dt.float32
BF16 = mybir.dt.bfloat16
AF = mybir.ActivationFunctionType
ALU = mybir.AluOpType
AX = mybir.AxisListType

ENGS = ("pe", "act", "dve", "pool", "sp")
NDSEM = 40

D = 2048
SEQ = 4096
NB = 4
EPS = 1e-6
A_INNER = 4096
A_COLS = 16400
A_V0 = 4096
A_O0 = 8192
A_Z0 = 12288
A_G0 = 16384
WIN = 4096
HALO0 = 1536
OWN0 = 2048
NH1 = WIN - HALO0
BIG = 30000.0
LN16 = float(np.log(16.0))


class Buf:
    __slots__ = ("name", "w", "r")

    def __init__(self, name=""):
        self.name = name
        self.w = None
        self.r = {}


class Sched:
    def __init__(self, nc, ctx):
        self.nc = nc
        self.ctx = ctx
        self.ops = {e: [] for e in ENGS}
        self.known = {e: {} for e in ENGS}
        self.awaited = {e: set() for e in ENGS}
        self.pend = {e: [] for e in ENGS}
        self.last = {e: None for e in ENGS}
        self.ndma = 0
        self.dma_tok = [None] * NDSEM
        self.nbuf = 0

    def buf(self, name=""):
        self.nbuf += 1
        return Buf(name or f"b{self.nbuf}")

    def bufs(self, n, name=""):
        return [self.buf(f"{name}{i}") for i in range(n)]

    def _collect(self, eng, reads, writes, extra=()):
        deps = {}

        def add(tok):
            if tok is None:
                return
            sk, v = tok
            if sk == eng and eng == "pe":
                return
            if self.known[eng].get(sk, 0) >= v:
                return
            if deps.get(sk, 0) < v:
                deps[sk] = v

        for b in reads:
            add(b.w)
        for b in writes:
            add(b.w)
            for t in b.r.values():
                add(t)
        for t in extra:
            add(t)
        waits = []
        for sk, v in deps.items():
            self.known[eng][sk] = v
            waits.append((sk, v))
            if not isinstance(sk, tuple):
                self.awaited[sk].add(v)
        return waits

    def op(self, eng, fn, reads=(), writes=()):
        waits = self.pend[eng] + self._collect(eng, reads, writes)
        self.pend[eng] = []
        idx = len(self.ops[eng]) + 1
        self.ops[eng].append((waits, fn, None))
        tok = (eng, idx)
        self.last[eng] = tok
        for b in reads:
            b.r[eng] = tok
        for b in writes:
            b.w = tok
            b.r = {}
        return tok

    def dma(self, q, out, in_, reads=(), writes=(), slow=False):
        k = self.ndma % NDSEM
        n = self.ndma // NDSEM + 1
        self.ndma += 1
        waits = self.pend[q] + self._collect(q, reads, writes, extra=(self.dma_tok[k],))
        self.pend[q] = []
        tok = (("d", k), n)
        self.dma_tok[k] = tok
        self.ops[q].append((waits, (out, in_, slow), tok))
        for b in reads:
            b.r[("d", k)] = tok
        for b in writes:
            b.w = tok
            b.r = {}
        return tok

    def wait_all(self, eng, toks):
        self.pend[eng] = self.pend[eng] + self._collect(eng, (), (), extra=toks)

    def barrier(self):
        toks = [t for t in self.last.values() if t is not None] + [t for t in self.dma_tok if t is not None]
        for e in ENGS:
            self.wait_all(e, toks)

    def finalize(self):
        nc = self.nc
        ctx = self.ctx
        esem = {e: ctx.enter_context(nc.semaphore(f"s_{e}")) for e in ENGS}
        dsem = [ctx.enter_context(nc.semaphore(f"d_{k}")) for k in range(NDSEM)]
        val = {}
        for e in ENGS:
            aw = sorted(self.awaited[e])
            val[e] = {idx: i + 1 for i, idx in enumerate(aw)}
        self.maxval = {e: len(val[e]) for e in ENGS}

        def emit_waits(engobj, waits):
            for sk, v in waits:
                if isinstance(sk, tuple):
                    engobj.wait_ge(dsem[sk[1]], 16 * v)
                else:
                    engobj.wait_ge(esem[sk], val[sk][v])

        def emit(e, engobj):
            for i, (waits, fn, dtok) in enumerate(self.ops[e]):
                emit_waits(engobj, waits)
                if dtok is not None:
                    out, in_, slow = fn
                    if slow:
                        engobj.dma_start(out=out, in_=in_, allow_slow_non_contiguous=True).then_inc(dsem[dtok[0][1]], 16)
                    else:
                        engobj.dma_start(out=out, in_=in_).then_inc(dsem[dtok[0][1]], 16)
                else:
                    ins = fn(engobj)
                    if (i + 1) in val[e]:
                        ins.then_inc(esem[e], 1)
            emit_waits(engobj, self.pend[e])

        with nc.Block() as block:
            @block.tensor
            def _(eng):
                emit("pe", eng)

            @block.scalar
            def _(eng):
                emit("act", eng)

            @block.vector
            def _(eng):
                emit("dve", eng)

            @block.gpsimd
            def _(eng):
                emit("pool", eng)

            @block.sync
            def _(eng):
                emit("sp", eng)


def build_program(debug=False, stop_after=99):
    nc = bass.Bass("TRN2", target_bir_lowering=False)

    def din(name, shape, dt=F32):
        return nc.dram_tensor(name, list(shape), dt, kind="ExternalInput").ap()

    def dscr(name, shape, dt):
        kind = "ExternalOutput" if debug else "Internal"
        return nc.dram_tensor(name, list(shape), dt, kind=kind).ap()

    xw = din("xw", [WIN, D])
    cT = din("cT", [128, 16])
    flag = din("flag", [128, 2])
    consts = din("consts", [128, 4, 128])
    ada_w = din("ada_w", [2, D, 3 * D])
    ada_b = din("ada_b", [2, 3 * D])
    g_pre = din("g_pre", [2, D])
    g_post = din("g_post", [2, D])
    a_w_in = din("a_w_in", [D, A_COLS])
    convw_l = din("convw_l", [128, 32, 4])
    convb_l = din("convb_l", [128, 32])
    gateb_l = din("gateb_l", [8, 2])
    ghead_l = din("ghead_l", [128, 32])
    gvec_l = din("gvec_l", [128, 3, 16])
    a_w_out = din("a_w_out", [A_INNER, D])
    kv_ada_w = din("kv_ada_w", [D, 2 * D])
    kv_ada_b = din("kv_ada_b", [1, 2 * D])
    kv_g = din("kv_g", [1, D])
    kv_w = din("kv_w", [D, 2 * D])
    b_w_in = din("b_w_in", [D, 2 * D])
    bias_l = din("bias_l", [16, 128, 640])
    amask = din("amask", [128, 640])
    b_w_out = din("b_w_out", [D, D])
    y_out = nc.dram_tensor("y_out", [2048, D], F32, kind="ExternalOutput").ap()

    VECS = dscr("VECS", [8, D], F32)
    QKs = dscr("QKs", [32, 128, WIN], BF16)
    Vs = dscr("Vs", [WIN, A_INNER], BF16)
    GZs = dscr("GZs", [32, 128, NH1], BF16)
    GATES = dscr("GATES", [16, WIN], F32)
    YT = dscr("YT", [32, 128, NH1], BF16)
    X1 = dscr("X1", [NH1, D], F32)
    HKVT = dscr("HKVT", [16, 128, NH1], BF16)
    H1T = dscr("H1T", [16, 128, 2048], BF16)
    KT = dscr("KT", [16, 128, NH1], BF16)
    V1 = dscr("V1", [NH1, 2048], BF16)
    QT = dscr("QT", [16, 128, 2048], BF16)
    Z1 = dscr("Z1", [16, 128, 2048], BF16)
    YT1 = dscr("YT1", [16, 128, 2048], BF16)

    with ExitStack() as ctx:
        S = Sched(nc, ctx)

        uniq = [0]

        def sb(c, name, shape, dt):
            uniq[0] += 1
            return c.enter_context(nc.sbuf_tensor(f"{name}_{uniq[0]}", list(shape), dt))

        PS = [ctx.enter_context(nc.psum_tensor(f"ps{i}", [128, 512], F32)) for i in range(8)]
        PB = S.bufs(8, "ps")
        psrr = [0]

        def next_ps():
            i = psrr[0] % 8
            psrr[0] += 1
            return PS[i], PB[i]

        cst = sb(ctx, "cst", [128, 4, 128], F32)
        cstb = sb(ctx, "cstb", [128, 4, 128], BF16)
        flg = sb(ctx, "flg", [128, 2], F32)
        scT = sb(ctx, "scT", [128, 16], F32)
        Bc = S.buf("cst")
        Bcb = S.buf("cstb")
        Bflg = S.buf("flg")
        BscT = S.buf("scT")
        S.dma("sp", cst[:], consts, writes=[Bc])
        S.dma("sp", flg[:], flag, writes=[Bflg])
        S.dma("sp", scT[:], cT, writes=[BscT])
        S.op("dve", lambda e: e.tensor_copy(cstb[:], cst[:]), reads=[Bc], writes=[Bcb])
        S.op("act", lambda e: e.activation(scT[:], scT[:], AF.Silu), reads=[BscT], writes=[BscT])
        ident_b = cstb[:, 0, :]
        ones_f = cst[:, 1, :]
        ident_f = cst[:, 0, :]

        def gemv_items(pc, glist, nwt=4):
            wt = [sb(pc, f"p0w{i}", [128, 1024], F32) for i in range(nwt)]
            Bw = S.bufs(nwt, "p0w")
            accs = [sb(pc, f"p0acc{i}", [128, 2048], F32) for i in range(2)]
            Baccs = S.bufs(2, "p0acc")
            bia = sb(pc, "p0b", [128, 2048], F32)
            Bbia = S.buf("p0b")
            vec = bia
            groups = []
            for l in range(2):
                for j in range(3):
                    groups.append((ada_w[l, :, j * D:(j + 1) * D], ada_b[l:l + 1, j * D:(j + 1) * D]))
            for j in range(2):
                groups.append((kv_ada_w[:, j * D:(j + 1) * D], kv_ada_b[0:1, j * D:(j + 1) * D]))
            chunks = []
            for gi, g in enumerate(glist):
                for kc in range(16):
                    for hc in range(2):
                        chunks.append((g, kc, hc, gi % 2))
            st = {"iss": 0}

            def issue():
                i = st["iss"]
                if i >= len(chunks):
                    return
                st["iss"] += 1
                g, kc, hc, ai = chunks[i]
                wap, bap = groups[g]
                S.dma("sp", wt[i % nwt][:], wap[kc * 128:(kc + 1) * 128, hc * 1024:(hc + 1) * 1024], writes=[Bw[i % nwt]])

            def mk_chunk(i):
                g, kc, hc, ai = chunks[i]
                acc = accs[ai]
                Bacc = Baccs[ai]

                def f():
                    while st["iss"] <= i + nwt - 1 and st["iss"] < len(chunks):
                        issue()
                    w = wt[i % nwt]
                    bw = Bw[i % nwt]
                    asl = acc[:, hc * 1024:(hc + 1) * 1024]
                    if kc == 0:
                        S.op("dve", lambda e: e.tensor_scalar(asl, w[:], scT[:, kc:kc + 1], None, ALU.mult),
                             reads=[bw, BscT], writes=[Bacc])
                    else:
                        S.op("dve", lambda e: e.scalar_tensor_tensor(asl, w[:], scT[:, kc:kc + 1], asl, ALU.mult, ALU.add),
                             reads=[bw, BscT, Bacc], writes=[Bacc])
                return f

            def mk_fin(g, ai):
                acc = accs[ai]
                Bacc = Baccs[ai]

                def f():
                    wap, bap = groups[g]
                    S.dma("sp", bia[:], bap.partition_broadcast(128), writes=[Bbia])
                    pss = []
                    for q in range(4):
                        p, pb = next_ps()
                        S.op("pe", lambda e, p=p, q=q: e.matmul(p[:], ones_f, acc[:, q * 512:(q + 1) * 512], start=True, stop=True),
                             reads=[Bacc, Bc], writes=[pb])
                        pss.append((p, pb))
                    for q, (p, pb) in enumerate(pss):
                        S.op("dve", lambda e, p=p, q=q: e.tensor_tensor(vec[:, q * 512:(q + 1) * 512], p[:], bia[:, q * 512:(q + 1) * 512], ALU.add),
                             reads=[pb, Bbia], writes=[Bbia])
                    S.dma("sp", VECS[g:g + 1, :], vec[0:1, :], reads=[Bbia])
                return f

            items = []
            pending = None
            i = 0
            for gi, g in enumerate(glist):
                for kc in range(16):
                    for hc in range(2):
                        items.append(mk_chunk(i))
                        i += 1
                    if kc == 12 and pending is not None:
                        items.append(pending)
                        pending = None
                pending = mk_fin(g, gi % 2)
            items.append(pending)
            return items

        def phase0():
            with ExitStack() as pc:
                for it in gemv_items(pc, [0, 1], nwt=4):
                    it()
            S.barrier()

        gemv_rest = {"items": None}

        def pump(n=1):
            its = gemv_rest["items"]
            for _ in range(n):
                if its:
                    its.pop(0)()

        def phaseAB(half):
            T0 = half * 2048
            with ExitStack() as pc:
                hT = sb(pc, "hT", [128, 16, 2048], BF16)
                BhT = S.bufs(16, "hT")
                with ExitStack() as pa:
                    xt = [sb(pa, f"xt{i}", [128, 2048], F32) for i in range(2)]
                    Bxt = S.bufs(2, "xt")
                    xnb = [sb(pa, f"xnbA{i}", [128, 2048], BF16) for i in range(2)]
                    Bxnb = S.bufs(2, "xnbA")
                    ss = sb(pa, "ssA", [128, 16], F32)
                    Bss = S.bufs(16, "ssA")
                    fvA = sb(pa, "fvA", [128, 4, 16], F32)
                    BfvA = S.buf("fvA")
                    S.dma("sp", fvA[:, 3, :], gvec_l[:, 2, :], writes=[BfvA])
                    S.dma("sp", fvA[:, 2, :], VECS[1, :].rearrange("(k p) -> p k", p=128), writes=[BfvA], slow=True)
                    S.dma("sp", fvA[:, 1, :], VECS[0, :].rearrange("(k p) -> p k", p=128), writes=[BfvA], slow=True)
                    S.op("dve", lambda e: e.scalar_tensor_tensor(fvA[:, 0, :], fvA[:, 2, :], 1.0, fvA[:, 3, :], ALU.add, ALU.mult), reads=[BfvA], writes=[BfvA])
                    pendA = []
                    for tb in range(16):
                        x_ = xt[tb % 2]
                        bx = Bxt[tb % 2]
                        xn_ = xnb[tb % 2]
                        bxn = Bxnb[tb % 2]
                        S.dma("sp", x_[:], xw[T0 + tb * 128:T0 + (tb + 1) * 128, :], writes=[bx])
                        sc_ = ss[:, tb:tb + 1]
                        S.op("act", lambda e, x_=x_, sc_=sc_, xn_=xn_: e.activation(xn_[:], x_[:], AF.Square, accum_out=sc_),
                             reads=[bx], writes=[bxn, Bss[tb]])
                        while pendA:
                            pendA.pop(0)()
                        S.op("dve", lambda e, sc_=sc_: e.tensor_scalar(sc_, sc_, 1.0 / D, EPS, ALU.mult, ALU.add),
                             reads=[Bss[tb]], writes=[Bss[tb]])
                        S.op("act", lambda e, sc_=sc_: e.sqrt(sc_, sc_), reads=[Bss[tb]], writes=[Bss[tb]])
                        S.op("dve", lambda e, sc_=sc_: e.reciprocal(sc_, sc_), reads=[Bss[tb]], writes=[Bss[tb]])
                        S.op("dve", lambda e, x_=x_, sc_=sc_, xn_=xn_: e.tensor_scalar(xn_[:], x_[:], sc_, None, ALU.mult),
                             reads=[bx, Bss[tb]], writes=[bxn])
                        for hh in range(2):
                            p, pb = next_ps()
                            pv = p[:].bitcast(BF16)
                            for j in range(8):
                                kc = hh * 8 + j
                                S.op("pe", lambda e, pv=pv, j=j, kc=kc, xn_=xn_: e.transpose(pv[:, j * 128:(j + 1) * 128], xn_[:, kc * 128:(kc + 1) * 128], ident_b),
                                     reads=[bxn, Bcb], writes=[pb])
                            def evA(pv=pv, pb=pb, hh=hh, tb=tb):
                                for j in range(8):
                                    kc = hh * 8 + j
                                    S.op("act", lambda e, pv=pv, j=j, kc=kc, tb=tb: e.activation(hT[:, kc, tb * 128:(tb + 1) * 128], pv[:, j * 128:(j + 1) * 128], AF.Identity,
                                                                                                bias=fvA[:, 1, kc:kc + 1], scale=fvA[:, 0, kc:kc + 1]),
                                         reads=[pb, BfvA], writes=[BhT[tb]])
                            pendA.append(evA)
                    while pendA:
                        pendA.pop(0)()
                S.barrier()
                if stop_after == 1:
                    return
                with ExitStack() as pb_:
                    wst = [sb(pb_, f"wst{i}", [128, 16, 256], F32) for i in range(3)]
                    Bwst = S.bufs(3, "wst")
                    wbf = [sb(pb_, f"wbf{i}", [128, 16, 256], BF16) for i in range(3)]
                    Bwbf = S.bufs(3, "wbf")
                    Bwbf2 = S.bufs(3, "wbf2")
                    cw = sb(pb_, "cw", [128, 32, 4], F32)
                    cb = sb(pb_, "cb", [128, 32], F32)
                    gh = sb(pb_, "gh", [128, 32], F32)
                    Bcw = S.buf("cw")
                    S.dma("sp", cw[:], convw_l, writes=[Bcw])
                    S.dma("sp", cb[:], convb_l, writes=[Bcw])
                    S.dma("sp", gh[:], ghead_l, writes=[Bcw])
                    cwf = sb(pb_, "cwf", [128, 32], F32)
                    if half == 0:
                        S.op("dve", lambda e: e.tensor_scalar(cwf[:], cw[:, :, 3], flg[:, 0:1], None, ALU.mult), reads=[Bcw, Bflg], writes=[Bcw])
                    else:
                        S.op("dve", lambda e: e.tensor_copy(cwf[:], cw[:, :, 3]), reads=[Bcw], writes=[Bcw])
                    ub = [sb(pb_, f"ub{i}", [128, 515], F32) for i in range(2)]
                    Bub = S.bufs(2, "ub")
                    cacc = [sb(pb_, f"cacc{i}", [128, 512], F32) for i in range(2)]
                    Bcacc = S.bufs(2, "cacc")
                    ob = [sb(pb_, f"ob{i}", [128, 512], BF16) for i in range(3)]
                    Bob = S.bufs(3, "ob")
                    so = [sb(pb_, f"so{i}", [128, 512], F32) for i in range(2)]
                    Bso = S.bufs(2, "so")
                    sz = [sb(pb_, f"sz{i}", [128, 512], F32) for i in range(2)]
                    Bsz = S.bufs(2, "sz")
                    gt = [sb(pb_, f"gt{i}", [16, 512], F32) for i in range(2)]
                    Bgt = S.bufs(2, "gt")
                    cnt = {"w": 0, "u": 0, "ob": 0, "so": 0, "gt": 0, "iss": 0}
                    if half == 1:
                        gemv_rest["items"] = gemv_items(pb_, [2, 3, 4, 5, 6, 7], nwt=4)
                    w_view = a_w_in.rearrange("(kc p) c -> p kc c", p=128)
                    wseq = [(g * 256, 256) for g in range(16)] + [(A_V0 + g * 256, 256) for g in range(16)]
                    for g in range(16):
                        wseq += [(A_O0 + g * 256, 256), (A_Z0 + g * 256, 256)]
                    wseq.append((A_G0, 16))

                    cnt["cv"] = 0

                    def issue_load():
                        k = cnt["iss"]
                        cnt["iss"] += 1
                        c0, ncols = wseq[k]
                        i = k % 3
                        S.dma("sp", wst[i][:, :, 0:ncols], w_view[:, :, c0:c0 + ncols], writes=[Bwst[i]])

                    def issue_cv():
                        k = cnt["cv"]
                        cnt["cv"] += 1
                        c0, ncols = wseq[k]
                        i = k % 3
                        S.op("pool", lambda e, i=i, ncols=ncols: e.tensor_copy(wbf[i][:, 0:8, 0:ncols], wst[i][:, 0:8, 0:ncols]),
                             reads=[Bwst[i]], writes=[Bwbf[i]])
                        S.op("act", lambda e, i=i, ncols=ncols: e.copy(wbf[i][:, 8:16, 0:ncols], wst[i][:, 8:16, 0:ncols]),
                             reads=[Bwst[i]], writes=[Bwbf2[i]])

                    def load_w(c0, ncols):
                        k = cnt["w"]
                        assert wseq[k] == (c0, ncols)
                        cnt["w"] += 1
                        while cnt["iss"] <= k + 2 and cnt["iss"] < len(wseq):
                            issue_load()
                        while cnt["cv"] <= k + 1 and cnt["cv"] < len(wseq):
                            issue_cv()
                        i = k % 3
                        return wbf[i], [Bwbf[i], Bwbf2[i]]

                    def mm_feat(w, bw, cblk, tt, M=128):
                        p, pb = next_ps()
                        for kc in range(16):
                            S.op("pe", lambda e, p=p, kc=kc: e.matmul(p[0:M, :], w[:, kc, cblk * 128:cblk * 128 + M],
                                                                      hT[:, kc, tt * 512:(tt + 1) * 512],
                                                                      start=(kc == 0), stop=(kc == 15)),
                                 reads=bw + BhT[tt * 4:(tt + 1) * 4], writes=[pb])
                        return p, pb

                    tts_all = [0, 1, 2, 3]
                    tts_full = [3] if half == 0 else [0, 1, 2, 3]
                    tts_q = [2, 3] if half == 0 else [0, 1, 2, 3]

                    pend_silu = []
                    for grp in range(16):
                        isq = grp < 8
                        tts = tts_q if isq else tts_all
                        w, bw = load_w(grp * 256, 256)
                        for cblk in range(2):
                            blk = grp * 2 + cblk
                            for tt in tts:
                                p, pb = mm_feat(w, bw, cblk, tt)
                                i = cnt["u"] % 2
                                cnt["u"] += 1
                                u = ub[i]
                                bu = Bub[i]
                                ca = cacc[i]
                                bca = Bcacc[i]
                                first = (half == 0 and tt == tts[0])
                                if first:
                                    S.op("dve", lambda e, u=u: e.memset(u[:, 0:3], 0.0), writes=[bu])
                                else:
                                    S.op("act", lambda e, u=u, blk=blk: e.copy(u[:, 0:3], carry[:, blk, :]),
                                         reads=[Bcarry[blk]], writes=[bu])
                                if half == 0:
                                    S.op("act", lambda e, u=u, p=p: e.activation(u[:, 3:515], p[:], AF.Copy, scale=flg[:, 0:1]),
                                         reads=[pb, Bflg, bu], writes=[bu])
                                else:
                                    S.op("act", lambda e, u=u, p=p: e.copy(u[:, 3:515], p[:]), reads=[pb, bu], writes=[bu])
                                S.op("act", lambda e, u=u, blk=blk: e.copy(carry[:, blk, :], u[:, 512:515]),
                                     reads=[bu], writes=[Bcarry[blk]])
                                S.op("act", lambda e, p=p, ca=ca, blk=blk: e.activation(ca[:], p[:], AF.Identity, bias=cb[:, blk:blk + 1], scale=cwf[:, blk:blk + 1]),
                                     reads=[pb, Bcw], writes=[bca])
                                for j in (2, 1, 0):
                                    S.op("dve", lambda e, u=u, ca=ca, blk=blk, j=j: e.scalar_tensor_tensor(ca[:], u[:, j:j + 512], cw[:, blk, j:j + 1], ca[:], ALU.mult, ALU.add),
                                         reads=[bu, Bcw, bca], writes=[bca])
                                def fin_silu(ca=ca, bca=bca, blk=blk, tt=tt):
                                    oi = cnt["ob"] % 3
                                    cnt["ob"] += 1
                                    S.op("act", lambda e, ca=ca, oi=oi: e.activation(ob[oi][:], ca[:], AF.Silu),
                                         reads=[bca], writes=[Bob[oi]])
                                    S.dma("sp", QKs[blk, :, T0 + tt * 512:T0 + (tt + 1) * 512], ob[oi][:], reads=[Bob[oi]])
                                while pend_silu:
                                    pend_silu.pop(0)()
                                pend_silu.append(fin_silu)
                    while pend_silu:
                        pend_silu.pop(0)()
                    for grp in range(16):
                        w, bw = load_w(A_V0 + grp * 256, 256)
                        for tb2 in range(8):
                            p, pb = next_ps()
                            for s2 in range(2):
                                tb = tb2 * 2 + s2
                                for kc in range(16):
                                    S.op("pe", lambda e, p=p, kc=kc, tb=tb, s2=s2, w=w: e.matmul(p[:, s2 * 256:(s2 + 1) * 256], hT[:, kc, tb * 128:(tb + 1) * 128],
                                                                                               w[:, kc, :], start=(kc == 0), stop=(kc == 15)),
                                         reads=bw + [BhT[tb]], writes=[pb])
                            oi = cnt["ob"] % 3
                            cnt["ob"] += 1
                            S.op("act", lambda e, p=p, oi=oi: e.copy(ob[oi][:], p[:]), reads=[pb], writes=[Bob[oi]])
                            S.dma("sp", Vs[T0 + tb2 * 256:T0 + (tb2 + 1) * 256, A_V0 - A_V0 + grp * 256:grp * 256 + 256].rearrange("(s p) c -> p s c", p=128),
                                  ob[oi][:].rearrange("p (s c) -> p s c", s=2), reads=[Bob[oi]])
                    for grp in range(16):
                        wo, bwo = load_w(A_O0 + grp * 256, 256)
                        wz, bwz = load_w(A_Z0 + grp * 256, 256)
                        for cblk in range(2):
                            eblk = grp * 2 + cblk
                            for tt in tts_full:
                                if half == 1:
                                    pump(2)
                                po, pbo = mm_feat(wo, bwo, cblk, tt)
                                pz, pbz = mm_feat(wz, bwz, cblk, tt)
                                i = cnt["so"] % 2
                                cnt["so"] += 1
                                S.op("act", lambda e, i=i, po=po: e.activation(so[i][:], po[:], AF.Sigmoid), reads=[pbo], writes=[Bso[i]])
                                S.op("act", lambda e, i=i, pz=pz: e.activation(sz[i][:], pz[:], AF.Silu), reads=[pbz], writes=[Bsz[i]])
                                oi = cnt["ob"] % 3
                                cnt["ob"] += 1
                                S.op("dve", lambda e, i=i, oi=oi, eblk=eblk: e.scalar_tensor_tensor(ob[oi][:], sz[i][:], gh[:, eblk:eblk + 1], so[i][:], ALU.mult, ALU.mult),
                                     reads=[Bso[i], Bsz[i], Bcw], writes=[Bob[oi]])
                                t0 = T0 + tt * 512 - HALO0
                                S.dma("sp", GZs[eblk, :, t0:t0 + 512], ob[oi][:], reads=[Bob[oi]])
                    if half == 1:
                        pump(1000)
                    w, bw = load_w(A_G0, 16)
                    for tt in tts_all:
                        p, pb = mm_feat(w, bw, 0, tt, M=16)
                        i = cnt["gt"] % 2
                        cnt["gt"] += 1
                        S.op("act", lambda e, i=i, p=p: e.copy(gt[i][:], p[0:16, :]), reads=[pb], writes=[Bgt[i]])
                        S.dma("sp", GATES[:, T0 + tt * 512:T0 + (tt + 1) * 512], gt[i][:], reads=[Bgt[i]])
            S.barrier()


        def phaseC():
            NBLK = WIN // 128
            FB0 = HALO0 // 128
            with ExitStack() as pc:
                TT = sb(pc, "TT", [128, 3, NBLK, 8], F32)
                BTT = S.buf("TT")
                Rend = sb(pc, "Rend", [128, NBLK, 8], F32)
                Rprev = sb(pc, "Rprev", [128, NBLK, 8], F32)
                fa = sb(pc, "fa", [128, NBLK, 8], F32)
                fbeta = sb(pc, "fbeta", [128, NBLK, 8], F32)
                fb2 = sb(pc, "fb2", [128, NBLK, 8], F32)
                fdec = sb(pc, "fdec", [128, NBLK, 8], F32)
                fem = sb(pc, "fem", [128, NBLK, 8], F32)
                Bfac = S.buf("fac")
                with ExitStack() as pg:
                    gi = sb(pg, "gi", [8, WIN], F32)
                    gf = sb(pg, "gf", [8, WIN], F32)
                    one8 = sb(pg, "one8", [8, WIN], F32)
                    Bn = sb(pg, "Bn", [8, WIN], F32)
                    U = sb(pg, "U", [8, WIN], F32)
                    R = sb(pg, "R", [8, WIN], F32)
                    M = sb(pg, "M", [8, WIN], F32)
                    gb = sb(pg, "gb", [8, 4], F32)
                    Bg = S.buf("gates")
                    S.dma("sp", gi[:], GATES[0:8, :], writes=[Bg])
                    S.dma("sp", gf[:], GATES[8:16, :], writes=[Bg])
                    S.dma("sp", gb[:, 0:2], gateb_l, writes=[Bg])
                    S.op("pool", lambda e: e.memset(one8[:], 1.0), writes=[Bg])
                    S.op("dve", lambda e: e.tensor_scalar(gi[:], gi[:], gb[:, 0:1], None, ALU.add), reads=[Bg], writes=[Bg])
                    S.op("dve", lambda e: e.tensor_scalar(gi[:, 0:OWN0], gi[:, 0:OWN0], flg[0:8, 0:1], flg[0:8, 1:2], ALU.mult, ALU.add),
                         reads=[Bg, Bflg], writes=[Bg])
                    S.op("dve", lambda e: e.tensor_scalar(gb[:, 2:3], gb[:, 1:2], -1.0, None, ALU.mult), reads=[Bg], writes=[Bg])
                    S.op("act", lambda e: e.activation(gf[:], gf[:], AF.Exp, bias=gb[:, 2:3], scale=-1.0), reads=[Bg], writes=[Bg])
                    S.op("dve", lambda e: e.tensor_scalar(gf[:], gf[:], 1.0, None, ALU.add), reads=[Bg], writes=[Bg])
                    S.op("act", lambda e: e.activation(gf[:], gf[:], AF.Ln), reads=[Bg], writes=[Bg])
                    S.op("dve", lambda e: e.tensor_scalar(gf[:, 0:OWN0], gf[:, 0:OWN0], flg[0:8, 0:1], None, ALU.mult), reads=[Bg, Bflg], writes=[Bg])
                    S.op("dve", lambda e: e.tensor_tensor_scan(Bn[:], one8[:], gf[:], 0.0, ALU.mult, ALU.add), reads=[Bg], writes=[Bg])
                    S.op("dve", lambda e: e.tensor_tensor(U[:], gi[:], Bn[:], ALU.add), reads=[Bg], writes=[Bg])
                    S.op("dve", lambda e: e.tensor_tensor_scan(R[:], U[:], U[:], 0.0, ALU.max, ALU.max), reads=[Bg], writes=[Bg])
                    S.op("dve", lambda e: e.tensor_tensor(M[:], R[:], Bn[:], ALU.subtract), reads=[Bg], writes=[Bg])
                    for qi, src in enumerate((U, R, M)):
                        for half_ in range(2):
                            p, pb = next_ps()
                            for jj in range(16):
                                j = half_ * 16 + jj
                                S.op("pe", lambda e, p=p, jj=jj, j=j, src=src: e.transpose(p[:, jj * 8:(jj + 1) * 8], src[0:8, j * 128:(j + 1) * 128], ident_f[0:8, 0:8]),
                                     reads=[Bg, Bc], writes=[pb])
                            S.op("act", lambda e, p=p, qi=qi, half_=half_: e.copy(TT[:, qi, half_ * 16:(half_ + 1) * 16, :], p[:, 0:128].rearrange("p (j h) -> p j h", h=8)),
                                 reads=[pb], writes=[BTT])
                    p, pb = next_ps()
                    S.op("pe", lambda e, p=p: e.matmul(p[:, 0:256], cst[:, 3, :], TT[:, 1, :, :], start=True, stop=True),
                         reads=[BTT, Bc], writes=[pb])
                    S.op("act", lambda e, p=p: e.copy(Rend[:], p[:, 0:256].rearrange("p (j h) -> p j h", h=8)), reads=[pb], writes=[Bfac])
                    S.op("pool", lambda e: e.memset(Rprev[:, 0, :], 0.0), writes=[Bfac])
                    S.op("dve", lambda e: e.tensor_copy(Rprev[:, 1:NBLK, :], Rend[:, 0:NBLK - 1, :]), reads=[Bfac], writes=[Bfac])
                    S.op("dve", lambda e: e.tensor_tensor(fa[:], TT[:, 0, :, :], Rend[:], ALU.subtract), reads=[BTT, Bfac], writes=[Bfac])
                    S.op("act", lambda e: e.activation(fa[:], fa[:], AF.Exp), reads=[Bfac], writes=[Bfac])
                    S.op("dve", lambda e: e.tensor_tensor(fbeta[:], Rend[:], TT[:, 1, :, :], ALU.subtract), reads=[BTT, Bfac], writes=[Bfac])
                    S.op("dve", lambda e: e.tensor_scalar(fbeta[:], fbeta[:], -LN16, None, ALU.add), reads=[Bfac], writes=[Bfac])
                    S.op("act", lambda e: e.activation(fbeta[:], fbeta[:], AF.Exp), reads=[Bfac], writes=[Bfac])
                    S.op("dve", lambda e: e.scalar_tensor_tensor(fb2[:], fbeta[:], 1.0 / 512.0, fbeta[:], ALU.mult, ALU.mult), reads=[Bfac], writes=[Bfac])
                    S.op("dve", lambda e: e.tensor_tensor(fdec[:], Rprev[:], Rend[:], ALU.subtract), reads=[Bfac], writes=[Bfac])
                    S.op("act", lambda e: e.activation(fdec[:], fdec[:], AF.Exp), reads=[Bfac], writes=[Bfac])
                    S.op("dve", lambda e: e.tensor_tensor(fem[:], TT[:, 1, :, :], TT[:, 2, :, :], ALU.subtract), reads=[BTT], writes=[Bfac])
                    S.op("dve", lambda e: e.tensor_tensor(fem[:], fem[:], Rend[:], ALU.subtract), reads=[Bfac], writes=[Bfac])
                    S.op("dve", lambda e: e.tensor_scalar(fem[:], fem[:], 2.0, 2.0 * LN16 + float(np.log(EPS)), ALU.mult, ALU.add), reads=[Bfac], writes=[Bfac])
                    S.op("act", lambda e: e.activation(fem[:], fem[:], AF.Exp), reads=[Bfac], writes=[Bfac])
                S.barrier()
                kf = [sb(pc, f"kf{i}", [128, 16, 128], BF16) for i in range(2)]
                qf = [sb(pc, f"qf{i}", [128, 16, 128], BF16) for i in range(2)]
                vt = [sb(pc, f"vt{i}", [128, 4096], BF16) for i in range(2)]
                gz = [sb(pc, f"gz{i}", [128, 32, 128], BF16) for i in range(2)]
                yt = [sb(pc, f"yt{i}", [128, 32, 128], BF16) for i in range(2)]
                Bkf = S.bufs(2, "kf"); Bqf = S.bufs(2, "qf"); Bvt = S.bufs(2, "vt"); Bgz = S.bufs(2, "gz"); Byt = S.bufs(2, "yt")
                Cst = sb(pc, "Cst", [128, 8, 2, 512], F32)
                BC = S.bufs(8, "Cst")
                Nst = sb(pc, "Nst", [128, 8, 2], F32)
                BN = S.buf("Nst")
                Cb = [sb(pc, f"Cb{i}", [128, 2, 512], BF16) for i in range(2)]
                BCb = S.bufs(2, "Cb")
                Cbn = sb(pc, "Cbn", [128, 8, 2], BF16)
                BCbn = S.buf("Cbn")
                kT = [sb(pc, f"kT{i}", [128, 256], BF16) for i in range(3)]
                BkT = S.bufs(3, "kT")
                vp = [sb(pc, f"vp{i}", [128, 513], BF16) for i in range(3)]
                Bvp = S.bufs(3, "vp")
                sd = [sb(pc, f"sd{i}", [128, 128], BF16) for i in range(2)]
                Bsd = S.bufs(2, "sd")
                hn = [sb(pc, f"hn{i}", [128, 512], BF16) for i in range(2)]
                Bhn = S.bufs(2, "hn")
                junk = sb(pc, "junkC", [128, 512], BF16)
                Bjunk = S.buf("junkC")
                sm = sb(pc, "smC", [128, 8, 8], F32)
                Bsm = S.bufs(8, "smC")
                tri_f = cst[:, 2, :]
                S.op("pool", lambda e: e.memset(Cst[:], 0.0), writes=BC)
                S.op("pool", lambda e: e.memset(Nst[:], 0.0), writes=[BN])
                pKT, bKT = PS[0], PB[0]
                pS, bS = PS[1], PB[1]
                pACC = [PS[2], PS[3]]; bACC = [PB[2], PB[3]]
                pSM = PS[4]
                bDEN = [PB[4]] * 8; bPN = [PB[4]] * 8
                pHT, bHT = PS[5], PB[5]
                pC = [PS[6], PS[7]]; bC = [PB[6], PB[7]]
                def loadsC(jb):
                    i2 = jb % 2
                    t0 = jb * 128
                    S.dma("sp", kf[i2][:], QKs[16:32, :, t0:t0 + 128].rearrange("b p t -> p b t"), writes=[Bkf[i2]])
                    S.dma("sp", vt[i2][:], Vs[t0:t0 + 128, :], writes=[Bvt[i2]])
                    if jb >= FB0:
                        S.dma("sp", qf[i2][:], QKs[0:16, :, t0:t0 + 128].rearrange("b p t -> p b t"), writes=[Bqf[i2]])
                        S.dma("sp", gz[i2][:], GZs[:, :, t0 - HALO0:t0 - HALO0 + 128].rearrange("b p t -> p b t"), writes=[Bgz[i2]])

                loadsC(0)
                for jb in range(NBLK):
                    full = jb >= FB0
                    i2 = jb % 2
                    t0 = jb * 128
                    if jb + 1 < NBLK:
                        loadsC(jb + 1)
                    if full:
                        S.op("dve", lambda e, jb=jb: e.tensor_tensor(Cbn[:], Nst[:], fdec[:, jb, :].unsqueeze(2).to_broadcast([128, 8, 2]), ALU.mult),
                             reads=[BN, Bfac], writes=[BCbn])
                    def A1(h, jb=jb, i2=i2, full=full):
                        hi = (jb * 8 + h) % 2
                        hk = (jb * 8 + h) % 3
                        a_ap = fa[:, jb, h:h + 1]
                        dec_ap = fdec[:, jb, h:h + 1]
                        pv = pKT[:].bitcast(BF16)
                        for c in range(2):
                            S.op("pe", lambda e, pv=pv, c=c, h=h: e.transpose(pv[:, c * 128:(c + 1) * 128], kf[i2][:, h * 2 + c, :], ident_b),
                                 reads=[Bkf[i2], Bcb], writes=[bKT])
                        if full:
                            for c in range(2):
                                S.op("pe", lambda e, c=c, h=h: e.matmul(pS[:, 0:128], kf[i2][:, h * 2 + c, :], qf[i2][:, h * 2 + c, :], start=(c == 0), stop=(c == 1)),
                                     reads=[Bkf[i2], Bqf[i2]], writes=[bS])
                        S.op("act", lambda e, pv=pv, hk=hk: e.copy(kT[hk][:], pv[:, 0:256]), reads=[bKT], writes=[BkT[hk]])
                        S.op("dve", lambda e, hk=hk, h=h, a_ap=a_ap: e.tensor_scalar(vp[hk][:, 0:512], vt[i2][:, h * 512:(h + 1) * 512], a_ap, None, ALU.mult),
                             reads=[Bvt[i2], Bfac], writes=[Bvp[hk]])
                        S.op("dve", lambda e, hk=hk, a_ap=a_ap: e.tensor_copy(vp[hk][:, 512:513], a_ap), reads=[Bfac, Bvp[hk]], writes=[Bvp[hk]])
                        if full:
                            S.op("act", lambda e, hi=hi, h=h, dec_ap=dec_ap: e.activation(Cb[hi][:], Cst[:, h, :, :], AF.Copy, scale=dec_ap),
                                 reads=[BC[h], Bfac], writes=[BCb[hi]])
                            S.op("dve", lambda e, hi=hi: e.tensor_tensor(sd[hi][:], pS[:, 0:128], tri_f, ALU.mult), reads=[bS, Bc], writes=[Bsd[hi]])

                    def A2(h, jb=jb, i2=i2, full=full):
                        if not full:
                            return
                        hi = (jb * 8 + h) % 2
                        hk = (jb * 8 + h) % 3
                        pa = pACC[hi]; ba = bACC[hi]
                        for c in range(2):
                            S.op("pe", lambda e, pa=pa, c=c, h=h, hi=hi: e.matmul(pa[:], qf[i2][:, h * 2 + c, :], Cb[hi][:, c, :], start=(c == 0), stop=False),
                                 reads=[Bqf[i2], BCb[hi]], writes=[ba])
                        S.op("pe", lambda e, pa=pa, hi=hi, hk=hk: e.matmul(pa[:], sd[hi][:], vp[hk][:, 0:512], start=False, stop=True),
                             reads=[Bsd[hi], Bvp[hk]], writes=[ba])
                        for c in range(2):
                            S.op("pe", lambda e, c=c, h=h: e.matmul(pSM[:, h:h + 1], qf[i2][:, h * 2 + c, :], Cbn[:, h, c:c + 1], start=(c == 0), stop=False),
                                 reads=[Bqf[i2], BCbn], writes=[bDEN[h]])
                        S.op("pe", lambda e, hi=hi, h=h, hk=hk: e.matmul(pSM[:, h:h + 1], sd[hi][:], vp[hk][:, 512:513], start=False, stop=True),
                             reads=[Bsd[hi], Bvp[hk]], writes=[bDEN[h]])

                    def B1(h, jb=jb, i2=i2, full=full):
                        if not full:
                            return
                        hi = (jb * 8 + h) % 2
                        pa = pACC[hi]; ba = bACC[hi]
                        smh = sm[:, h, :]
                        S.op("act", lambda e, pa=pa, smh=smh: e.activation(junk[:], pa[:], AF.Square, scale=512.0 ** -0.5, accum_out=smh[:, 0:1]),
                             reads=[ba], writes=[Bjunk, Bsm[h]])
                        S.op("act", lambda e, smh=smh, h=h: e.activation(smh[:, 1:2], pSM[:, h:h + 1], AF.Square, scale=EPS ** 0.5),
                             reads=[bDEN[h], Bsm[h]], writes=[Bsm[h]])
                        S.op("dve", lambda e, smh=smh, h=h, jb=jb: e.scalar_tensor_tensor(smh[:, 3:4], smh[:, 1:2], fem[:, jb, h:h + 1], smh[:, 0:1], ALU.max, ALU.add),
                             reads=[Bfac, Bsm[h]], writes=[Bsm[h]])
                        S.op("act", lambda e, smh=smh: e.sqrt(smh[:, 3:4], smh[:, 3:4]), reads=[Bsm[h]], writes=[Bsm[h]])
                        S.op("dve", lambda e, smh=smh: e.reciprocal(smh[:, 4:5], smh[:, 3:4]), reads=[Bsm[h]], writes=[Bsm[h]])
                        S.op("act", lambda e, pa=pa, hi=hi, smh=smh: e.activation(hn[hi][:], pa[:], AF.Copy, scale=smh[:, 4:5]),
                             reads=[ba, Bsm[h]], writes=[Bhn[hi]])

                    def B2(h, jb=jb, i2=i2, full=full):
                        hi = (jb * 8 + h) % 2
                        hk = (jb * 8 + h) % 3
                        dec_ap = fdec[:, jb, h:h + 1]
                        if full:
                            ph = pHT[:].bitcast(BF16)
                            for ec in range(4):
                                S.op("pe", lambda e, ph=ph, ec=ec, hi=hi: e.transpose(ph[:, ec * 128:(ec + 1) * 128], hn[hi][:, ec * 128:(ec + 1) * 128], ident_b),
                                     reads=[Bhn[hi], Bcb], writes=[bHT])
                            S.op("dve", lambda e, ph=ph, h=h: e.tensor_tensor(yt[i2][:, h * 4:(h + 1) * 4, :], ph[:, 0:512].rearrange("p (a t) -> p a t", a=4),
                                                                              gz[i2][:, h * 4:(h + 1) * 4, :], ALU.mult),
                                 reads=[bHT, Bgz[i2]], writes=[Byt[i2]])
                        for c in range(2):
                            S.op("pe", lambda e, c=c, hk=hk: e.matmul(pC[c][:], kT[hk][:, c * 128:(c + 1) * 128], vp[hk][:, 0:512], start=True, stop=True),
                                 reads=[BkT[hk], Bvp[hk]], writes=[bC[c]])
                            S.op("pe", lambda e, c=c, hk=hk, h=h: e.matmul(pSM[:, 8 + h * 2 + c:8 + h * 2 + c + 1], kT[hk][:, c * 128:(c + 1) * 128], vp[hk][:, 512:513], start=True, stop=True),
                                 reads=[BkT[hk], Bvp[hk]], writes=[bPN[h]])
                            S.op("dve", lambda e, c=c, h=h, dec_ap=dec_ap: e.scalar_tensor_tensor(Cst[:, h, c, :], Cst[:, h, c, :], dec_ap, pC[c][:], ALU.mult, ALU.add),
                                 reads=[BC[h], Bfac, bC[c]], writes=[BC[h]])

                    A1(0)
                    A2(0)
                    for h in range(8):
                        if h + 1 < 8:
                            A1(h + 1)
                        B1(h)
                        if h + 1 < 8:
                            A2(h + 1)
                        if h >= 1:
                            B2(h - 1)
                    B2(7)
                    S.op("dve", lambda e, jb=jb: e.tensor_tensor(Nst[:], Nst[:], fdec[:, jb, :].unsqueeze(2).to_broadcast([128, 8, 2]), ALU.mult),
                         reads=[BN, Bfac, BCbn], writes=[BN])
                    S.op("dve", lambda e: e.tensor_tensor(Nst[:], Nst[:], pSM[:, 8:24].rearrange("p (h c) -> p h c", c=2), ALU.add),
                         reads=[BN] + bPN, writes=[BN])
                    if full:
                        S.dma("sp", YT[:, :, t0 - HALO0:t0 - HALO0 + 128].rearrange("b p t -> p b t"), yt[i2][:], reads=[Byt[i2]])
            S.barrier()


        def phase_out(layer):
            if layer == 0:
                NE, w_ap, YTsrc, nblk = 32, a_w_out, YT, NH1 // 128
            else:
                NE, w_ap, YTsrc, nblk = 16, b_w_out, YT1, 16
            with ExitStack() as pc:
                wob = sb(pc, "wob", [128, NE, 2048], BF16)
                Bwob = S.bufs(NE, "wob")
                GP = sb(pc, "GP", [128, 2048], F32)
                BGP = S.buf("GP")
                fv = sb(pc, "fv", [128, 6, 16], F32)
                Bfv = S.buf("fv")
                gl = sb(pc, "gl", [128, 3, 16], F32)
                with ExitStack() as pw:
                    wst = [sb(pw, f"wstO{i}", [128, 2048], F32) for i in range(2)]
                    Bwst = S.bufs(2, "wstO")
                    for ec in range(NE):
                        i = ec % 2
                        S.dma("sp", wst[i][:], w_ap[ec * 128:(ec + 1) * 128, :], writes=[Bwst[i]])
                        if ec % 2 == 0:
                            S.op("dve", lambda e, i=i, ec=ec: e.tensor_copy(wob[:, ec, :], wst[i][:]), reads=[Bwst[i]], writes=[Bwob[ec]])
                        else:
                            S.op("act", lambda e, i=i, ec=ec: e.copy(wob[:, ec, :], wst[i][:]), reads=[Bwst[i]], writes=[Bwob[ec]])
                    tl = wst[0]
                    btl = Bwst[0]
                    gate_row = 2 if layer == 0 else 5
                    S.dma("sp", GP[:], VECS[gate_row:gate_row + 1, :].partition_broadcast(128), writes=[BGP])
                    S.dma("sp", tl[:], g_post[layer:layer + 1, :].partition_broadcast(128), writes=[btl])
                    S.op("dve", lambda e: e.tensor_tensor(GP[:], GP[:], tl[:], ALU.mult), reads=[btl, BGP], writes=[BGP])
                    if layer == 0:
                        S.dma("sp", gl[:], gvec_l, writes=[Bfv])
                        for slot, row in ((4, 7), (1, 6), (5, 4), (3, 3)):
                            S.dma("sp", fv[:, slot, :], VECS[row, :].rearrange("(k p) -> p k", p=128), writes=[Bfv], slow=True)
                        S.op("dve", lambda e: e.scalar_tensor_tensor(fv[:, 0, :], fv[:, 4, :], 1.0, gl[:, 1, :], ALU.add, ALU.mult), reads=[Bfv], writes=[Bfv])
                        S.op("dve", lambda e: e.scalar_tensor_tensor(fv[:, 2, :], fv[:, 5, :], 1.0, gl[:, 0, :], ALU.add, ALU.mult), reads=[Bfv], writes=[Bfv])
                    S.barrier()
                ytl = [sb(pc, f"ytl{i}", [128, NE, 128], BF16) for i in range(2)]
                Bytl = S.bufs(2, "ytl")
                xr = [sb(pc, f"xr{i}", [128, 2048], F32) for i in range(2)]
                Bxr = S.bufs(2, "xr")
                x1 = [sb(pc, f"x1{i}", [128, 2048], F32) for i in range(2)]
                Bx1 = S.bufs(2, "x1")
                Bx1q = [S.bufs(4, "x1q0"), S.bufs(4, "x1q1")]
                junk = sb(pc, "junkO", [128, 512], BF16)
                Bjunk = S.buf("junkO")
                ssq = sb(pc, "ssqO", [128, 8], F32)
                Bssq = S.buf("ssqO")
                pso = [PS[0], PS[1], PS[2], PS[3]]
                bso = [PB[0], PB[1], PB[2], PB[3]]
                if layer == 0:
                    xnb = [sb(pc, f"xnb{i}", [128, 2048], BF16) for i in range(2)]
                    Bxnb = S.bufs(2, "xnb")
                    hTt = [sb(pc, f"hTt{i}", [128, 16, 128], BF16) for i in range(2)]
                    BhTt = S.bufs(2, "hTt")
                    BhTt2 = S.bufs(2, "hTt2")
                ptr = [PS[4], PS[5], PS[6], PS[7]]
                btr = [PB[4], PB[5], PB[6], PB[7]]
                cnt = {"t": 0, "h": 0}

                def rstd_chain(col):
                    S.op("dve", lambda e: e.tensor_scalar(col, col, 1.0 / D, EPS, ALU.mult, ALU.add), reads=[Bssq], writes=[Bssq])
                    S.op("act", lambda e: e.sqrt(col, col), reads=[Bssq], writes=[Bssq])
                    S.op("dve", lambda e: e.reciprocal(col, col), reads=[Bssq], writes=[Bssq])

                def loadsO(jb):
                    i2 = jb % 2
                    t0 = jb * 128
                    S.dma("sp", ytl[i2][:], YTsrc[:, :, t0:t0 + 128].rearrange("b p t -> p b t"), writes=[Bytl[i2]])
                    if layer == 0:
                        S.dma("sp", xr[i2][:], xw[HALO0 + t0:HALO0 + t0 + 128, :], writes=[Bxr[i2]])
                    else:
                        S.dma("sp", xr[i2][:], X1[512 + t0:512 + t0 + 128, :], writes=[Bxr[i2]])

                deferred = []
                loadsO(0)
                for jb in range(nblk):
                    i2 = jb % 2
                    t0 = jb * 128
                    if jb + 1 < nblk:
                        loadsO(jb + 1)
                    order = [(q, ec) for ec in range(NE) for q in range(4)] if jb == 0 else [(q, ec) for q in range(4) for ec in range(NE)]
                    for q, ec in order:
                        S.op("pe", lambda e, q=q, ec=ec, i2=i2: e.matmul(pso[q][:], ytl[i2][:, ec, :], wob[:, ec, q * 512:(q + 1) * 512], start=(ec == 0), stop=(ec == NE - 1)),
                             reads=[Bytl[i2], Bwob[ec]], writes=[bso[q]])
                    for q in range(4):
                        S.op("act", lambda e, q=q, i2=i2: e.copy(x1[i2][:, q * 512:(q + 1) * 512], pso[q][:]),
                             reads=[bso[q]], writes=[Bx1q[i2][q]] + ([Bx1[i2]] if q == 0 else []))
                        S.op("dve", lambda e, q=q, i2=i2: e.scalar_tensor_tensor(junk[:], x1[i2][:, q * 512:(q + 1) * 512], 1.0, x1[i2][:, q * 512:(q + 1) * 512],
                                                                               ALU.mult, ALU.mult, accum_out=ssq[:, q:q + 1]),
                             reads=[Bx1q[i2][q]], writes=[Bjunk, Bssq])
                    S.op("dve", lambda e: e.tensor_reduce(ssq[:, 4:5], ssq[:, 0:4], AX.X, ALU.add), reads=[Bssq], writes=[Bssq])
                    rstd_chain(ssq[:, 4:5])
                    S.op("dve", lambda e, i2=i2: e.scalar_tensor_tensor(x1[i2][:], x1[i2][:], ssq[:, 4:5], GP[:], ALU.mult, ALU.mult),
                         reads=Bx1q[i2] + [Bssq, BGP, Bx1[i2]], writes=[Bx1[i2]] + Bx1q[i2])
                    S.op("dve", lambda e, i2=i2: e.tensor_tensor(x1[i2][:], x1[i2][:], xr[i2][:], ALU.add), reads=[Bxr[i2], Bx1[i2]], writes=[Bx1[i2]])
                    if layer == 0:
                        S.dma("sp", X1[t0:t0 + 128, :], x1[i2][:], reads=[Bx1[i2]])
                        S.op("act", lambda e, i2=i2: e.activation(xnb[i2][:], x1[i2][:], AF.Square, accum_out=ssq[:, 5:6]),
                             reads=[Bx1[i2]], writes=[Bxnb[i2], Bssq])
                        rstd_chain(ssq[:, 5:6])
                        S.op("dve", lambda e, i2=i2: e.tensor_scalar(xnb[i2][:], x1[i2][:], ssq[:, 5:6], None, ALU.mult),
                             reads=[Bx1[i2], Bssq], writes=[Bxnb[i2]])
                        def E2(jb=jb, i2=i2, t0=t0):
                            pvs = []
                            for hh in range(2):
                                p, pb = ptr[i2 * 2 + hh], btr[i2 * 2 + hh]
                                pv = p[:].bitcast(BF16)
                                for j in range(8):
                                    kc = hh * 8 + j
                                    S.op("pe", lambda e, pv=pv, j=j, kc=kc: e.transpose(pv[:, j * 128:(j + 1) * 128], xnb[i2][:, kc * 128:(kc + 1) * 128], ident_b),
                                         reads=[Bxnb[i2], Bcb], writes=[pb])
                                pvs.append((pv, pb))
                            sets = [(0, 1, HKVT[:, :, t0:t0 + 128])]
                            if jb >= 4:
                                sets.append((2, 3, H1T[:, :, t0 - 512:t0 - 512 + 128]))
                            for gs, ss_, dst in sets:
                                hi = cnt["h"] % 2
                                cnt["h"] += 1
                                for hh in range(2):
                                    pv, pb = pvs[hh]
                                    for j in range(8):
                                        kc = hh * 8 + j
                                        S.op("act", lambda e, pv=pv, j=j, kc=kc, gs=gs, ss_=ss_, hi=hi: e.activation(hTt[hi][:, kc, :], pv[:, j * 128:(j + 1) * 128], AF.Identity,
                                                                                                               bias=fv[:, ss_, kc:kc + 1], scale=fv[:, gs, kc:kc + 1]),
                                             reads=[pb, Bfv], writes=[BhTt[hi]])
                                S.dma("sp", dst.rearrange("k p t -> p k t"), hTt[hi][:], reads=[BhTt[hi], BhTt2[hi]], writes=[])
                        while deferred:
                            deferred.pop(0)()
                        deferred.append(E2)
                    else:
                        S.dma("sp", y_out[t0:t0 + 128, :], x1[i2][:], reads=[Bx1[i2]])
                while deferred:
                    deferred.pop(0)()
            S.barrier()

        def phase_proj1(src, ntok, w_ap, feat_dst, feat_silu, tok_dst):
            with ExitStack() as pc:
                aT = sb(pc, "aT", [128, 16, ntok], BF16)
                BaTk = S.bufs(16, "aTk")
                for kc in range(16):
                    S.dma("sp", aT[:, kc, :], src[kc, :, :], writes=[BaTk[kc]])
                wst = [sb(pc, f"wstE{i}", [128, 16, 256], F32) for i in range(3)]
                Bwst = S.bufs(3, "wstE")
                wbf = [sb(pc, f"wbfE{i}", [128, 16, 256], BF16) for i in range(3)]
                Bwbf = S.bufs(3, "wbfE")
                ob = [sb(pc, f"obE{i}", [128, 512], BF16) for i in range(3)]
                Bob = S.bufs(3, "obE")
                cnt = {"w": 0, "ob": 0, "iss": 0}
                w_view = w_ap.rearrange("(kc p) c -> p kc c", p=128)
                ntt = ntok // 512
                wseq = [g * 256 for g in range(16)]

                cnt["cv"] = 0
                Bwbf2 = S.bufs(3, "wbfE2")

                def issue_load():
                    k = cnt["iss"]
                    cnt["iss"] += 1
                    c0 = wseq[k]
                    i = k % 3
                    S.dma("sp", wst[i][:], w_view[:, :, c0:c0 + 256], writes=[Bwst[i]])

                def issue_cv():
                    k = cnt["cv"]
                    cnt["cv"] += 1
                    i = k % 3
                    S.op("pool", lambda e, i=i: e.tensor_copy(wbf[i][:, 0:8, :], wst[i][:, 0:8, :]), reads=[Bwst[i]], writes=[Bwbf[i]])
                    S.op("act", lambda e, i=i: e.copy(wbf[i][:, 8:16, :], wst[i][:, 8:16, :]), reads=[Bwst[i]], writes=[Bwbf2[i]])

                def load_w(c0):
                    k = cnt["w"]
                    assert wseq[k] == c0
                    cnt["w"] += 1
                    while cnt["iss"] <= k + 2 and cnt["iss"] < len(wseq):
                        issue_load()
                    while cnt["cv"] <= k + 1 and cnt["cv"] < len(wseq):
                        issue_cv()
                    i = k % 3
                    return wbf[i], [Bwbf[i], Bwbf2[i]]

                def feat_group(c0, dst, silu):
                    w, bw = load_w(c0)
                    for cblk in range(2):
                        blk = (c0 % 2048) // 128 + cblk
                        for tt in range(ntt):
                            p, pb = next_ps()
                            for kc in range(16):
                                S.op("pe", lambda e, p=p, kc=kc, w=w, cblk=cblk, tt=tt: e.matmul(p[:], w[:, kc, cblk * 128:(cblk + 1) * 128], aT[:, kc, tt * 512:(tt + 1) * 512],
                                                                                             start=(kc == 0), stop=(kc == 15)),
                                     reads=bw + [BaTk[kc]], writes=[pb])
                            oi = cnt["ob"] % 3
                            cnt["ob"] += 1
                            if silu:
                                S.op("act", lambda e, p=p, oi=oi: e.activation(ob[oi][:], p[:], AF.Silu), reads=[pb], writes=[Bob[oi]])
                            else:
                                S.op("act", lambda e, p=p, oi=oi: e.copy(ob[oi][:], p[:]), reads=[pb], writes=[Bob[oi]])
                            S.dma("sp", dst[blk, :, tt * 512:(tt + 1) * 512], ob[oi][:], reads=[Bob[oi]])

                for grp in range(8):
                    feat_group(grp * 256, feat_dst, False)
                for grp in range(8):
                    if tok_dst is None:
                        feat_group(2048 + grp * 256, feat_silu, True)
                    else:
                        w, bw = load_w(2048 + grp * 256)
                        for tb2 in range(ntok // 256):
                            p, pb = next_ps()
                            for s2 in range(2):
                                tb = tb2 * 2 + s2
                                for kc in range(16):
                                    S.op("pe", lambda e, p=p, kc=kc, tb=tb, s2=s2, w=w: e.matmul(p[:, s2 * 256:(s2 + 1) * 256], aT[:, kc, tb * 128:(tb + 1) * 128],
                                                                                               w[:, kc, :], start=(kc == 0), stop=(kc == 15)),
                                         reads=bw + [BaTk[kc]], writes=[pb])
                            oi = cnt["ob"] % 3
                            cnt["ob"] += 1
                            S.op("act", lambda e, p=p, oi=oi: e.copy(ob[oi][:], p[:]), reads=[pb], writes=[Bob[oi]])
                            S.dma("sp", tok_dst[tb2 * 256:(tb2 + 1) * 256, grp * 256:(grp + 1) * 256].rearrange("(s p) c -> p s c", p=128),
                                  ob[oi][:].rearrange("p (s c) -> p s c", s=2), reads=[Bob[oi]])
            S.barrier()

        def phaseF():
            SCALE = 128.0 ** -0.5
            with ExitStack() as pc:
                bm = sb(pc, "bm", [128, 16, 640], F32)
                Bbm = S.buf("bm")
                am = sb(pc, "am", [128, 640], F32)
                S.dma("sp", bm[:], bias_l.rearrange("h p j -> p h j"), writes=[Bbm])
                S.dma("sp", am[:], amask, writes=[Bbm])
                S.op("dve", lambda e: e.tensor_tensor(bm[:], bm[:], am[:].unsqueeze(1).to_broadcast([128, 16, 640]), ALU.add), reads=[Bbm], writes=[Bbm])
                qT = [sb(pc, f"qT{i}", [128, 16, 128], BF16) for i in range(2)]
                Kt = [sb(pc, f"Kt{i}", [128, 16, 640], BF16) for i in range(2)]
                Vt = [sb(pc, f"Vt{i}", [128, 5, 2048], BF16) for i in range(2)]
                zt = [sb(pc, f"zt{i}", [128, 16, 128], BF16) for i in range(2)]
                yt = [sb(pc, f"ytF{i}", [128, 16, 128], BF16) for i in range(2)]
                BqT = S.bufs(2, "qT"); BKt = S.bufs(2, "Kt"); BVt = S.bufs(2, "Vt"); Bzt = S.bufs(2, "zt"); Byt = S.bufs(2, "ytF")
                st = [sb(pc, f"st{i}", [128, 640], F32) for i in range(3)]
                Bst = S.bufs(3, "st")
                pb16 = [sb(pc, f"pb16{i}", [128, 640], BF16) for i in range(3)]
                Bpb = S.bufs(3, "pb16")
                pT = [sb(pc, f"pT{i}", [128, 640], BF16) for i in range(2)]
                BpT = S.bufs(2, "pT")
                on = [sb(pc, f"on{i}", [128, 4, 128], BF16) for i in range(2)]
                Bon = S.bufs(2, "on")
                sm = sb(pc, "smF", [128, 3, 16], F32)
                Bsm = S.bufs(16, "smF")
                Brinv = S.bufs(4, "rinv")
                pA = [PS[0], PS[1], PS[2]]; bA = [PB[0], PB[1], PB[2]]
                pBk = [PS[3][:, k * 128:(k + 1) * 128] for k in range(3)]; bBk = [PB[3]] * 3
                pTr, bTr = PS[4], PB[4]
                pO = [PS[5], PS[6]]; bO = [PB[5], PB[6]]
                pOT, bOT = PS[7], PB[7]
                def loadsF(qb):
                    i2 = qb % 2
                    t0 = qb * 128
                    S.dma("sp", qT[i2][:], QT[:, :, t0:t0 + 128].rearrange("h p t -> p h t"), writes=[BqT[i2]])
                    S.dma("sp", Kt[i2][:], KT[:, :, t0:t0 + 640].rearrange("h p t -> p h t"), writes=[BKt[i2]])
                    S.dma("sp", Vt[i2][:], V1[t0:t0 + 640, :].rearrange("(kb p) c -> p kb c", p=128), writes=[BVt[i2]])
                    S.dma("sp", zt[i2][:], Z1[:, :, t0:t0 + 128].rearrange("h p t -> p h t"), writes=[Bzt[i2]])

                loadsF(0)
                for qb in range(16):
                    i2 = qb % 2
                    t0 = qb * 128
                    if qb + 1 < 16:
                        loadsF(qb + 1)
                    def FA(h, qb=qb, i2=i2):
                        hi = h % 3
                        S.op("pe", lambda e, hi=hi, h=h: e.matmul(pA[hi][:], qT[i2][:, h, :], Kt[i2][:, h, 0:512], start=True, stop=True),
                             reads=[BqT[i2], BKt[i2]], writes=[bA[hi]])
                        S.op("pe", lambda e, hi=hi, h=h: e.matmul(pBk[hi], qT[i2][:, h, :], Kt[i2][:, h, 512:640], start=True, stop=True),
                             reads=[BqT[i2], BKt[i2]], writes=[bBk[hi]])
                        S.op("dve", lambda e, hi=hi, h=h: e.scalar_tensor_tensor(st[hi][:, 0:512], pA[hi][:], SCALE, bm[:, h, 0:512], ALU.mult, ALU.add),
                             reads=[bA[hi], Bbm], writes=[Bst[hi]])
                        S.op("dve", lambda e, hi=hi, h=h: e.scalar_tensor_tensor(st[hi][:, 512:640], pBk[hi], SCALE, bm[:, h, 512:640], ALU.mult, ALU.add),
                             reads=[bBk[hi], Bbm, Bst[hi]], writes=[Bst[hi]])
                        if qb < 4:
                            j0 = 512 - qb * 128
                            S.op("dve", lambda e, hi=hi, j0=j0: e.tensor_scalar(st[hi][:, 0:j0], st[hi][:, 0:j0], flg[:, 1:2], None, ALU.add),
                                 reads=[Bst[hi], Bflg], writes=[Bst[hi]])
                        S.op("dve", lambda e, hi=hi, h=h: e.tensor_reduce(sm[:, 0, h:h + 1], st[hi][:], AX.X, ALU.max, negate=True),
                             reads=[Bst[hi]], writes=[Bsm[h]])
                        S.op("act", lambda e, hi=hi, h=h: e.activation(pb16[hi][:], st[hi][:], AF.Exp, bias=sm[:, 0, h:h + 1], accum_out=sm[:, 1, h:h + 1]),
                             reads=[Bst[hi], Bsm[h]], writes=[Bpb[hi], Bsm[h]])

                    def FB(h, qb=qb, i2=i2):
                        hi = h % 2
                        h3 = h % 3
                        pv = pTr[:].bitcast(BF16)
                        for kb in range(5):
                            S.op("pe", lambda e, pv=pv, kb=kb, h3=h3: e.transpose(pv[:, kb * 128:(kb + 1) * 128], pb16[h3][:, kb * 128:(kb + 1) * 128], ident_b),
                                 reads=[Bpb[h3], Bcb], writes=[bTr])
                        S.op("act", lambda e, pv=pv, hi=hi: e.copy(pT[hi][:], pv[:, 0:640]), reads=[bTr], writes=[BpT[hi]])
                        if h + 2 < 16:
                            FA(h + 2)
                        quad = h // 4
                        oq = quad % 2
                        for kb in range(5):
                            S.op("pe", lambda e, kb=kb, hi=hi, h=h, oq=oq: e.matmul(pO[oq][:, (h % 4) * 128:(h % 4 + 1) * 128], pT[hi][:, kb * 128:(kb + 1) * 128],
                                                                                   Vt[i2][:, kb, h * 128:(h + 1) * 128], start=(kb == 0), stop=(kb == 4)),
                                 reads=[BpT[hi], BVt[i2]], writes=[bO[oq]])
                        if h % 4 == 3:
                            h0 = quad * 4
                            S.op("dve", lambda e, h0=h0: e.reciprocal(sm[:, 2, h0:h0 + 4], sm[:, 1, h0:h0 + 4]),
                                 reads=Bsm[h0:h0 + 4], writes=[Brinv[quad]])
                            S.op("dve", lambda e, h0=h0, oq=oq: e.tensor_tensor(on[oq][:], pO[oq][:].rearrange("p (a d) -> p a d", a=4),
                                                                                sm[:, 2, h0:h0 + 4].unsqueeze(2).to_broadcast([128, 4, 128]), ALU.mult),
                                 reads=[bO[oq], Brinv[quad]], writes=[Bon[oq]])
                            pv2 = pOT[:].bitcast(BF16)
                            for a in range(4):
                                S.op("pe", lambda e, pv2=pv2, a=a, oq=oq: e.transpose(pv2[:, a * 128:(a + 1) * 128], on[oq][:, a, :], ident_b),
                                     reads=[Bon[oq], Bcb], writes=[bOT])
                            S.op("dve", lambda e, pv2=pv2, h0=h0: e.tensor_tensor(yt[i2][:, h0:h0 + 4, :], pv2[:, 0:512].rearrange("p (a t) -> p a t", a=4),
                                                                                 zt[i2][:, h0:h0 + 4, :], ALU.mult),
                                 reads=[bOT, Bzt[i2]], writes=[Byt[i2]])

                    FA(0)
                    FA(1)
                    for h in range(16):
                        FB(h)
                    S.dma("sp", YT1[:, :, t0:t0 + 128].rearrange("h p t -> p h t"), yt[i2][:], reads=[Byt[i2]])
            S.barrier()

        carry = sb(ctx, "carry", [128, 32, 3], F32)
        Bcarry = S.bufs(32, "carry")

        phase0()
        if stop_after >= 1:
            phaseAB(0)
        if stop_after >= 3:
            phaseAB(1)
        if stop_after >= 4:
            phaseC()
        if stop_after >= 5:
            phase_out(0)
        if stop_after >= 6:
            phase_proj1(HKVT, NH1, kv_w, KT, None, V1)
            phase_proj1(H1T, 2048, b_w_in, QT, Z1, None)
        if stop_after >= 7:
            phaseF()
        if stop_after >= 8:
            phase_out(1)

        S.wait_all("sp", [t for t in S.dma_tok if t is not None])
        S.finalize()
    return nc


def make_consts():
    c = np.zeros((128, 4, 128), np.float32)
    c[:, 0, :] = np.eye(128, dtype=np.float32)
    c[:, 1, :] = 1.0
    c[:, 2, :] = np.triu(np.ones((128, 128), np.float32))
    c[127, 3, :] = 1.0
    return c


def make_amask():
    r = np.arange(128)[:, None]
    j = np.arange(640)[None, :]
    key = j - 512
    ch = r // 64
    lo = ch * 64 - 512
    hi = ch * 64 + 64
    ok = (key >= lo) & (key < hi)
    return np.where(ok, 0.0, -BIG).astype(np.float32)


def make_core_inputs(inputs, b, p):
    x = np.asarray(inputs["x"])
    if p == 1:
        xw = x[b]
    else:
        xw = np.concatenate([x[b, 2048:], x[b, :2048]], axis=0)
    f = 1.0 if p == 1 else 0.0
    flag = np.zeros((128, 2), np.float32)
    flag[:, 0] = f
    flag[:, 1] = (f - 1.0) * BIG
    c = np.asarray(inputs["c"])[b]
    cT = np.ascontiguousarray(c.reshape(16, 128).T)
    conv_w = np.asarray(inputs["a_conv_w"])[0]
    convw_l = np.ascontiguousarray(conv_w.reshape(4, 32, 128).transpose(2, 1, 0))
    convb_l = np.ascontiguousarray(np.asarray(inputs["a_conv_b"])[0].reshape(32, 128).T)
    gateb_l = np.ascontiguousarray(np.asarray(inputs["a_gate_b"])[0].reshape(2, 8).T)
    ghead_l = np.ascontiguousarray(np.asarray(inputs["a_g_head"])[0].reshape(32, 128).T)
    gvec_l = np.ascontiguousarray(np.stack([np.asarray(inputs["g_pre"])[1].reshape(16, 128).T,
                                            np.asarray(inputs["kv_g"]).reshape(16, 128).T,
                                            np.asarray(inputs["g_pre"])[0].reshape(16, 128).T], axis=1))
    rel = np.asarray(inputs["b_rel"])[0]
    r = np.arange(128)[:, None]
    j = np.arange(640)[None, :]
    bucket = np.clip(r + 512 - j, -128, 128) + 128
    bias_l = np.ascontiguousarray(rel[:, bucket])
    return {
        "xw": np.ascontiguousarray(xw), "cT": cT, "flag": flag, "consts": make_consts(),
        "ada_w": np.asarray(inputs["ada_w"]), "ada_b": np.asarray(inputs["ada_b"]),
        "g_pre": np.asarray(inputs["g_pre"]), "g_post": np.asarray(inputs["g_post"]),
        "a_w_in": np.asarray(inputs["a_w_in"])[0], "convw_l": convw_l, "convb_l": convb_l,
        "gateb_l": gateb_l, "ghead_l": ghead_l, "gvec_l": gvec_l, "a_w_out": np.asarray(inputs["a_w_out"])[0],
        "kv_ada_w": np.asarray(inputs["kv_ada_w"]), "kv_ada_b": np.asarray(inputs["kv_ada_b"]).reshape(1, -1),
        "kv_g": np.asarray(inputs["kv_g"]).reshape(1, -1), "kv_w": np.asarray(inputs["kv_w"]),
        "b_w_in": np.asarray(inputs["b_w_in"])[0], "bias_l": bias_l, "amask": make_amask(),
        "b_w_out": np.asarray(inputs["b_w_out"])[0],
    }


def kernel(**inputs):
    nc = build_program()
    in_maps = []
    for core in range(8):
        b, p = core // 2, core % 2
        in_maps.append(make_core_inputs(inputs, b, p))
    res = run_bass_kernel_spmd(nc, in_maps, core_ids=list(range(8)))
    out = np.zeros((NB, SEQ, D), np.float32)
    for core in range(8):
        b, p = core // 2, core % 2
        out[b, p * 2048:(p + 1) * 2048] = res.results[core]["y_out"]
    return out
```
